# Optimizing a Trainium2 kernel written in Bass

```python
import jax, jax.numpy as jnp
from jax import lax
import numpy as np

D_MODEL = 2048
BATCH = 16
SEQ = 2048
DEPTH = 2

N_MEM = 256
HEAD_DIM = D_MODEL // 16
N_MIX_HEADS = 12
N_MEM_HEADS = 4
MIX_W = N_MIX_HEADS * HEAD_DIM
MEM_W = N_MEM_HEADS * HEAD_DIM
D_MIX = MIX_W + MEM_W
D_FF = 256 * ((8 * D_MODEL // 3 + 255) // 256)
CHUNK = 128
Q_BLOCK = 128
N_A = DEPTH // 2
N_B = DEPTH - N_A
ROPE_BASE = 10000.0
EPS = 1e-6
MACARON_W = 0.5

kernel_name = "yoco_retention_stickbreaking_macaron_memory"


def rmsnorm(x, g):
    xf = x.astype(jnp.float32)
    y = xf * lax.rsqrt(jnp.mean(xf * xf, axis=-1, keepdims=True) + EPS)
    return (y * g.astype(jnp.float32)).astype(x.dtype)


def head_norm(y):
    yf = y.astype(jnp.float32)
    mu = jnp.mean(yf, axis=-1, keepdims=True)
    var = jnp.mean(jnp.square(yf - mu), axis=-1, keepdims=True)
    return ((yf - mu) * lax.rsqrt(var + EPS)).astype(y.dtype)


def swiglu(h, w_gate, w_up, w_down):
    return (jax.nn.silu(h @ w_gate) * (h @ w_up)) @ w_down


def rotary(t, positions):
    d = t.shape[-1]
    inv = ROPE_BASE ** (-jnp.arange(0, d, 2, dtype=jnp.float32) / d)
    ang = positions.astype(jnp.float32)[..., None] * inv
    cos = jnp.cos(ang)[:, :, None, :]
    sin = jnp.sin(ang)[:, :, None, :]
    tf = t.astype(jnp.float32)
    t1, t2 = tf[..., : d // 2], tf[..., d // 2:]
    return jnp.concatenate([t1 * cos - t2 * sin, t2 * cos + t1 * sin], axis=-1).astype(t.dtype)


def retention_log_decay():
    return jnp.log1p(-jnp.exp2(-5.0 - jnp.arange(N_MIX_HEADS, dtype=jnp.float32)))


def retention_chunkwise(q, k, v):
    B, S, H, d = q.shape
    N = S // CHUNK
    dt = q.dtype
    to_chunks = lambda t: t.reshape(B, N, CHUNK, H, d).transpose(0, 3, 1, 2, 4)
    qc, kc, vc = to_chunks(q), to_chunks(k), to_chunks(v)
    lg = retention_log_decay()
    idx = jnp.arange(CHUNK, dtype=jnp.float32)
    diff = idx[:, None] - idx[None, :]
    dmask = jnp.where(diff >= 0, jnp.exp(lg[:, None, None] * jnp.maximum(diff, 0.0)), 0.0)
    scores = jnp.einsum('bhncd,bhnmd->bhncm', qc, kc) * dmask[None, :, None].astype(dt)
    inner = jnp.einsum('bhncm,bhnmd->bhncd', scores, vc)
    k_decay = jnp.exp(lg[:, None] * (CHUNK - 1 - idx)[None, :]).astype(dt)
    kv = jnp.einsum('bhnmd,bhnme->bhnde', kc * k_decay[None, :, None, :, None], vc)
    chunk_decay = jnp.exp(lg * CHUNK).astype(dt)

    def step(state, kv_i):
        return state * chunk_decay[None, :, None, None] + kv_i, state

    _, prev = lax.scan(step, jnp.zeros((B, H, d, d), dt), jnp.moveaxis(kv, 2, 0))
    prev = jnp.moveaxis(prev, 0, 2)
    q_decay = jnp.exp(lg[:, None] * (idx + 1.0)[None, :]).astype(dt)
    cross = jnp.einsum('bhncd,bhnde->bhnce', qc * q_decay[None, :, None, :, None], prev)
    out = inner + cross
    return out.transpose(0, 2, 3, 1, 4).reshape(B, S, H, d)


def stick_breaking(q, k, v):
    B, S, H, d = q.shape
    scale = d ** -0.5
    outs = []
    for i in range(S // Q_BLOCK):
        L = (i + 1) * Q_BLOCK
        qs = q[:, i * Q_BLOCK:L]
        z = jnp.einsum('bqhd,bkhd->bhqk', qs, k[:, :L]).astype(jnp.float32) * scale
        t_pos = i * Q_BLOCK + jnp.arange(Q_BLOCK)
        causal = jnp.arange(L)[None, :] < t_pos[:, None]
        log_beta = jax.nn.log_sigmoid(z)
        log_1mb = jnp.where(causal, jax.nn.log_sigmoid(-z), 0.0)
        after = lax.cumsum(log_1mb, axis=3, reverse=True) - log_1mb
        A = jnp.where(causal, jnp.exp(log_beta + after), 0.0)
        outs.append(jnp.einsum('bhqk,bkhd->bqhd', A.astype(v.dtype), v[:, :L]))
    return jnp.concatenate(outs, axis=1)


def memory_attention(qm, mem_n, w_mem_kv):
    B, S, _ = qm.shape
    M = mem_n.shape[1]
    mkv = mem_n @ w_mem_kv
    mk = mkv[..., :MEM_W].reshape(B, M, N_MEM_HEADS, HEAD_DIM)
    mv = mkv[..., MEM_W:].reshape(B, M, N_MEM_HEADS, HEAD_DIM)
    qh = qm.reshape(B, S, N_MEM_HEADS, HEAD_DIM)
    s = jnp.einsum('bshd,bmhd->bhsm', qh, mk).astype(jnp.float32) * (HEAD_DIM ** -0.5)
    p = jax.nn.softmax(s, axis=-1).astype(mv.dtype)
    return jnp.einsum('bhsm,bmhd->bshd', p, mv).reshape(B, S, MEM_W)


def mixer_a(h, mem_n, positions, w_in, w_mem_kv, w_o):
    B, S, _ = h.shape
    proj = h @ w_in
    q = proj[..., 0:MIX_W].reshape(B, S, N_MIX_HEADS, HEAD_DIM)
    k = proj[..., MIX_W:2 * MIX_W].reshape(B, S, N_MIX_HEADS, HEAD_DIM)
    v = proj[..., 2 * MIX_W:3 * MIX_W].reshape(B, S, N_MIX_HEADS, HEAD_DIM)
    g = proj[..., 3 * MIX_W:4 * MIX_W]
    qm = proj[..., 4 * MIX_W:]
    q = rotary(q, positions)
    k = rotary(k, positions) * (HEAD_DIM ** -0.5)
    y = retention_chunkwise(q, k, v)
    y = head_norm(y).reshape(B, S, MIX_W) * jax.nn.silu(g)
    ym = memory_attention(qm, mem_n, w_mem_kv)
    return jnp.concatenate([y, ym], axis=-1) @ w_o


def mixer_b(h, mem_n, k_sh, v_sh, w_in, w_mem_kv, w_o):
    B, S, _ = h.shape
    proj = h @ w_in
    q = proj[..., :MIX_W].reshape(B, S, N_MIX_HEADS, HEAD_DIM)
    qm = proj[..., MIX_W:]
    y = stick_breaking(q, k_sh, v_sh).reshape(B, S, MIX_W)
    ym = memory_attention(qm, mem_n, w_mem_kv)
    return jnp.concatenate([y, ym], axis=-1) @ w_o


def setup_inputs(seed: int = 0) -> dict:
    key = jax.random.key(seed)
    ks = iter(jax.random.split(key, 32))
    f32 = jnp.float32

    def w(shape, fan_in):
        return jax.random.normal(next(ks), shape, f32) * (fan_in ** -0.5)

    def gain(shape):
        return 1.0 + 0.02 * jax.random.normal(next(ks), shape, f32)

    x = jax.random.normal(next(ks), (BATCH, SEQ, D_MODEL), f32)
    mem = jax.random.normal(next(ks), (BATCH, N_MEM, D_MODEL), f32)
    offset = jax.random.randint(next(ks), (BATCH, 1), 0, 4096, dtype=jnp.int32)
    positions = (jnp.arange(SEQ, dtype=jnp.int32)[None, :] + offset).astype(jnp.int32)
    return {
        "x": x,
        "mem": mem,
        "positions": positions,
        "ffn1_norm_pre": gain((DEPTH, D_MODEL)),
        "ffn1_norm_post": gain((DEPTH, D_MODEL)),
        "ffn1_w_gate": w((DEPTH, D_MODEL, D_FF), D_MODEL),
        "ffn1_w_up": w((DEPTH, D_MODEL, D_FF), D_MODEL),
        "ffn1_w_down": w((DEPTH, D_FF, D_MODEL), D_FF),
        "mix_norm_pre": gain((DEPTH, D_MODEL)),
        "mix_norm_post": gain((DEPTH, D_MODEL)),
        "mem_norm": gain((DEPTH, D_MODEL)),
        "w_mem_kv": w((DEPTH, D_MODEL, 2 * MEM_W), D_MODEL),
        "w_o": w((DEPTH, D_MIX, D_MODEL), D_MIX),
        "ret_w_in": w((N_A, D_MODEL, 4 * MIX_W + MEM_W), D_MODEL),
        "kv_norm": gain((D_MODEL,)),
        "w_kv_shared": w((D_MODEL, 2 * MIX_W), D_MODEL),
        "sb_w_in": w((N_B, D_MODEL, MIX_W + MEM_W), D_MODEL),
        "ffn2_norm_pre": gain((DEPTH, D_MODEL)),
        "ffn2_norm_post": gain((DEPTH, D_MODEL)),
        "ffn2_w_gate": w((DEPTH, D_MODEL, D_FF), D_MODEL),
        "ffn2_w_up": w((DEPTH, D_MODEL, D_FF), D_MODEL),
        "ffn2_w_down": w((DEPTH, D_FF, D_MODEL), D_FF),
    }


def reference(x, mem, positions, ffn1_norm_pre, ffn1_norm_post, ffn1_w_gate, ffn1_w_up, ffn1_w_down,
              mix_norm_pre, mix_norm_post, mem_norm, w_mem_kv, w_o, ret_w_in, kv_norm, w_kv_shared,
              sb_w_in, ffn2_norm_pre, ffn2_norm_post, ffn2_w_gate, ffn2_w_up, ffn2_w_down):
    B, S, _ = x.shape
    k_sh = v_sh = None
    for l in range(DEPTH):
        if l == N_A:
            hs = rmsnorm(x, kv_norm)
            kv = hs @ w_kv_shared
            k_sh = kv[..., :MIX_W].reshape(B, S, N_MIX_HEADS, HEAD_DIM)
            v_sh = kv[..., MIX_W:].reshape(B, S, N_MIX_HEADS, HEAD_DIM)
        h = rmsnorm(x, ffn1_norm_pre[l])
        x = x + MACARON_W * rmsnorm(swiglu(h, ffn1_w_gate[l], ffn1_w_up[l], ffn1_w_down[l]), ffn1_norm_post[l])
        mem_n = rmsnorm(mem, mem_norm[l])
        h = rmsnorm(x, mix_norm_pre[l])
        if l < N_A:
            y = mixer_a(h, mem_n, positions, ret_w_in[l], w_mem_kv[l], w_o[l])
        else:
            y = mixer_b(h, mem_n, k_sh, v_sh, sb_w_in[l - N_A], w_mem_kv[l], w_o[l])
        x = x + rmsnorm(y, mix_norm_post[l])
        h = rmsnorm(x, ffn2_norm_pre[l])
        x = x + MACARON_W * rmsnorm(swiglu(h, ffn2_w_gate[l], ffn2_w_up[l], ffn2_w_down[l]), ffn2_norm_post[l])
    return x
```

```python
import numpy as np
import ml_dtypes
from contextlib import ExitStack as ExitStackCompat
import concourse.bass as bass
import concourse.mybir as mybir
from concourse.bass_utils import run_bass_kernel_spmd

F32 = mybir.dt.float32
BF16 = mybir.dt.bfloat16
I32 = mybir.dt.int32
AF = mybir.ActivationFunctionType
ALU = mybir.AluOpType
AX = mybir.AxisListType

D = 2048
DFF = 5632
NFF = DFF // 128
KD = D // 128
SEQ = 2048
NSEQ = 2
NTOK = SEQ * NSEQ
NMEM = 256
HD = 128
NH = 12
NMH = 4
MIXW = NH * HD
MEMW = NMH * HD
EPS = 1e-6
N_CORES = 8

N_DSEM = 12
CSEM_CAP = 30000


_UID = [0]


def _sbt(nc, name, shape, dt):
    _UID[0] += 1
    return nc.sbuf_tensor("%s_%d" % (name, _UID[0]), shape, dt)


class Buf:
    __slots__ = ("name", "w", "r")

    def __init__(self, name=""):
        self.name = name
        self.w = None
        self.r = {}


class Op:
    __slots__ = ("q", "fn", "deps", "need", "no", "dma")


class Sched:
    QUEUES = ("pe", "act", "dve", "pool", "sp")

    def __init__(self, nc):
        self.nc = nc
        self.ops = {q: [] for q in self.QUEUES}
        self.count = {}
        self.sems = {}
        self.waited = {q: {} for q in self.QUEUES}
        self._semctx = []

    def _sem(self, fam, idx):
        key = (fam, idx)
        if key not in self.sems:
            cm = self.nc.semaphore("s_%s_%s_%d" % (fam[0], "d" if fam[1] else "c", idx))
            h = cm.__enter__()
            self._semctx.append(cm)
            self.sems[key] = h
        return self.sems[key]

    def close(self):
        for cm in reversed(self._semctx):
            cm.__exit__(None, None, None)
        self._semctx = []

    def add(self, q, fn, reads=(), writes=(), dma=False):
        op = Op()
        op.q = q
        op.fn = fn
        op.dma = dma
        op.need = dma
        op.no = None
        deps = []
        for b in reads:
            if b.w is not None:
                deps.append(b.w)
        for b in writes:
            if b.w is not None:
                deps.append(b.w)
            deps.extend(b.r.values())
        fam = (q, dma)
        for b in reads:
            b.r[fam] = op
        for b in writes:
            b.w = op
            b.r = {}
        out = []
        seen = set()
        for d in deps:
            if d is op or id(d) in seen:
                continue
            seen.add(id(d))
            if d.q == "pe" and q == "pe" and not d.dma and not dma:
                continue
            d.need = True
            out.append(d)
        op.deps = out
        self.ops[q].append(op)
        return op

    def _semval(self, d):
        fam = (d.q, d.dma)
        if d.dma:
            idx = (d.no - 1) % N_DSEM
            return self._sem(fam, idx), 16 * ((d.no - 1) // N_DSEM + 1)
        idx = (d.no - 1) // CSEM_CAP
        v = (d.no - 1) % CSEM_CAP + 1
        return self._sem(fam, idx), v

    def emit(self, barrier=True):
        nc = self.nc
        if barrier:
            for q in self.QUEUES:
                for op in reversed(self.ops[q]):
                    if not op.dma:
                        op.need = True
                        break
        for q in self.QUEUES:
            for op in self.ops[q]:
                if op.need and op.no is None:
                    fam = (q, op.dma)
                    self.count[fam] = self.count.get(fam, 0) + 1
                    op.no = self.count[fam]
        finals = []
        if barrier:
            for fam, n in self.count.items():
                if fam[1]:
                    for idx in range(min(n, N_DSEM)):
                        last_i = n - ((n - 1 - idx) % N_DSEM)
                        finals.append((self._sem(fam, idx), 16 * ((last_i - 1) // N_DSEM + 1)))
                else:
                    idx = (n - 1) // CSEM_CAP
                    v = (n - 1) % CSEM_CAP + 1
                    finals.append((self._sem(fam, idx), v))
        for q in self.QUEUES:
            for op in self.ops[q]:
                for d in op.deps:
                    self._semval(d)
                if op.need:
                    self._semval(op)

        def run_queue(q, e):
            waited = self.waited[q]
            for op in self.ops[q]:
                for d in op.deps:
                    sem, val = self._semval(d)
                    if waited.get(sem, 0) < val:
                        e.wait_ge(sem, val)
                        waited[sem] = val
                if op.dma and op.no > N_DSEM:
                    sem, val = self._semval(op)
                    if waited.get(sem, 0) < val - 16:
                        e.wait_ge(sem, val - 16)
                        waited[sem] = val - 16
                ins = op.fn(e)
                if op.need:
                    sem, _ = self._semval(op)
                    ins.then_inc(sem, 16 if op.dma else 1)
            for sem, val in finals:
                if waited.get(sem, 0) < val:
                    e.wait_ge(sem, val)
                    waited[sem] = val

        with nc.Block() as block:
            @block.tensor
            def _(e):
                run_queue("pe", e)

            @block.scalar
            def _(e):
                run_queue("act", e)

            @block.vector
            def _(e):
                run_queue("dve", e)

            @block.gpsimd
            def _(e):
                run_queue("pool", e)

            @block.sync
            def _(e):
                run_queue("sp", e)
        self.ops = {q: [] for q in self.QUEUES}


class Ring:
    def __init__(self, aps, name=""):
        self.aps = list(aps)
        self.bufs = [Buf("%s%d" % (name, i)) for i in range(len(self.aps))]
        self.i = 0

    def next(self):
        k = self.i % len(self.aps)
        self.i += 1
        return self.aps[k], self.bufs[k]


class Ctx:
    pass


def ffn_phase(C, x_in, x_out, wg, wu, wd, g_pre, g_post, ntok, dbg=None):
    nc, S = C.nc, C.S
    T = 512
    NJ = T // 128
    NT = ntok // T
    WC = 256
    NWC = DFF // WC
    QC = 11
    NQ = NFF // QC
    wg_v = wg.rearrange("(k p) c -> p k c", p=128)
    wu_v = wu.rearrange("(k p) c -> p k c", p=128)
    wd_v = wd.rearrange("(c p) n -> p c n", p=128)
    with (
        _sbt(nc, "f_hT", [128, KD, T], BF16) as hT,
        _sbt(nc, "f_aT", [128, NFF, T], BF16) as aT,
        _sbt(nc, "f_wg", [128, 2, KD, WC], BF16) as wgt,
        _sbt(nc, "f_wu", [128, 2, KD, WC], BF16) as wut,
        _sbt(nc, "f_wd", [128, 2, QC, 512], BF16) as wdt,
        _sbt(nc, "f_xs", [128, 4, D], F32) as xs,
        _sbt(nc, "f_xn", [128, 2, D], BF16) as xn,
        _sbt(nc, "f_ys", [128, 3, 512], F32) as ys,
        _sbt(nc, "f_sg", [128, 2, 512], F32) as sg,
        _sbt(nc, "f_gpre", [128, D], F32) as gpre,
        _sbt(nc, "f_gpost", [128, D], F32) as gpost,
        _sbt(nc, "f_junk", [128, D], BF16) as junk,
        _sbt(nc, "f_st", [128, 64], F32) as st,
    ):
        xs_ring = Ring([xs[:, i, :] for i in range(4)], "xs")
        xn_ring = Ring([xn[:, i, :] for i in range(2)], "xn")
        ys_ring = Ring([ys[:, i, :] for i in range(3)], "ys")
        sg_ring = Ring([sg[:, i, :] for i in range(2)], "sg")
        wg_ring = Ring([wgt[:, i] for i in range(2)], "wg")
        wu_ring = Ring([wut[:, i] for i in range(2)], "wu")
        wd_ring = Ring([wdt[:, i] for i in range(2)], "wd")
        b_gpre, b_gpost, b_eps = Buf(), Buf(), Buf()
        b_junk = Buf()
        S.add("sp", lambda e: e.dma_start(out=gpre[:], in_=g_pre.partition_broadcast(128)),
              writes=[b_gpre], dma=True)
        S.add("sp", lambda e: e.dma_start(out=gpost[:], in_=g_post.partition_broadcast(128)),
              writes=[b_gpost], dma=True)
        S.add("dve", lambda e: e.memset(st[:, 0:1], EPS), writes=[b_eps])

        b_hT = [Buf() for _ in range(NJ)]
        b_aT = [Buf() for _ in range(NFF)]
        b_stat = [[Buf() for _ in range(4)] for _ in range(2)]
        for t in range(NT):
            base = 4 + (t % 2) * 28
            c_sspre = st[:, base:base + 4]
            c_rpre = st[:, base + 4:base + 8]
            c_sspost = st[:, base + 8:base + 24]
            c_rpost = st[:, base + 24:base + 28]
            b_sspre, b_rpre, b_sspost, b_rpost = b_stat[t % 2]
            tok0 = t * T
            x_tiles = []
            for j in range(NJ):
                xa, xb = xs_ring.next()
                r0 = tok0 + j * 128
                S.add("sp", (lambda e, xa=xa, r0=r0: e.dma_start(out=xa, in_=x_in[r0:r0 + 128, :])),
                      writes=[xb], dma=True)
                S.add("act", (lambda e, xa=xa, j=j: e.activation(out=junk[:], in_=xa, func=AF.Square,
                                                                   accum_out=c_sspre[:, j:j + 1])),
                      reads=[xb], writes=[b_sspre])
                x_tiles.append((xa, xb))
            S.add("act", lambda e: e.activation(out=c_rpre, in_=c_sspre, func=AF.Sqrt,
                                                scale=1.0 / D, bias=st[:, 0:1]),
                  reads=[b_sspre, b_eps], writes=[b_rpre])
            S.add("dve", lambda e: e.reciprocal(out=c_rpre, in_=c_rpre), writes=[b_rpre])
            for j in range(NJ):
                xa, xb = x_tiles[j]
                na, nb = xn_ring.next()
                S.add("dve", (lambda e, xa=xa, na=na, j=j: e.scalar_tensor_tensor(
                    out=na, in0=xa, scalar=c_rpre[:, j:j + 1], in1=gpre[:], op0=ALU.mult, op1=ALU.mult)),
                    reads=[xb, b_rpre, b_gpre], writes=[nb])
                for half in range(2):
                    pa, pb = C.psum.next()
                    pv = pa.bitcast(BF16)

                    def tr(e, pv=pv, na=na, half=half):
                        ins = None
                        for i in range(8):
                            k = half * 8 + i
                            ins = e.transpose(out=pv[:, i * 128:(i + 1) * 128],
                                              in_=na[:, k * 128:(k + 1) * 128], identity=C.ident[:])
                        return ins
                    S.add("pe", tr, reads=[nb, C.b_ident], writes=[pb])
                    eng = "act" if half == 0 else "dve"
                    dst = hT[:, half * 8:(half + 1) * 8, j * 128:(j + 1) * 128]
                    src = pv.rearrange("p (i t) -> p i t", i=8)
                    if eng == "act":
                        S.add("act", (lambda e, dst=dst, src=src: e.activation(out=dst, in_=src, func=AF.Copy)),
                              reads=[pb], writes=[b_hT[j]])
                    else:
                        S.add("dve", (lambda e, dst=dst, src=src: e.tensor_copy(out=dst, in_=src)),
                              reads=[pb], writes=[b_hT[j]])
            for wc in range(NWC):
                ga, gb = wg_ring.next()
                ua, ub = wu_ring.next()
                c0 = wc * WC
                S.add("pool", (lambda e, ga=ga, c0=c0: e.dma_start(out=ga, in_=wg_v[:, :, c0:c0 + WC])),
                      writes=[gb], dma=True)
                S.add("pool", (lambda e, ua=ua, c0=c0: e.dma_start(out=ua, in_=wu_v[:, :, c0:c0 + WC])),
                      writes=[ub], dma=True)
                for cl in range(WC // 128):
                    c = wc * (WC // 128) + cl
                    pg, pgb = C.psum.next()
                    pu, pub = C.psum.next()

                    def mm(e, w=ga, cl=cl, ps=pg):
                        ins = None
                        for k in range(KD):
                            ins = e.matmul(ps, lhsT=w[:, k, cl * 128:(cl + 1) * 128], rhs=hT[:, k, :],
                                           start=(k == 0), stop=(k == KD - 1))
                        return ins
                    S.add("pe", mm, reads=[gb] + b_hT, writes=[pgb])

                    def mm2(e, w=ua, cl=cl, ps=pu):
                        ins = None
                        for k in range(KD):
                            ins = e.matmul(ps, lhsT=w[:, k, cl * 128:(cl + 1) * 128], rhs=hT[:, k, :],
                                           start=(k == 0), stop=(k == KD - 1))
                        return ins
                    S.add("pe", mm2, reads=[ub] + b_hT, writes=[pub])
                    sa, sb = sg_ring.next()
                    S.add("act", (lambda e, sa=sa, pg=pg: e.activation(out=sa, in_=pg, func=AF.Silu)),
                          reads=[pgb], writes=[sb])
                    S.add("dve", (lambda e, sa=sa, pu=pu, c=c: e.tensor_tensor(out=aT[:, c, :], in0=pu, in1=sa,
                                                                               op=ALU.mult)),
                          reads=[pub, sb], writes=[b_aT[c]])
            if dbg is not None and t == 0:
                S.add("sp", lambda e: e.dma_start(out=dbg["hT"], in_=hT[:]), reads=b_hT, dma=True)
                S.add("sp", lambda e: e.dma_start(out=dbg["aT"], in_=aT[:]), reads=b_aT, dma=True)
                S.add("sp", lambda e: e.dma_start(out=dbg["st"], in_=st[:]), reads=[b_rpre], dma=True)
            b_y = [[Buf() for _ in range(4)] for _ in range(NJ)]
            for n in range(4):
                accs = [C.psum.next() for _ in range(NJ)]
                for qi in range(NQ):
                    wa, wb = wd_ring.next()
                    S.add("pool", (lambda e, wa=wa, qi=qi, n=n: e.dma_start(
                        out=wa, in_=wd_v[:, qi * QC:(qi + 1) * QC, n * 512:(n + 1) * 512])),
                        writes=[wb], dma=True)
                    for j in range(NJ):
                        def mm3(e, wa=wa, qi=qi, j=j, ps=accs[j][0]):
                            ins = None
                            for ci in range(QC):
                                c = qi * QC + ci
                                ins = e.matmul(ps, lhsT=aT[:, c, j * 128:(j + 1) * 128], rhs=wa[:, ci, :],
                                               start=(c == 0), stop=(c == NFF - 1))
                            return ins
                        S.add("pe", mm3, reads=[wb] + b_aT[qi * QC:(qi + 1) * QC], writes=[accs[j][1]])
                for j in range(NJ):
                    ya, yb = ys_ring.next()
                    ps, psb = accs[j]
                    S.add("act", (lambda e, ya=ya, ps=ps: e.activation(out=ya, in_=ps, func=AF.Copy)),
                          reads=[psb], writes=[yb])
                    S.add("act", (lambda e, ps=ps, j=j, n=n: e.activation(
                        out=junk[:, 0:512], in_=ps, func=AF.Square,
                        accum_out=c_sspost[:, j * 4 + n:j * 4 + n + 1])),
                        reads=[psb], writes=[b_sspost])
                    r0 = tok0 + j * 128
                    S.add("sp", (lambda e, ya=ya, r0=r0, n=n: e.dma_start(
                        out=x_out[r0:r0 + 128, n * 512:(n + 1) * 512], in_=ya)),
                        reads=[yb], writes=[b_y[j][n]], dma=True)
            S.add("dve", lambda e: e.tensor_reduce(out=c_rpost, in_=c_sspost.rearrange("p (j n) -> p j n", n=4),
                                                   axis=AX.X, op=ALU.add),
                  reads=[b_sspost], writes=[b_rpost])
            S.add("act", lambda e: e.activation(out=c_rpost, in_=c_rpost, func=AF.Sqrt,
                                                scale=1.0 / D, bias=st[:, 0:1]),
                  reads=[b_eps], writes=[b_rpost])
            S.add("dve", lambda e: e.reciprocal(out=c_rpost, in_=c_rpost), writes=[b_rpost])
            for j in range(NJ):
                r0 = tok0 + j * 128
                ya, yb = xs_ring.next()
                xa, xb = xs_ring.next()
                S.add("sp", (lambda e, ya=ya, r0=r0: e.dma_start(out=ya, in_=x_out[r0:r0 + 128, :])),
                      reads=b_y[j], writes=[yb], dma=True)
                S.add("sp", (lambda e, xa=xa, r0=r0: e.dma_start(out=xa, in_=x_in[r0:r0 + 128, :])),
                      writes=[xb], dma=True)
                S.add("dve", (lambda e, ya=ya, j=j: e.scalar_tensor_tensor(
                    out=ya, in0=ya, scalar=c_rpost[:, j:j + 1], in1=gpost[:], op0=ALU.mult, op1=ALU.mult)),
                    reads=[b_rpost, b_gpost], writes=[yb])
                S.add("dve", (lambda e, ya=ya, xa=xa: e.scalar_tensor_tensor(
                    out=ya, in0=ya, scalar=0.5, in1=xa, op0=ALU.mult, op1=ALU.add)),
                    reads=[xb], writes=[yb])
                S.add("sp", (lambda e, ya=ya, r0=r0: e.dma_start(out=x_out[r0:r0 + 128, :], in_=ya)),
                      reads=[yb], writes=b_y[j], dma=True)
        S.emit()


def make_ctx(nc, stack, ident_dram):
    C = Ctx()
    C.nc = nc
    C.S = Sched(nc)
    banks = []
    for i in range(8):
        h = stack.enter_context(nc.psum_tensor("psb%d" % i, [128, 512], F32))
        banks.append(h[:])
    C.psum = Ring(banks, "ps")
    C.ident = stack.enter_context(_sbt(nc, "ident_sb", [128, 128], BF16))
    C.b_ident = Buf("ident")
    C.S.add("sp", lambda e: e.dma_start(out=C.ident[:], in_=ident_dram), writes=[C.b_ident], dma=True)
    C.flip = 0
    return C


def evac_copy(C, dst, src, reads, writes, eng=None):
    S = C.S
    if eng is None:
        eng = "act" if (C.flip % 2 == 0) else "dve"
        C.flip += 1
    if eng == "act":
        return S.add("act", (lambda e: e.activation(out=dst, in_=src, func=AF.Copy)), reads=reads, writes=writes)
    return S.add("dve", (lambda e: e.tensor_copy(out=dst, in_=src)), reads=reads, writes=writes)


class NormT:
    def __init__(self, C, stack, pfx, gain_dram):
        nc = C.nc
        self.C = C
        self.xs = stack.enter_context(_sbt(nc, pfx + "_xs", [128, 4, D], F32))
        self.xn = stack.enter_context(_sbt(nc, pfx + "_xn", [128, 2, D], BF16))
        self.g = stack.enter_context(_sbt(nc, pfx + "_g", [128, D], F32))
        self.junk = stack.enter_context(_sbt(nc, pfx + "_junk", [128, D], BF16))
        self.st = stack.enter_context(_sbt(nc, pfx + "_st", [128, 20], F32))
        self.xs_ring = Ring([self.xs[:, i, :] for i in range(4)])
        self.xn_ring = Ring([self.xn[:, i, :] for i in range(2)])
        self.b_g, self.b_eps = Buf(), Buf()
        self.b_stat = [[Buf(), Buf()], [Buf(), Buf()]]
        self.n = 0
        g = self.g
        C.S.add("sp", lambda e: e.dma_start(out=g[:], in_=gain_dram.partition_broadcast(128)),
                writes=[self.b_g], dma=True)
        st = self.st
        C.S.add("dve", lambda e: e.memset(st[:, 0:1], EPS), writes=[self.b_eps])

    def tile(self, x_in, tok0, NJ, hT, b_hT):
        C, S = self.C, self.C.S
        st, junk, gt = self.st, self.junk, self.g
        base = 4 + (self.n % 2) * 8
        b_ss, b_r = self.b_stat[self.n % 2]
        self.n += 1
        c_ss = st[:, base:base + NJ]
        c_r = st[:, base + 4:base + 4 + NJ]
        x_tiles = []
        for j in range(NJ):
            xa, xb = self.xs_ring.next()
            r0 = tok0 + j * 128
            S.add("sp", (lambda e, xa=xa, r0=r0: e.dma_start(out=xa, in_=x_in[r0:r0 + 128, :])),
                  writes=[xb], dma=True)
            S.add("act", (lambda e, xa=xa, j=j: e.activation(out=junk[:], in_=xa, func=AF.Square,
                                                               accum_out=c_ss[:, j:j + 1])),
                  reads=[xb], writes=[b_ss])
            x_tiles.append((xa, xb))
        S.add("act", lambda e: e.activation(out=c_r, in_=c_ss, func=AF.Sqrt, scale=1.0 / D, bias=st[:, 0:1]),
              reads=[b_ss, self.b_eps], writes=[b_r])
        S.add("dve", lambda e: e.reciprocal(out=c_r, in_=c_r), writes=[b_r])
        for j in range(NJ):
            xa, xb = x_tiles[j]
            na, nb = self.xn_ring.next()
            S.add("dve", (lambda e, xa=xa, na=na, j=j: e.scalar_tensor_tensor(
                out=na, in0=xa, scalar=c_r[:, j:j + 1], in1=gt[:], op0=ALU.mult, op1=ALU.mult)),
                reads=[xb, b_r, self.b_g], writes=[nb])
            for half in range(2):
                pa, pb = C.psum.next()
                pv = pa.bitcast(BF16)

                def tr(e, pv=pv, na=na, half=half):
                    ins = None
                    for i in range(8):
                        k = half * 8 + i
                        ins = e.transpose(out=pv[:, i * 128:(i + 1) * 128],
                                          in_=na[:, k * 128:(k + 1) * 128], identity=C.ident[:])
                    return ins
                S.add("pe", tr, reads=[nb, C.b_ident], writes=[pb])
                dst = hT[:, half * 8:(half + 1) * 8, j * 128:(j + 1) * 128]
                src = pv.rearrange("p (i t) -> p i t", i=8)
                evac_copy(C, dst, src, [pb], [b_hT[j]], eng=("act" if half == 0 else "dve"))


def linear_phase(C, stack_outer, pfx, ntok, x_src, W, ncols, epilogue, norm_gain=None, T=512, pre=None, post=None):
    nc, S = C.nc, C.S
    NJ = T // 128
    NT = ntok // T
    NG = (ncols + 511) // 512
    W_v = W.rearrange("(k p) c -> p k c", p=128)
    with ExitStackCompat() as stack:
        hTt = stack.enter_context(_sbt(nc, pfx + "_hT", [128, 2, KD, T], BF16))
        wt = stack.enter_context(_sbt(nc, pfx + "_w", [128, 3, KD, 512], BF16))
        w_ring = Ring([wt[:, i] for i in range(3)])
        hT_bufs = [[Buf() for _ in range(NJ)] for _ in range(2)]
        hT_kbufs = [[Buf() for _ in range(KD)] for _ in range(2)]
        nt = NormT(C, stack, pfx, norm_gain) if norm_gain is not None else None
        if pre is not None:
            pre(stack)
        for t in range(NT):
            tok0 = t * T
            hT = hTt[:, t % 2]
            b_hT = hT_bufs[t % 2]
            if nt is not None:
                nt.tile(x_src, tok0, NJ, hT, b_hT)
            else:
                for k in range(KD):
                    S.add("sp", (lambda e, hT=hT, k=k, tok0=tok0: e.dma_start_transpose(
                        out=hT[:, k, :], in_=x_src[tok0:tok0 + T, k * 128:(k + 1) * 128])),
                        writes=[hT_kbufs[t % 2][k]], dma=True)
            for n in range(NG):
                gs = min(512, ncols - n * 512)
                wa, wb = w_ring.next()
                S.add("pool", (lambda e, wa=wa, n=n, gs=gs: e.dma_start(
                    out=wa[:, :, 0:gs], in_=W_v[:, :, n * 512:n * 512 + gs])), writes=[wb], dma=True)
                for j in range(NJ):
                    pa, pb = C.psum.next()

                    def mm(e, wa=wa, hT=hT, j=j, gs=gs, ps=pa):
                        ins = None
                        for k in range(KD):
                            ins = e.matmul(ps[:, 0:gs], lhsT=hT[:, k, j * 128:(j + 1) * 128], rhs=wa[:, k, 0:gs],
                                           start=(k == 0), stop=(k == KD - 1))
                        return ins
                    S.add("pe", mm, reads=[wb] + ([b_hT[j]] if nt is not None else hT_kbufs[t % 2]), writes=[pb])
                    epilogue(t, j, n, gs, pa, pb, tok0 + j * 128)
            if post is not None:
                post(t, tok0, NJ)
        S.emit()


class PlainEpi:
    def __init__(self, C, stack, pfx, dst, silu_groups=(), col0=0):
        self.C = C
        self.dst = dst
        self.silu = set(silu_groups)
        self.col0 = col0
        t = stack.enter_context(_sbt(C.nc, pfx + "_stg", [128, 4, 512], BF16))
        self.ring = Ring([t[:, i, :] for i in range(4)])

    def __call__(self, t, j, n, gs, pa, pb, r0):
        C, S = self.C, self.C.S
        sa, sb = self.ring.next()
        if n in self.silu:
            S.add("act", (lambda e: e.activation(out=sa[:, 0:gs], in_=pa[:, 0:gs], func=AF.Silu)),
                  reads=[pb], writes=[sb])
        else:
            evac_copy(C, sa[:, 0:gs], pa[:, 0:gs], [pb], [sb])
        dst, c0 = self.dst, self.col0 + n * 512
        S.add("sp", (lambda e: e.dma_start(out=dst[r0:r0 + 128, c0:c0 + gs], in_=sa[:, 0:gs])),
              reads=[sb], dma=True)


def simple_linear(C, pfx, ntok, x_src, gain, W, ncols, dst, T=512):
    epi = {}

    def pre(stack):
        epi["e"] = PlainEpi(C, stack, pfx, dst)

    linear_phase(C, None, pfx, ntok, x_src, W, ncols, lambda *a: epi["e"](*a), norm_gain=gain, T=T, pre=pre)


def inproj_a_phase(C, x_src, gain, W, P, pos, invf):
    nc, S = C.nc, C.S
    R = {}
    NSUB = NTOK // 128
    PI = float(np.pi)

    def pre(stack):
        posi = stack.enter_context(_sbt(nc, "ia_posi", [128, NSUB], I32))
        posf = stack.enter_context(_sbt(nc, "ia_posf", [128, NSUB], F32))
        inv = stack.enter_context(_sbt(nc, "ia_inv", [128, 64], F32))
        ang = stack.enter_context(_sbt(nc, "ia_ang", [128, NSUB, 64], F32))
        tmp = stack.enter_context(_sbt(nc, "ia_tmp", [128, NSUB, 64], F32))
        cs = stack.enter_context(_sbt(nc, "ia_cs", [128, 4, NSUB, 64], F32))
        cst = stack.enter_context(_sbt(nc, "ia_c", [128, 2], F32))
        rt = stack.enter_context(_sbt(nc, "ia_rt", [128, 2, 4, 256], F32))
        R["cs"] = cs
        R["rt"] = Ring([rt[:, i] for i in range(2)])
        R["epi"] = PlainEpi(C, stack, "ia", P, silu_groups=(9, 10, 11))
        b_pos, b_inv, b_ang, b_tmp, b_c = Buf(), Buf(), Buf(), Buf(), Buf()
        R["b_cs"] = Buf()
        S.add("sp", lambda e: e.dma_start(out=posi[:], in_=pos), writes=[b_pos], dma=True)
        S.add("sp", lambda e: e.dma_start(out=inv[:], in_=invf.partition_broadcast(128)), writes=[b_inv], dma=True)
        S.add("dve", lambda e: e.tensor_copy(out=posf[:], in_=posi[:]), reads=[b_pos], writes=[b_pos])
        for n in range(NSUB):
            S.add("dve", (lambda e, n=n: e.tensor_scalar(out=ang[:, n, :], in0=inv[:], scalar1=posf[:, n:n + 1],
                                                         scalar2=None, op0=ALU.mult)),
                  reads=[b_pos, b_inv], writes=[b_ang])
        angf = ang[:].rearrange("p n i -> p (n i)")
        tmpf = tmp[:].rearrange("p n i -> p (n i)")
        MAGIC = 12582912.0
        C1 = 6.28125
        C2 = float(2 * np.pi - 6.28125)
        PIC = 3.141592
        S.add("dve", lambda e: e.memset(cst[:, 1:2], PI / 2), writes=[b_c])
        S.add("dve", lambda e: e.tensor_scalar(out=tmpf, in0=angf, scalar1=float(1.0 / (2 * np.pi)), scalar2=MAGIC,
                                               op0=ALU.mult, op1=ALU.add), reads=[b_ang], writes=[b_tmp])
        S.add("dve", lambda e: e.tensor_scalar(out=tmpf, in0=tmpf, scalar1=MAGIC, scalar2=None, op0=ALU.subtract),
              writes=[b_tmp])
        S.add("dve", lambda e: e.scalar_tensor_tensor(out=angf, in0=tmpf, scalar=-C1, in1=angf, op0=ALU.mult,
                                                      op1=ALU.add), reads=[b_tmp], writes=[b_ang])
        S.add("dve", lambda e: e.scalar_tensor_tensor(out=angf, in0=tmpf, scalar=-C2, in1=angf, op0=ALU.mult,
                                                      op1=ALU.add), reads=[b_tmp], writes=[b_ang])
        S.add("dve", lambda e: e.tensor_scalar(out=tmpf, in0=angf, scalar1=PIC, scalar2=None, op0=ALU.is_gt),
              reads=[b_ang], writes=[b_tmp])
        S.add("dve", lambda e: e.scalar_tensor_tensor(out=angf, in0=tmpf, scalar=float(-2 * np.pi), in1=angf,
                                                      op0=ALU.mult, op1=ALU.add), reads=[b_tmp], writes=[b_ang])
        S.add("dve", lambda e: e.tensor_scalar(out=angf, in0=angf, scalar1=-PIC, scalar2=PIC, op0=ALU.max,
                                               op1=ALU.min), writes=[b_ang])
        sinv = cs[:, 1].rearrange("p n i -> p (n i)")
        cosv = cs[:, 0].rearrange("p n i -> p (n i)")
        S.add("act", lambda e: e.activation(out=sinv, in_=angf, func=AF.Sin), reads=[b_ang], writes=[R["b_cs"]])
        S.add("act", lambda e: e.activation(out=tmpf, in_=angf, func=AF.Abs), reads=[b_ang], writes=[b_tmp])
        S.add("act", lambda e: e.activation(out=cosv, in_=tmpf, func=AF.Sin, bias=cst[:, 1:2], scale=-1.0),
              reads=[b_tmp, b_c], writes=[R["b_cs"]])
        for which in (0, 1):
            srcv = cs[:, which].rearrange("p n i -> p (n i)")
            dstv = cs[:, 2 + which].rearrange("p n i -> p (n i)")
            S.add("dve", (lambda e, srcv=srcv, dstv=dstv: e.tensor_scalar(
                out=dstv, in0=srcv, scalar1=float(HD ** -0.5), scalar2=None, op0=ALU.mult)),
                writes=[R["b_cs"]])

    def epilogue(t, j, n, gs, pa, pb, r0):
        if n >= 6:
            return R["epi"](t, j, n, gs, pa, pb, r0)
        nn = r0 // 128
        cs = R["cs"]
        koff = 0 if n < 3 else 2
        cosb = cs[:, koff + 0, nn:nn + 1, :].to_broadcast([128, 4, 64])
        sinb = cs[:, koff + 1, nn:nn + 1, :].to_broadcast([128, 4, 64])
        ps4 = pa.rearrange("p (h two i) -> p h two i", h=4, two=2)
        t1, t2 = ps4[:, :, 0, :], ps4[:, :, 1, :]
        rta, rtb = R["rt"].next()
        tv = [rta[:, i].rearrange("p (h i) -> p h i", h=4) for i in range(4)]
        sa, sb = R["epi"].ring.next()
        so = sa.rearrange("p (h two i) -> p h two i", h=4, two=2)
        bcs = R["b_cs"]
        ba, bb2, bc, bd = Buf(), Buf(), Buf(), Buf()
        S.add("dve", lambda e: e.tensor_tensor(out=tv[0], in0=t1, in1=cosb, op=ALU.mult), reads=[pb, bcs], writes=[rtb, ba])
        S.add("dve", lambda e: e.tensor_tensor(out=tv[1], in0=t2, in1=sinb, op=ALU.mult), reads=[pb, bcs], writes=[bb2])
        S.add("dve", lambda e: e.tensor_tensor(out=tv[2], in0=t2, in1=cosb, op=ALU.mult), reads=[pb, bcs], writes=[bc])
        S.add("dve", lambda e: e.tensor_tensor(out=tv[3], in0=t1, in1=sinb, op=ALU.mult), reads=[pb, bcs], writes=[bd])
        S.add("dve", lambda e: e.tensor_tensor(out=so[:, :, 0, :], in0=tv[0], in1=tv[1], op=ALU.subtract),
              reads=[ba, bb2], writes=[sb])
        S.add("dve", lambda e: e.tensor_tensor(out=so[:, :, 1, :], in0=tv[2], in1=tv[3], op=ALU.add),
              reads=[bc, bd, rtb], writes=[sb])
        c0 = n * 512
        S.add("sp", (lambda e: e.dma_start(out=P[r0:r0 + 128, c0:c0 + 512], in_=sa)), reads=[sb, rtb], dma=True)

    linear_phase(C, None, "ia", NTOK, x_src, W, 4 * MIXW + MEMW, epilogue, norm_gain=gain, pre=pre)


def load_T(C, dstT, src, tok0, ntok, col0, writes):
    S = C.S
    for r in range(ntok // 512):
        S.add("sp", (lambda e, r=r: e.dma_start_transpose(
            out=dstT[:, r * 512:(r + 1) * 512], in_=src[tok0 + r * 512:tok0 + (r + 1) * 512, col0:col0 + 128])),
            writes=[writes[r]], dma=True)


def retention_phase(C, P, ycat, dmaskT_d, qdec_d, kdec_d):
    nc, S = C.nc, C.S
    NC = SEQ // 128
    lg = [float(np.log1p(-2.0 ** (-5.0 - h))) for h in range(NH)]
    cdec = [float(np.exp(np.float32(l) * 128)) for l in lg]
    with ExitStackCompat() as stack:
        T_ = lambda name, shape, dt: stack.enter_context(_sbt(nc, "rt_" + name, shape, dt))
        dmask = T_("dmask", [128, NH, 128], F32)
        qdec = T_("qdec", [128, NH, 128], F32)
        kdec = T_("kdec", [128, NH], F32)
        qT = T_("qT", [128, 2, SEQ], BF16)
        kT = T_("kT", [128, 2, SEQ], BF16)
        qTd = T_("qTd", [128, 2, SEQ], BF16)
        ktok = T_("ktok", [128, 2, NC, 128], BF16)
        vtok = T_("vtok", [128, 2, NC, 128], BF16)
        gtok = T_("gtok", [128, 2, NC, 128], BF16)
        osb = T_("osb", [128, 2, NC, 128], F32)
        yo = T_("yo", [128, 2, NC, 128], BF16)
        state = T_("state", [128, 2, 128], F32)
        stbf = T_("stbf", [128, 2, 128], BF16)
        stm = T_("stm", [128, 3, 128], BF16)
        junk = T_("junk", [128, 128], BF16)
        st = T_("st", [128, 2, 5, NC], F32)
        cst = T_("cst", [128, 1], F32)
        b_const = Buf()
        S.add("sp", lambda e: e.dma_start(out=dmask[:], in_=dmaskT_d), writes=[b_const], dma=True)
        S.add("sp", lambda e: e.dma_start(out=qdec[:], in_=qdec_d), writes=[b_const], dma=True)
        S.add("sp", lambda e: e.dma_start(out=kdec[:], in_=kdec_d), writes=[b_const], dma=True)
        b_eps = Buf()
        S.add("dve", lambda e: e.memset(cst[:, 0:1], EPS), writes=[b_eps])
        b_ycst = Buf()
        stm_ring = Ring([stm[:, i, :] for i in range(3)])
        stbf_ring = Ring([stbf[:, i, :] for i in range(2)])
        slot_b = [{k: Buf() for k in ("qT0", "qT1", "qT2", "qT3", "kT0", "kT1", "kT2", "kT3", "qTd", "ktok",
                                      "vtok", "gtok", "osb", "yo", "state", "sum", "sq", "stat")} for _ in range(2)]
        it = 0
        for s in range(NSEQ):
            for h in range(NH):
                sl = it % 2
                it += 1
                B = slot_b[sl]
                tok0 = s * SEQ
                bq = [B["qT%d" % r] for r in range(4)]
                bk = [B["kT%d" % r] for r in range(4)]
                load_T(C, qT[:, sl, :], P, tok0, SEQ, h * HD, bq)
                load_T(C, kT[:, sl, :], P, tok0, SEQ, MIXW + h * HD, bk)
                for name, col, tl in (("ktok", MIXW + h * HD, ktok), ("vtok", 2 * MIXW + h * HD, vtok),
                                      ("gtok", 3 * MIXW + h * HD, gtok)):
                    S.add("sp", (lambda e, tl=tl, col=col, sl=sl, tok0=tok0: e.dma_start(
                        out=tl[:, sl], in_=P[tok0:tok0 + SEQ, col:col + HD].rearrange("(n p) d -> p n d", p=128))),
                        writes=[B[name]], dma=True)
                S.add("dve", (lambda e, sl=sl, h=h: e.tensor_tensor(
                    out=qTd[:, sl, :].rearrange("p (n c) -> p n c", c=128),
                    in0=qT[:, sl, :].rearrange("p (n c) -> p n c", c=128),
                    in1=qdec[:, h:h + 1, :].to_broadcast([128, NC, 128]), op=ALU.mult)),
                    reads=bq + [b_const], writes=[B["qTd"]])
                S.add("act", (lambda e, sl=sl, h=h: e.activation(
                    out=ktok[:, sl].rearrange("p n d -> p (n d)"), in_=ktok[:, sl].rearrange("p n d -> p (n d)"),
                    func=AF.Copy, scale=kdec[:, h:h + 1])),
                    reads=[b_const], writes=[B["ktok"]])
                sbf_prev = None
                for n in range(NC):
                    csl = slice(n * 128, (n + 1) * 128)
                    r = n // 4
                    p1, p1b = C.psum.next()
                    S.add("pe", (lambda e, p1=p1, sl=sl, csl=csl: e.matmul(
                        p1[:, 0:128], lhsT=kT[:, sl, csl], rhs=qT[:, sl, csl], start=True, stop=True)),
                        reads=[bk[r], bq[r]], writes=[p1b])
                    ma, mb = stm_ring.next()
                    S.add("dve", (lambda e, ma=ma, p1=p1, h=h: e.tensor_tensor(
                        out=ma, in0=p1[:, 0:128], in1=dmask[:, h, :], op=ALU.mult)),
                        reads=[p1b, b_const], writes=[mb])
                    p2, p2b = C.psum.next()

                    def mm_o(e, p2=p2, ma=ma, sl=sl, n=n, csl=csl, sbf=sbf_prev):
                        ins = e.matmul(p2[:, 0:128], lhsT=ma, rhs=vtok[:, sl, n, :], start=True, stop=(n == 0))
                        if n > 0:
                            ins = e.matmul(p2[:, 0:128], lhsT=qTd[:, sl, csl], rhs=sbf[0], start=False, stop=True)
                        return ins
                    rd = [mb, B["vtok"], B["qTd"]] + ([sbf_prev[1]] if n > 0 else [])
                    S.add("pe", mm_o, reads=rd, writes=[p2b])
                    S.add("act", (lambda e, p2=p2, sl=sl, n=n: e.activation(
                        out=osb[:, sl, n, :], in_=p2[:, 0:128], func=AF.Copy, accum_out=st[:, sl, 0, n:n + 1])),
                        reads=[p2b], writes=[B["osb"], B["sum"]])
                    S.add("act", (lambda e, p2=p2, sl=sl, n=n: e.activation(
                        out=junk[:], in_=p2[:, 0:128], func=AF.Square, accum_out=st[:, sl, 1, n:n + 1])),
                        reads=[p2b], writes=[B["sq"]])
                    if n < NC - 1:
                        p3, p3b = C.psum.next()
                        S.add("pe", (lambda e, p3=p3, sl=sl, n=n: e.matmul(
                            p3[:, 0:128], lhsT=ktok[:, sl, n, :], rhs=vtok[:, sl, n, :], start=True, stop=True)),
                            reads=[B["ktok"], B["vtok"]], writes=[p3b])
                        if n == 0:
                            S.add("dve", (lambda e, p3=p3, sl=sl: e.tensor_copy(out=state[:, sl, :], in_=p3[:, 0:128])),
                                  reads=[p3b], writes=[B["state"]])
                        else:
                            S.add("dve", (lambda e, p3=p3, sl=sl, h=h: e.scalar_tensor_tensor(
                                out=state[:, sl, :], in0=state[:, sl, :], scalar=cdec[h], in1=p3[:, 0:128],
                                op0=ALU.mult, op1=ALU.add)),
                                reads=[p3b], writes=[B["state"]])
                        sa, sb = stbf_ring.next()
                        S.add("act", (lambda e, sa=sa, sl=sl: e.activation(out=sa, in_=state[:, sl, :], func=AF.Copy)),
                              reads=[B["state"]], writes=[sb])
                        sbf_prev = (sa, sb)
                c_sum, c_sq = st[:, sl, 0, :], st[:, sl, 1, :]
                c_mean, c_var, c_rstd = st[:, sl, 2, :], st[:, sl, 3, :], st[:, sl, 4, :]
                bs = B["stat"]
                S.add("dve", (lambda e, c_mean=c_mean, c_sum=c_sum: e.tensor_scalar(
                    out=c_mean, in0=c_sum, scalar1=1.0 / HD, scalar2=None, op0=ALU.mult)),
                    reads=[B["sum"]], writes=[bs])
                S.add("dve", (lambda e, c_mean=c_mean, c_rstd=c_rstd: e.tensor_tensor(
                    out=c_rstd, in0=c_mean, in1=c_mean, op=ALU.mult)), writes=[bs])
                S.add("dve", (lambda e, c_var=c_var, c_sq=c_sq, c_rstd=c_rstd: e.scalar_tensor_tensor(
                    out=c_var, in0=c_sq, scalar=1.0 / HD, in1=c_rstd, op0=ALU.mult, op1=ALU.subtract)),
                    reads=[B["sq"]], writes=[bs])
                S.add("act", (lambda e, c_var=c_var: e.activation(out=c_var, in_=c_var, func=AF.Sqrt,
                                                                  bias=cst[:, 0:1], scale=1.0)),
                      reads=[b_eps], writes=[bs])
                S.add("dve", (lambda e, c_var=c_var, c_rstd=c_rstd: e.reciprocal(out=c_rstd, in_=c_var)), writes=[bs])
                o3 = osb[:, sl]
                S.add("dve", (lambda e, o3=o3, c_mean=c_mean: e.tensor_tensor(
                    out=o3, in0=o3, in1=c_mean.unsqueeze(2).to_broadcast([128, NC, 128]), op=ALU.subtract)),
                    reads=[bs], writes=[B["osb"]])
                S.add("dve", (lambda e, o3=o3, c_rstd=c_rstd: e.tensor_tensor(
                    out=o3, in0=o3, in1=c_rstd.unsqueeze(2).to_broadcast([128, NC, 128]), op=ALU.mult)),
                    reads=[bs], writes=[B["osb"]])
                S.add("dve", (lambda e, o3=o3, sl=sl: e.tensor_tensor(
                    out=yo[:, sl], in0=o3, in1=gtok[:, sl], op=ALU.mult)),
                    reads=[B["osb"], B["gtok"]], writes=[B["yo"]])
                S.add("sp", (lambda e, sl=sl, tok0=tok0, h=h: e.dma_start(
                    out=ycat[tok0:tok0 + SEQ, h * HD:(h + 1) * HD].rearrange("(n p) d -> p n d", p=128),
                    in_=yo[:, sl])), reads=[B["yo"]], writes=[b_ycst], dma=True)
        S.emit()


def memattn_phase(C, Q, qcol0, MKV, ycat):
    nc, S = C.nc, C.S
    NTL = SEQ // 128
    scale = float(HD ** -0.5)
    with ExitStackCompat() as stack:
        T_ = lambda name, shape, dt: stack.enter_context(_sbt(nc, "ma_" + name, shape, dt))
        mkT = T_("mkT", [128, NMH, NMEM], BF16)
        mv = T_("mv", [128, 2, MEMW], BF16)
        qmT = T_("qmT", [128, NMH, SEQ], BF16)
        p_sb = T_("p", [128, 3, NMEM], BF16)
        pT = T_("pT", [128, 3, 2, 128], BF16)
        yo = T_("yo", [128, 3, MEMW], BF16)
        st = T_("st", [128, 4, 4], F32)
        p_ring = Ring([p_sb[:, i, :] for i in range(3)])
        pT_ring = Ring([pT[:, i] for i in range(3)])
        yo_ring = Ring([yo[:, i, :] for i in range(3)])
        st_ring = Ring([st[:, i, :] for i in range(4)])
        b_mk, b_mv = Buf(), Buf()
        b_q = [[Buf() for _ in range(4)] for _ in range(NMH)]
        for s in range(NSEQ):
            tok0 = s * SEQ
            for hm in range(NMH):
                S.add("sp", (lambda e, hm=hm, s=s: e.dma_start_transpose(
                    out=mkT[:, hm, :], in_=MKV[s * NMEM:(s + 1) * NMEM, hm * HD:(hm + 1) * HD])),
                    writes=[b_mk], dma=True)
                load_T(C, qmT[:, hm, :], Q, tok0, SEQ, qcol0 + hm * HD, b_q[hm])
            S.add("sp", (lambda e, s=s: e.dma_start(
                out=mv[:], in_=MKV[s * NMEM:(s + 1) * NMEM, MEMW:2 * MEMW].rearrange("(c p) d -> p c d", p=128))),
                writes=[b_mv], dma=True)
            for j in range(NTL):
                ya, yb = yo_ring.next()
                for hm in range(NMH):
                    ps, psb = C.psum.next()
                    S.add("pe", (lambda e, ps=ps, hm=hm, j=j: e.matmul(
                        ps[:, 0:NMEM], lhsT=qmT[:, hm, j * 128:(j + 1) * 128], rhs=mkT[:, hm, :],
                        start=True, stop=True)), reads=[b_q[hm][j // 4], b_mk], writes=[psb])
                    sa, sb = st_ring.next()
                    S.add("dve", (lambda e, ps=ps, sa=sa: e.tensor_reduce(
                        out=sa[:, 0:1], in_=ps[:, 0:NMEM], axis=AX.X, op=ALU.max)), reads=[psb], writes=[sb])
                    S.add("dve", (lambda e, sa=sa: e.tensor_scalar(
                        out=sa[:, 1:2], in0=sa[:, 0:1], scalar1=-scale, scalar2=None, op0=ALU.mult)), writes=[sb])
                    pa, pb = p_ring.next()
                    S.add("act", (lambda e, ps=ps, sa=sa, pa=pa: e.activation(
                        out=pa, in_=ps[:, 0:NMEM], func=AF.Exp, bias=sa[:, 1:2], scale=scale,
                        accum_out=sa[:, 2:3])), reads=[psb], writes=[pb, sb])
                    S.add("dve", (lambda e, sa=sa: e.reciprocal(out=sa[:, 3:4], in_=sa[:, 2:3])), writes=[sb])
                    pt, ptb = C.psum.next()
                    ptv = pt.bitcast(BF16)

                    def tr(e, ptv=ptv, pa=pa):
                        ins = None
                        for c in range(2):
                            ins = e.transpose(out=ptv[:, c * 128:(c + 1) * 128], in_=pa[:, c * 128:(c + 1) * 128],
                                              identity=C.ident[:])
                        return ins
                    S.add("pe", tr, reads=[pb, C.b_ident], writes=[ptb])
                    ta, tb = pT_ring.next()
                    evac_copy(C, ta, ptv[:, 0:256].rearrange("p (c t) -> p c t", c=2), [ptb], [tb])
                    po, pob = C.psum.next()

                    def mm(e, po=po, ta=ta, hm=hm):
                        ins = None
                        for c in range(2):
                            ins = e.matmul(po[:, 0:128], lhsT=ta[:, c, :], rhs=mv[:, c, hm * HD:(hm + 1) * HD],
                                           start=(c == 0), stop=(c == 1))
                        return ins
                    S.add("pe", mm, reads=[tb, b_mv], writes=[pob])
                    S.add("act", (lambda e, po=po, ya=ya, hm=hm, sa=sa: e.activation(
                        out=ya[:, hm * HD:(hm + 1) * HD], in_=po[:, 0:128], func=AF.Copy, scale=sa[:, 3:4])),
                        reads=[pob, sb], writes=[yb])
                r0 = tok0 + j * 128
                S.add("sp", (lambda e, ya=ya, r0=r0: e.dma_start(out=ycat[r0:r0 + 128, MIXW:MIXW + MEMW], in_=ya)),
                      reads=[yb], dma=True)
        S.emit()


class PostNorm:
    def __init__(self, C, stack, pfx, gain_dram, x_in, x_out, wres):
        nc = C.nc
        self.C, self.x_in, self.x_out, self.wres = C, x_in, x_out, wres
        self.ys = stack.enter_context(_sbt(nc, pfx + "_ys", [128, 3, 512], F32))
        self.xy = stack.enter_context(_sbt(nc, pfx + "_xy", [128, 4, D], F32))
        self.g = stack.enter_context(_sbt(nc, pfx + "_gp", [128, D], F32))
        self.junk = stack.enter_context(_sbt(nc, pfx + "_pj", [128, 512], BF16))
        self.st = stack.enter_context(_sbt(nc, pfx + "_pst", [128, 2, 24], F32))
        self.cst = stack.enter_context(_sbt(nc, pfx + "_pc", [128, 1], F32))
        self.ys_ring = Ring([self.ys[:, i, :] for i in range(3)])
        self.xy_ring = Ring([self.xy[:, i, :] for i in range(4)])
        self.b_g, self.b_eps = Buf(), Buf()
        self.b_st = [[Buf(), Buf()], [Buf(), Buf()]]
        self.b_y = {}
        g, cst = self.g, self.cst
        C.S.add("sp", lambda e: e.dma_start(out=g[:], in_=gain_dram.partition_broadcast(128)),
                writes=[self.b_g], dma=True)
        C.S.add("dve", lambda e: e.memset(cst[:, 0:1], EPS), writes=[self.b_eps])

    def epilogue(self, t, j, n, gs, pa, pb, r0):
        C, S = self.C, self.C.S
        ya, yb = self.ys_ring.next()
        st = self.st
        b_ss = self.b_st[t % 2][0]
        x_out = self.x_out
        junk = self.junk
        S.add("act", (lambda e: e.activation(out=ya, in_=pa, func=AF.Copy)), reads=[pb], writes=[yb])
        S.add("act", (lambda e: e.activation(out=junk[:], in_=pa, func=AF.Square,
                                             accum_out=st[:, t % 2, j * 4 + n:j * 4 + n + 1])),
              reads=[pb], writes=[b_ss])
        by = Buf()
        self.b_y.setdefault((t, j), []).append(by)
        S.add("sp", (lambda e: e.dma_start(out=x_out[r0:r0 + 128, n * 512:(n + 1) * 512], in_=ya)),
              reads=[yb], writes=[by], dma=True)

    def finalize(self, t, tok0, NJ):
        C, S = self.C, self.C.S
        st, gt, cst = self.st, self.g, self.cst
        b_ss, b_r = self.b_st[t % 2]
        c_ss = st[:, t % 2, 0:16]
        c_r = st[:, t % 2, 16:16 + NJ]
        x_in, x_out, wres = self.x_in, self.x_out, self.wres
        S.add("dve", lambda e: e.tensor_reduce(out=c_r, in_=c_ss[:, 0:NJ * 4].rearrange("p (j n) -> p j n", n=4),
                                               axis=AX.X, op=ALU.add), reads=[b_ss], writes=[b_r])
        S.add("act", lambda e: e.activation(out=c_r, in_=c_r, func=AF.Sqrt, scale=1.0 / D, bias=cst[:, 0:1]),
              reads=[self.b_eps], writes=[b_r])
        S.add("dve", lambda e: e.reciprocal(out=c_r, in_=c_r), writes=[b_r])
        for j in range(NJ):
            r0 = tok0 + j * 128
            ya, yb = self.xy_ring.next()
            xa, xb = self.xy_ring.next()
            bys = self.b_y.pop((t, j))
            S.add("sp", (lambda e, ya=ya, r0=r0: e.dma_start(out=ya, in_=x_out[r0:r0 + 128, :])),
                  reads=bys, writes=[yb], dma=True)
            S.add("sp", (lambda e, xa=xa, r0=r0: e.dma_start(out=xa, in_=x_in[r0:r0 + 128, :])),
                  writes=[xb], dma=True)
            S.add("dve", (lambda e, ya=ya, j=j: e.scalar_tensor_tensor(
                out=ya, in0=ya, scalar=c_r[:, j:j + 1], in1=gt[:], op0=ALU.mult, op1=ALU.mult)),
                reads=[b_r, self.b_g], writes=[yb])
            S.add("dve", (lambda e, ya=ya, xa=xa: e.scalar_tensor_tensor(
                out=ya, in0=ya, scalar=wres, in1=xa, op0=ALU.mult, op1=ALU.add)),
                reads=[xb], writes=[yb])
            S.add("sp", (lambda e, ya=ya, r0=r0: e.dma_start(out=x_out[r0:r0 + 128, :], in_=ya)),
                  reads=[yb], writes=bys, dma=True)


def outproj_phase(C, ycat, W, gain, x_in, x_out):
    R = {}

    def pre(stack):
        R["pn"] = PostNorm(C, stack, "op", gain, x_in, x_out, 1.0)

    def epilogue(t, j, n, gs, pa, pb, r0):
        R["pn"].epilogue(t, j, n, gs, pa, pb, r0)

    def post(t, tok0, NJ):
        R["pn"].finalize(t, tok0, NJ)

    linear_phase(C, None, "op", NTOK, ycat, W, D, epilogue, norm_gain=None, pre=pre, post=post)


def stickbreak_phase(C, Q, KV, ycat, tril_f_d, tril_b_d):
    nc, S = C.nc, C.S
    NB = SEQ // 128
    scale = float(HD ** -0.5)
    with ExitStackCompat() as stack:
        T_ = lambda name, shape, dt: stack.enter_context(_sbt(nc, "sb_" + name, shape, dt))
        qT = T_("qT", [128, 2, SEQ], BF16)
        kT = T_("kT", [128, 2, SEQ], BF16)
        vtok = T_("vtok", [128, 2, NB, 128], BF16)
        Et = T_("E", [128, 2, SEQ], F32)
        Lt = T_("L", [128, 2, SEQ], F32)
        Ct = T_("C", [128, 2, SEQ], F32)
        At = T_("A", [128, 2, SEQ], BF16)
        ATt = T_("AT", [128, 2, NB, 128], BF16)
        yo = T_("yo", [128, 2, NB, 128], BF16)
        trf = T_("trf", [128, 128], F32)
        trb = T_("trb", [128, 128], BF16)
        ones = T_("ones", [128, 1], F32)
        b_const = Buf()
        S.add("sp", lambda e: e.dma_start(out=trf[:], in_=tril_f_d), writes=[b_const], dma=True)
        S.add("sp", lambda e: e.dma_start(out=trb[:], in_=tril_b_d), writes=[b_const], dma=True)
        S.add("dve", lambda e: e.memset(ones[:], 1.0), writes=[b_const])
        b_ycst = Buf()
        hb = [{k: Buf() for k in ("q0", "q1", "q2", "q3", "k0", "k1", "k2", "k3", "v", "yo")} for _ in range(2)]
        wb = [{k: Buf() for k in ("E", "L", "C", "A", "AT")} for _ in range(2)]
        it = 0
        wi = 0
        for s in range(NSEQ):
            tok0 = s * SEQ
            for h in range(NH):
                sl = it % 2
                it += 1
                B = hb[sl]
                bq = [B["q%d" % r] for r in range(4)]
                bk = [B["k%d" % r] for r in range(4)]
                load_T(C, qT[:, sl, :], Q, tok0, SEQ, h * HD, bq)
                load_T(C, kT[:, sl, :], KV, tok0, SEQ, h * HD, bk)
                S.add("sp", (lambda e, sl=sl, tok0=tok0, h=h: e.dma_start(
                    out=vtok[:, sl], in_=KV[tok0:tok0 + SEQ, MIXW + h * HD:MIXW + (h + 1) * HD].rearrange(
                        "(n p) d -> p n d", p=128))), writes=[B["v"]], dma=True)
                for i in range(NB):
                    L = (i + 1) * 128
                    ws = wi % 2
                    wi += 1
                    W_ = wb[ws]
                    E, Lb, Cb, A, AT = Et[:, ws, :], Lt[:, ws, :], Ct[:, ws, :], At[:, ws, :], ATt[:, ws]
                    nbk = (L + 511) // 512
                    zb = [C.psum.next() for _ in range(nbk)]
                    for c in range(nbk):
                        w = min(512, L - c * 512)
                        S.add("pe", (lambda e, c=c, w=w, ps=zb[c][0], sl=sl, i=i: e.matmul(
                            ps[:, 0:w], lhsT=qT[:, sl, i * 128:(i + 1) * 128], rhs=kT[:, sl, c * 512:c * 512 + w],
                            start=True, stop=True)), reads=[bq[i // 4], bk[c]], writes=[zb[c][1]])
                    for c in range(nbk):
                        w = min(512, L - c * 512)
                        S.add("act", (lambda e, c=c, w=w, ps=zb[c][0], E=E: e.activation(
                            out=E[:, c * 512:c * 512 + w], in_=ps[:, 0:w], func=AF.Exp, scale=-scale)),
                            reads=[zb[c][1]], writes=[W_["E"]])
                    S.add("act", (lambda e, E=E, L=L: e.activation(out=E[:, 0:L], in_=E[:, 0:L], func=AF.Ln,
                                                                    bias=ones[:, 0:1], scale=1.0)),
                          reads=[b_const], writes=[W_["E"]])
                    for c in range(nbk):
                        w = min(512, L - c * 512)
                        S.add("dve", (lambda e, c=c, w=w, ps=zb[c][0], E=E, Lb=Lb: e.scalar_tensor_tensor(
                            out=Lb[:, c * 512:c * 512 + w], in0=ps[:, 0:w], scalar=-scale,
                            in1=E[:, c * 512:c * 512 + w], op0=ALU.mult, op1=ALU.subtract)),
                            reads=[zb[c][1], W_["E"]], writes=[W_["L"]])
                    S.add("dve", (lambda e, Lb=Lb, L=L: e.tensor_tensor(
                        out=Lb[:, L - 128:L], in0=Lb[:, L - 128:L], in1=trf[:], op=ALU.mult)),
                        reads=[b_const], writes=[W_["L"]])
                    S.add("dve", (lambda e, Lb=Lb, Cb=Cb, L=L: e.tensor_tensor_scan(
                        out=Cb[:, 0:L], data0=ones[:, 0:1].to_broadcast([128, L]), data1=Lb[:, 0:L], initial=0.0,
                        op0=ALU.mult, op1=ALU.add)), reads=[W_["L"], b_const], writes=[W_["C"]])
                    S.add("dve", (lambda e, Lb=Lb, Cb=Cb, E=E, L=L: e.scalar_tensor_tensor(
                        out=Lb[:, 0:L], in0=Cb[:, 0:L], scalar=-1.0, in1=E[:, 0:L], op0=ALU.mult, op1=ALU.subtract)),
                        reads=[W_["C"], W_["E"]], writes=[W_["L"]])
                    S.add("act", (lambda e, Lb=Lb, Cb=Cb, A=A, L=L: e.activation(
                        out=A[:, 0:L], in_=Lb[:, 0:L], func=AF.Exp, bias=Cb[:, L - 1:L], scale=1.0)),
                        reads=[W_["L"], W_["C"]], writes=[W_["A"]])
                    S.add("dve", (lambda e, A=A, L=L: e.tensor_tensor(
                        out=A[:, L - 128:L], in0=A[:, L - 128:L], in1=trb[:], op=ALU.mult)),
                        reads=[b_const], writes=[W_["A"]])
                    ntb = (i + 1 + 7) // 8
                    for tb_ in range(ntb):
                        nblk = min(8, i + 1 - tb_ * 8)
                        pt, ptb = C.psum.next()
                        ptv = pt.bitcast(BF16)

                        def tr(e, ptv=ptv, A=A, tb_=tb_, nblk=nblk):
                            ins = None
                            for bb in range(nblk):
                                blk = tb_ * 8 + bb
                                ins = e.transpose(out=ptv[:, bb * 128:(bb + 1) * 128],
                                                  in_=A[:, blk * 128:(blk + 1) * 128], identity=C.ident[:])
                            return ins
                        S.add("pe", tr, reads=[W_["A"], C.b_ident], writes=[ptb])
                        evac_copy(C, AT[:, tb_ * 8:tb_ * 8 + nblk, :],
                                  ptv[:, 0:nblk * 128].rearrange("p (b t) -> p b t", b=nblk), [ptb], [W_["AT"]])
                    po, pob = C.psum.next()

                    def mm(e, po=po, AT=AT, sl=sl, i=i):
                        ins = None
                        for bb in range(i + 1):
                            ins = e.matmul(po[:, 0:128], lhsT=AT[:, bb, :], rhs=vtok[:, sl, bb, :],
                                           start=(bb == 0), stop=(bb == i))
                        return ins
                    S.add("pe", mm, reads=[W_["AT"], B["v"]], writes=[pob])
                    evac_copy(C, yo[:, sl, i, :], po[:, 0:128], [pob], [B["yo"]])
                S.add("sp", (lambda e, sl=sl, tok0=tok0, h=h: e.dma_start(
                    out=ycat[tok0:tok0 + SEQ, h * HD:(h + 1) * HD].rearrange("(n p) d -> p n d", p=128),
                    in_=yo[:, sl])), reads=[B["yo"]], writes=[b_ycst], dma=True)
        S.emit()


W_SPECS = [
    ("ffn1_norm_pre", [2, D]), ("ffn1_norm_post", [2, D]),
    ("ffn1_w_gate", [2, D, DFF]), ("ffn1_w_up", [2, D, DFF]), ("ffn1_w_down", [2, DFF, D]),
    ("mix_norm_pre", [2, D]), ("mix_norm_post", [2, D]), ("mem_norm", [2, D]),
    ("w_mem_kv", [2, D, 2 * MEMW]), ("w_o", [2, D, D]), ("ret_w_in", [1, D, 4 * MIXW + MEMW]),
    ("kv_norm", [D]), ("w_kv_shared", [D, 2 * MIXW]), ("sb_w_in", [1, D, MIXW + MEMW]),
    ("ffn2_norm_pre", [2, D]), ("ffn2_norm_post", [2, D]),
    ("ffn2_w_gate", [2, D, DFF]), ("ffn2_w_up", [2, D, DFF]), ("ffn2_w_down", [2, DFF, D]),
]


def make_consts():
    h = np.arange(NH, dtype=np.float32)
    lg = np.log1p(-np.exp2(-5.0 - h)).astype(np.float32)
    idx = np.arange(128, dtype=np.float32)
    diff = idx[None, :] - idx[:, None]
    dm = np.where(diff >= 0, np.exp(lg[:, None, None] * np.maximum(diff, 0.0)[None]), 0.0).astype(np.float32)
    dmaskT = np.ascontiguousarray(dm.transpose(1, 0, 2))
    qd = np.exp(lg[:, None] * (idx + 1.0)[None, :]).astype(np.float32)
    qdec = np.ascontiguousarray(np.broadcast_to(qd[None], (128, NH, 128))).astype(np.float32)
    kdec = np.ascontiguousarray(np.exp(lg[None, :] * (127.0 - idx)[:, None]).astype(np.float32))
    invf = (10000.0 ** (-np.arange(0, HD, 2, dtype=np.float32) / HD)).astype(np.float32)
    tril = (idx[None, :] < idx[:, None]).astype(np.float32)
    return {
        "c_ident": np.eye(128, dtype=np.float32).astype(ml_dtypes.bfloat16),
        "c_invf": invf, "c_dmaskT": dmaskT, "c_qdec": qdec, "c_kdec": kdec,
        "c_tril_f": tril, "c_tril_b": tril.astype(ml_dtypes.bfloat16),
    }


def build_program(phases=None):
    nc = bass.Bass("TRN2", target_bir_lowering=False)
    IN = lambda name, shape, dt=F32: nc.dram_tensor(name, shape, dt, kind="ExternalInput").ap()
    SCR = lambda name, shape, dt: nc.dram_tensor(name, shape, dt, kind="Internal").ap()
    x = IN("x", [NTOK, D])
    mem = IN("mem", [NSEQ * NMEM, D])
    pos = IN("positions", [128, NTOK // 128], I32)
    Wt = {name: IN(name, shape) for name, shape in W_SPECS}
    cst = {
        "c_ident": IN("c_ident", [128, 128], BF16), "c_invf": IN("c_invf", [64]),
        "c_dmaskT": IN("c_dmaskT", [128, NH, 128]), "c_qdec": IN("c_qdec", [128, NH, 128]),
        "c_kdec": IN("c_kdec", [128, NH]), "c_tril_f": IN("c_tril_f", [128, 128]),
        "c_tril_b": IN("c_tril_b", [128, 128], BF16),
    }
    out = nc.dram_tensor("out", [NTOK, D], F32, kind="ExternalOutput").ap()
    xa = SCR("s_xa", [NTOK, D], F32)
    xb = SCR("s_xb", [NTOK, D], F32)
    P = SCR("s_P", [NTOK, 4 * MIXW + MEMW], BF16)
    KVs = SCR("s_KV", [NTOK, 2 * MIXW], BF16)
    ycat = SCR("s_ycat", [NTOK, D], BF16)
    MKV = SCR("s_MKV", [NSEQ * NMEM, 2 * MEMW], BF16)
    with ExitStackCompat() as stack:
        C = make_ctx(nc, stack, cst["c_ident"])

        def ffn(l, which, src, dst):
            p = "ffn%d_" % which
            ffn_phase(C, src, dst, Wt[p + "w_gate"][l], Wt[p + "w_up"][l], Wt[p + "w_down"][l],
                      Wt[p + "norm_pre"][l], Wt[p + "norm_post"][l], NTOK)

        ffn(0, 1, x, xa)
        simple_linear(C, "mk0", NSEQ * NMEM, mem, Wt["mem_norm"][0], Wt["w_mem_kv"][0], 2 * MEMW, MKV)
        inproj_a_phase(C, xa, Wt["mix_norm_pre"][0], Wt["ret_w_in"][0], P, pos, cst["c_invf"])
        retention_phase(C, P, ycat, cst["c_dmaskT"], cst["c_qdec"], cst["c_kdec"])
        memattn_phase(C, P, 4 * MIXW, MKV, ycat)
        outproj_phase(C, ycat, Wt["w_o"][0], Wt["mix_norm_post"][0], xa, xb)
        ffn(0, 2, xb, xa)
        simple_linear(C, "kvs", NTOK, xa, Wt["kv_norm"], Wt["w_kv_shared"], 2 * MIXW, KVs)
        ffn(1, 1, xa, xb)
        simple_linear(C, "mk1", NSEQ * NMEM, mem, Wt["mem_norm"][1], Wt["w_mem_kv"][1], 2 * MEMW, MKV)
        simple_linear(C, "ib", NTOK, xb, Wt["mix_norm_pre"][1], Wt["sb_w_in"][0], MIXW + MEMW, P)
        stickbreak_phase(C, P, KVs, ycat, cst["c_tril_f"], cst["c_tril_b"])
        memattn_phase(C, P, MIXW, MKV, ycat)
        outproj_phase(C, ycat, Wt["w_o"][1], Wt["mix_norm_post"][1], xb, xa)
        ffn(1, 2, xa, out)
        C.S.close()
    return nc


_CACHE = {}


def kernel(**inputs):
    if "nc" not in _CACHE:
        _CACHE["nc"] = build_program()
        _CACHE["consts"] = make_consts()
    nc = _CACHE["nc"]
    consts = _CACHE["consts"]
    x = np.ascontiguousarray(inputs["x"], dtype=np.float32)
    mem = np.ascontiguousarray(inputs["mem"], dtype=np.float32)
    pos = np.ascontiguousarray(inputs["positions"], dtype=np.int32)
    shared = {name: np.ascontiguousarray(inputs[name], dtype=np.float32) for name, _ in W_SPECS}
    shared.update(consts)
    in_maps = []
    for c in range(N_CORES):
        m = dict(shared)
        m["x"] = x[c * NSEQ:(c + 1) * NSEQ].reshape(NTOK, D)
        m["mem"] = mem[c * NSEQ:(c + 1) * NSEQ].reshape(NSEQ * NMEM, D)
        m["positions"] = np.ascontiguousarray(pos[c * NSEQ:(c + 1) * NSEQ].reshape(NTOK // 128, 128).T)
        in_maps.append(m)
    res = run_bass_kernel_spmd(nc, in_maps, core_ids=list(range(N_CORES)))
    outs = [np.asarray(r["out"]).reshape(NSEQ, SEQ, D) for r in res.results]
    return np.concatenate(outs, axis=0).astype(np.float32)
```

```python
import numpy as np
import ml_dtypes
from contextlib import ExitStack as ExitStackCompat
import concourse.bass as bass
import concourse.mybir as mybir
from concourse.bass_utils import run_bass_kernel_spmd

F32 = mybir.dt.float32
BF16 = mybir.dt.bfloat16
I32 = mybir.dt.int32
AF = mybir.ActivationFunctionType
ALU = mybir.AluOpType
AX = mybir.AxisListType

D = 2048
DFF = 5632
NFF = DFF // 128
KD = D // 128
SEQ = 2048
NSEQ = 2
NTOK = SEQ * NSEQ
NMEM = 256
HD = 128
NH = 12
NMH = 4
MIXW = NH * HD
MEMW = NMH * HD
EPS = 1e-6
N_CORES = 8

SB_SEQUENTIAL = False
SB_OLDMATH = True
SB_POOL_ENG = "pool"
N_DSEM = 12
CSEM_CAP = 30000


_UID = [0]


def _sbt(nc, name, shape, dt):
    _UID[0] += 1
    return nc.sbuf_tensor("%s_%d" % (name, _UID[0]), shape, dt)


class Buf:
    __slots__ = ("name", "w", "r")

    def __init__(self, name=""):
        self.name = name
        self.w = None
        self.r = {}


class Op:
    __slots__ = ("q", "fn", "deps", "need", "no", "dma")


class Sched:
    QUEUES = ("pe", "act", "dve", "pool", "sp")

    def __init__(self, nc):
        self.nc = nc
        self.ops = {q: [] for q in self.QUEUES}
        self.count = {}
        self.sems = {}
        self.waited = {q: {} for q in self.QUEUES}
        self._semctx = []

    def _sem(self, fam, idx):
        key = (fam, idx)
        if key not in self.sems:
            cm = self.nc.semaphore("s_%s_%s_%d" % (fam[0], "d" if fam[1] else "c", idx))
            h = cm.__enter__()
            self._semctx.append(cm)
            self.sems[key] = h
        return self.sems[key]

    def close(self):
        for cm in reversed(self._semctx):
            cm.__exit__(None, None, None)
        self._semctx = []

    def add(self, q, fn, reads=(), writes=(), dma=False):
        op = Op()
        op.q = q
        op.fn = fn
        op.dma = dma
        op.need = dma
        op.no = None
        deps = []
        for b in reads:
            if b.w is not None:
                deps.append(b.w)
        for b in writes:
            if b.w is not None:
                deps.append(b.w)
            deps.extend(b.r.values())
        fam = (q, dma)
        for b in reads:
            b.r[fam] = op
        for b in writes:
            b.w = op
            b.r = {}
        out = []
        seen = set()
        for d in deps:
            if d is op or id(d) in seen:
                continue
            seen.add(id(d))
            if d.q == "pe" and q == "pe" and not d.dma and not dma:
                continue
            d.need = True
            out.append(d)
        op.deps = out
        self.ops[q].append(op)
        return op

    def _semval(self, d):
        fam = (d.q, d.dma)
        if d.dma:
            idx = (d.no - 1) % N_DSEM
            return self._sem(fam, idx), 16 * ((d.no - 1) // N_DSEM + 1)
        idx = (d.no - 1) // CSEM_CAP
        v = (d.no - 1) % CSEM_CAP + 1
        return self._sem(fam, idx), v

    def emit(self, barrier=True):
        nc = self.nc
        if barrier:
            for q in self.QUEUES:
                for op in reversed(self.ops[q]):
                    if not op.dma:
                        op.need = True
                        break
        for q in self.QUEUES:
            for op in self.ops[q]:
                if op.need and op.no is None:
                    fam = (q, op.dma)
                    self.count[fam] = self.count.get(fam, 0) + 1
                    op.no = self.count[fam]
        finals = []
        if barrier:
            for fam, n in self.count.items():
                if fam[1]:
                    for idx in range(min(n, N_DSEM)):
                        last_i = n - ((n - 1 - idx) % N_DSEM)
                        finals.append((self._sem(fam, idx), 16 * ((last_i - 1) // N_DSEM + 1)))
                else:
                    idx = (n - 1) // CSEM_CAP
                    v = (n - 1) % CSEM_CAP + 1
                    finals.append((self._sem(fam, idx), v))
        for q in self.QUEUES:
            for op in self.ops[q]:
                for d in op.deps:
                    self._semval(d)
                if op.need:
                    self._semval(op)

        def run_queue(q, e):
            waited = self.waited[q]
            for op in self.ops[q]:
                for d in op.deps:
                    sem, val = self._semval(d)
                    if waited.get(sem, 0) < val:
                        e.wait_ge(sem, val)
                        waited[sem] = val
                if op.dma and op.no > N_DSEM:
                    sem, val = self._semval(op)
                    if waited.get(sem, 0) < val - 16:
                        e.wait_ge(sem, val - 16)
                        waited[sem] = val - 16
                ins = op.fn(e)
                if op.need:
                    sem, _ = self._semval(op)
                    ins.then_inc(sem, 16 if op.dma else 1)
            for sem, val in finals:
                if waited.get(sem, 0) < val:
                    e.wait_ge(sem, val)
                    waited[sem] = val

        with nc.Block() as block:
            @block.tensor
            def _(e):
                run_queue("pe", e)

            @block.scalar
            def _(e):
                run_queue("act", e)

            @block.vector
            def _(e):
                run_queue("dve", e)

            @block.gpsimd
            def _(e):
                run_queue("pool", e)

            @block.sync
            def _(e):
                run_queue("sp", e)
        self.ops = {q: [] for q in self.QUEUES}


class Ring:
    def __init__(self, aps, name=""):
        self.aps = list(aps)
        self.bufs = [Buf("%s%d" % (name, i)) for i in range(len(self.aps))]
        self.i = 0

    def next(self):
        k = self.i % len(self.aps)
        self.i += 1
        return self.aps[k], self.bufs[k]


class Ctx:
    pass


def ffn_phase(C, x_in, x_out, wg, wu, wd, g_pre, g_post, ntok, dbg=None):
    nc, S = C.nc, C.S
    T = 512
    NJ = T // 128
    NT = ntok // T
    WC = 256
    NWC = DFF // WC
    QC = 11
    NQ = NFF // QC
    wg_v = wg.rearrange("(k p) c -> p k c", p=128)
    wu_v = wu.rearrange("(k p) c -> p k c", p=128)
    wd_v = wd.rearrange("(c p) n -> p c n", p=128)
    with (
        _sbt(nc, "f_hT", [128, KD, T], BF16) as hT,
        _sbt(nc, "f_aT", [128, NFF, T], BF16) as aT,
        _sbt(nc, "f_wg", [128, 2, KD, WC], BF16) as wgt,
        _sbt(nc, "f_wu", [128, 2, KD, WC], BF16) as wut,
        _sbt(nc, "f_wd", [128, 2, QC, 512], BF16) as wdt,
        _sbt(nc, "f_xs", [128, 4, D], F32) as xs,
        _sbt(nc, "f_xn", [128, 2, D], BF16) as xn,
        _sbt(nc, "f_ys", [128, 3, 512], F32) as ys,
        _sbt(nc, "f_sg", [128, 2, 512], F32) as sg,
        _sbt(nc, "f_gpre", [128, D], F32) as gpre,
        _sbt(nc, "f_gpost", [128, D], F32) as gpost,
        _sbt(nc, "f_junk", [128, D], BF16) as junk,
        _sbt(nc, "f_st", [128, 64], F32) as st,
    ):
        xs_ring = Ring([xs[:, i, :] for i in range(4)], "xs")
        xn_ring = Ring([xn[:, i, :] for i in range(2)], "xn")
        ys_ring = Ring([ys[:, i, :] for i in range(3)], "ys")
        sg_ring = Ring([sg[:, i, :] for i in range(2)], "sg")
        wg_ring = Ring([wgt[:, i] for i in range(2)], "wg")
        wu_ring = Ring([wut[:, i] for i in range(2)], "wu")
        wd_ring = Ring([wdt[:, i] for i in range(2)], "wd")
        b_gpre, b_gpost, b_eps = Buf(), Buf(), Buf()
        b_junk = Buf()
        S.add("sp", lambda e: e.dma_start(out=gpre[:], in_=g_pre.partition_broadcast(128)),
              writes=[b_gpre], dma=True)
        S.add("sp", lambda e: e.dma_start(out=gpost[:], in_=g_post.partition_broadcast(128)),
              writes=[b_gpost], dma=True)
        S.add("dve", lambda e: e.memset(st[:, 0:1], EPS), writes=[b_eps])

        b_hT = [Buf() for _ in range(NJ)]
        b_aT = [Buf() for _ in range(NFF)]
        b_stat = [[Buf() for _ in range(4)] for _ in range(2)]
        for t in range(NT):
            base = 4 + (t % 2) * 28
            c_sspre = st[:, base:base + 4]
            c_rpre = st[:, base + 4:base + 8]
            c_sspost = st[:, base + 8:base + 24]
            c_rpost = st[:, base + 24:base + 28]
            b_sspre, b_rpre, b_sspost, b_rpost = b_stat[t % 2]
            tok0 = t * T
            x_tiles = []
            for j in range(NJ):
                xa, xb = xs_ring.next()
                r0 = tok0 + j * 128
                S.add("sp", (lambda e, xa=xa, r0=r0: e.dma_start(out=xa, in_=x_in[r0:r0 + 128, :])),
                      writes=[xb], dma=True)
                S.add("act", (lambda e, xa=xa, j=j: e.activation(out=junk[:], in_=xa, func=AF.Square,
                                                                   accum_out=c_sspre[:, j:j + 1])),
                      reads=[xb], writes=[b_sspre])
                x_tiles.append((xa, xb))
            S.add("act", lambda e: e.activation(out=c_rpre, in_=c_sspre, func=AF.Sqrt,
                                                scale=1.0 / D, bias=st[:, 0:1]),
                  reads=[b_sspre, b_eps], writes=[b_rpre])
            S.add("dve", lambda e: e.reciprocal(out=c_rpre, in_=c_rpre), writes=[b_rpre])
            for j in range(NJ):
                xa, xb = x_tiles[j]
                na, nb = xn_ring.next()
                S.add("dve", (lambda e, xa=xa, na=na, j=j: e.scalar_tensor_tensor(
                    out=na, in0=xa, scalar=c_rpre[:, j:j + 1], in1=gpre[:], op0=ALU.mult, op1=ALU.mult)),
                    reads=[xb, b_rpre, b_gpre], writes=[nb])
                for half in range(2):
                    pa, pb = C.psum.next()
                    pv = pa.bitcast(BF16)

                    def tr(e, pv=pv, na=na, half=half):
                        ins = None
                        for i in range(8):
                            k = half * 8 + i
                            ins = e.transpose(out=pv[:, i * 128:(i + 1) * 128],
                                              in_=na[:, k * 128:(k + 1) * 128], identity=C.ident[:])
                        return ins
                    S.add("pe", tr, reads=[nb, C.b_ident], writes=[pb])
                    eng = "act" if half == 0 else "dve"
                    dst = hT[:, half * 8:(half + 1) * 8, j * 128:(j + 1) * 128]
                    src = pv.rearrange("p (i t) -> p i t", i=8)
                    if eng == "act":
                        S.add("act", (lambda e, dst=dst, src=src: e.activation(out=dst, in_=src, func=AF.Copy)),
                              reads=[pb], writes=[b_hT[j]])
                    else:
                        S.add("dve", (lambda e, dst=dst, src=src: e.tensor_copy(out=dst, in_=src)),
                              reads=[pb], writes=[b_hT[j]])
            for wc in range(NWC):
                ga, gb = wg_ring.next()
                ua, ub = wu_ring.next()
                c0 = wc * WC
                S.add("pool", (lambda e, ga=ga, c0=c0: e.dma_start(out=ga, in_=wg_v[:, :, c0:c0 + WC])),
                      writes=[gb], dma=True)
                S.add("pool", (lambda e, ua=ua, c0=c0: e.dma_start(out=ua, in_=wu_v[:, :, c0:c0 + WC])),
                      writes=[ub], dma=True)
                for cl in range(WC // 128):
                    c = wc * (WC // 128) + cl
                    pg, pgb = C.psum.next()
                    pu, pub = C.psum.next()

                    def mm(e, w=ga, cl=cl, ps=pg):
                        ins = None
                        for k in range(KD):
                            ins = e.matmul(ps, lhsT=w[:, k, cl * 128:(cl + 1) * 128], rhs=hT[:, k, :],
                                           start=(k == 0), stop=(k == KD - 1))
                        return ins
                    S.add("pe", mm, reads=[gb] + b_hT, writes=[pgb])

                    def mm2(e, w=ua, cl=cl, ps=pu):
                        ins = None
                        for k in range(KD):
                            ins = e.matmul(ps, lhsT=w[:, k, cl * 128:(cl + 1) * 128], rhs=hT[:, k, :],
                                           start=(k == 0), stop=(k == KD - 1))
                        return ins
                    S.add("pe", mm2, reads=[ub] + b_hT, writes=[pub])
                    sa, sb = sg_ring.next()
                    S.add("act", (lambda e, sa=sa, pg=pg: e.activation(out=sa, in_=pg, func=AF.Silu)),
                          reads=[pgb], writes=[sb])
                    S.add("dve", (lambda e, sa=sa, pu=pu, c=c: e.tensor_tensor(out=aT[:, c, :], in0=pu, in1=sa,
                                                                               op=ALU.mult)),
                          reads=[pub, sb], writes=[b_aT[c]])
            if dbg is not None and t == 0:
                S.add("sp", lambda e: e.dma_start(out=dbg["hT"], in_=hT[:]), reads=b_hT, dma=True)
                S.add("sp", lambda e: e.dma_start(out=dbg["aT"], in_=aT[:]), reads=b_aT, dma=True)
                S.add("sp", lambda e: e.dma_start(out=dbg["st"], in_=st[:]), reads=[b_rpre], dma=True)
            b_y = [[Buf() for _ in range(4)] for _ in range(NJ)]
            for n in range(4):
                accs = [C.psum.next() for _ in range(NJ)]
                for qi in range(NQ):
                    wa, wb = wd_ring.next()
                    S.add("pool", (lambda e, wa=wa, qi=qi, n=n: e.dma_start(
                        out=wa, in_=wd_v[:, qi * QC:(qi + 1) * QC, n * 512:(n + 1) * 512])),
                        writes=[wb], dma=True)
                    for j in range(NJ):
                        def mm3(e, wa=wa, qi=qi, j=j, ps=accs[j][0]):
                            ins = None
                            for ci in range(QC):
                                c = qi * QC + ci
                                ins = e.matmul(ps, lhsT=aT[:, c, j * 128:(j + 1) * 128], rhs=wa[:, ci, :],
                                               start=(c == 0), stop=(c == NFF - 1))
                            return ins
                        S.add("pe", mm3, reads=[wb] + b_aT[qi * QC:(qi + 1) * QC], writes=[accs[j][1]])
                for j in range(NJ):
                    ya, yb = ys_ring.next()
                    ps, psb = accs[j]
                    S.add("act", (lambda e, ya=ya, ps=ps: e.activation(out=ya, in_=ps, func=AF.Copy)),
                          reads=[psb], writes=[yb])
                    S.add("act", (lambda e, ps=ps, j=j, n=n: e.activation(
                        out=junk[:, 0:512], in_=ps, func=AF.Square,
                        accum_out=c_sspost[:, j * 4 + n:j * 4 + n + 1])),
                        reads=[psb], writes=[b_sspost])
                    r0 = tok0 + j * 128
                    S.add("sp", (lambda e, ya=ya, r0=r0, n=n: e.dma_start(
                        out=x_out[r0:r0 + 128, n * 512:(n + 1) * 512], in_=ya)),
                        reads=[yb], writes=[b_y[j][n]], dma=True)
            S.add("dve", lambda e: e.tensor_reduce(out=c_rpost, in_=c_sspost.rearrange("p (j n) -> p j n", n=4),
                                                   axis=AX.X, op=ALU.add),
                  reads=[b_sspost], writes=[b_rpost])
            S.add("act", lambda e: e.activation(out=c_rpost, in_=c_rpost, func=AF.Sqrt,
                                                scale=1.0 / D, bias=st[:, 0:1]),
                  reads=[b_eps], writes=[b_rpost])
            S.add("dve", lambda e: e.reciprocal(out=c_rpost, in_=c_rpost), writes=[b_rpost])
            for j in range(NJ):
                r0 = tok0 + j * 128
                ya, yb = xs_ring.next()
                xa, xb = xs_ring.next()
                S.add("sp", (lambda e, ya=ya, r0=r0: e.dma_start(out=ya, in_=x_out[r0:r0 + 128, :])),
                      reads=b_y[j], writes=[yb], dma=True)
                S.add("sp", (lambda e, xa=xa, r0=r0: e.dma_start(out=xa, in_=x_in[r0:r0 + 128, :])),
                      writes=[xb], dma=True)
                S.add("dve", (lambda e, ya=ya, j=j: e.scalar_tensor_tensor(
                    out=ya, in0=ya, scalar=c_rpost[:, j:j + 1], in1=gpost[:], op0=ALU.mult, op1=ALU.mult)),
                    reads=[b_rpost, b_gpost], writes=[yb])
                S.add("dve", (lambda e, ya=ya, xa=xa: e.scalar_tensor_tensor(
                    out=ya, in0=ya, scalar=0.5, in1=xa, op0=ALU.mult, op1=ALU.add)),
                    reads=[xb], writes=[yb])
                S.add("sp", (lambda e, ya=ya, r0=r0: e.dma_start(out=x_out[r0:r0 + 128, :], in_=ya)),
                      reads=[yb], writes=b_y[j], dma=True)
        S.emit()


def make_ctx(nc, stack, ident_dram):
    C = Ctx()
    C.nc = nc
    C.S = Sched(nc)
    banks = []
    for i in range(8):
        h = stack.enter_context(nc.psum_tensor("psb%d" % i, [128, 512], F32))
        banks.append(h[:])
    C.psum = Ring(banks, "ps")
    C.ident = stack.enter_context(_sbt(nc, "ident_sb", [128, 128], BF16))
    C.b_ident = Buf("ident")
    C.S.add("sp", lambda e: e.dma_start(out=C.ident[:], in_=ident_dram), writes=[C.b_ident], dma=True)
    C.flip = 0
    return C


def evac_copy(C, dst, src, reads, writes, eng=None):
    S = C.S
    if eng is None:
        eng = "act" if (C.flip % 2 == 0) else "dve"
        C.flip += 1
    if eng == "act":
        return S.add("act", (lambda e: e.activation(out=dst, in_=src, func=AF.Copy)), reads=reads, writes=writes)
    return S.add("dve", (lambda e: e.tensor_copy(out=dst, in_=src)), reads=reads, writes=writes)


class NormT:
    def __init__(self, C, stack, pfx, gain_dram):
        nc = C.nc
        self.C = C
        self.xs = stack.enter_context(_sbt(nc, pfx + "_xs", [128, 4, D], F32))
        self.xn = stack.enter_context(_sbt(nc, pfx + "_xn", [128, 2, D], BF16))
        self.g = stack.enter_context(_sbt(nc, pfx + "_g", [128, D], F32))
        self.junk = stack.enter_context(_sbt(nc, pfx + "_junk", [128, D], BF16))
        self.st = stack.enter_context(_sbt(nc, pfx + "_st", [128, 20], F32))
        self.xs_ring = Ring([self.xs[:, i, :] for i in range(4)])
        self.xn_ring = Ring([self.xn[:, i, :] for i in range(2)])
        self.b_g, self.b_eps = Buf(), Buf()
        self.b_stat = [[Buf(), Buf()], [Buf(), Buf()]]
        self.n = 0
        g = self.g
        C.S.add("sp", lambda e: e.dma_start(out=g[:], in_=gain_dram.partition_broadcast(128)),
                writes=[self.b_g], dma=True)
        st = self.st
        C.S.add("dve", lambda e: e.memset(st[:, 0:1], EPS), writes=[self.b_eps])

    def tile(self, x_in, tok0, NJ, hT, b_hT):
        C, S = self.C, self.C.S
        st, junk, gt = self.st, self.junk, self.g
        base = 4 + (self.n % 2) * 8
        b_ss, b_r = self.b_stat[self.n % 2]
        self.n += 1
        c_ss = st[:, base:base + NJ]
        c_r = st[:, base + 4:base + 4 + NJ]
        x_tiles = []
        for j in range(NJ):
            xa, xb = self.xs_ring.next()
            r0 = tok0 + j * 128
            S.add("sp", (lambda e, xa=xa, r0=r0: e.dma_start(out=xa, in_=x_in[r0:r0 + 128, :])),
                  writes=[xb], dma=True)
            S.add("act", (lambda e, xa=xa, j=j: e.activation(out=junk[:], in_=xa, func=AF.Square,
                                                               accum_out=c_ss[:, j:j + 1])),
                  reads=[xb], writes=[b_ss])
            x_tiles.append((xa, xb))
        S.add("act", lambda e: e.activation(out=c_r, in_=c_ss, func=AF.Sqrt, scale=1.0 / D, bias=st[:, 0:1]),
              reads=[b_ss, self.b_eps], writes=[b_r])
        S.add("dve", lambda e: e.reciprocal(out=c_r, in_=c_r), writes=[b_r])
        for j in range(NJ):
            xa, xb = x_tiles[j]
            na, nb = self.xn_ring.next()
            S.add("dve", (lambda e, xa=xa, na=na, j=j: e.scalar_tensor_tensor(
                out=na, in0=xa, scalar=c_r[:, j:j + 1], in1=gt[:], op0=ALU.mult, op1=ALU.mult)),
                reads=[xb, b_r, self.b_g], writes=[nb])
            for half in range(2):
                pa, pb = C.psum.next()
                pv = pa.bitcast(BF16)

                def tr(e, pv=pv, na=na, half=half):
                    ins = None
                    for i in range(8):
                        k = half * 8 + i
                        ins = e.transpose(out=pv[:, i * 128:(i + 1) * 128],
                                          in_=na[:, k * 128:(k + 1) * 128], identity=C.ident[:])
                    return ins
                S.add("pe", tr, reads=[nb, C.b_ident], writes=[pb])
                dst = hT[:, half * 8:(half + 1) * 8, j * 128:(j + 1) * 128]
                src = pv.rearrange("p (i t) -> p i t", i=8)
                evac_copy(C, dst, src, [pb], [b_hT[j]], eng=("act" if half == 0 else "dve"))


def linear_phase(C, stack_outer, pfx, ntok, x_src, W, ncols, epilogue, norm_gain=None, T=512, pre=None, post=None):
    nc, S = C.nc, C.S
    NJ = T // 128
    NT = ntok // T
    NG = (ncols + 511) // 512
    W_v = W.rearrange("(k p) c -> p k c", p=128)
    with ExitStackCompat() as stack:
        hTt = stack.enter_context(_sbt(nc, pfx + "_hT", [128, 2, KD, T], BF16))
        wt = stack.enter_context(_sbt(nc, pfx + "_w", [128, 3, KD, 512], BF16))
        w_ring = Ring([wt[:, i] for i in range(3)])
        hT_bufs = [[Buf() for _ in range(NJ)] for _ in range(2)]
        hT_kbufs = [[Buf() for _ in range(KD)] for _ in range(2)]
        nt = NormT(C, stack, pfx, norm_gain) if norm_gain is not None else None
        if pre is not None:
            pre(stack)
        for t in range(NT):
            tok0 = t * T
            hT = hTt[:, t % 2]
            b_hT = hT_bufs[t % 2]
            if nt is not None:
                nt.tile(x_src, tok0, NJ, hT, b_hT)
            else:
                for k in range(KD):
                    S.add("sp", (lambda e, hT=hT, k=k, tok0=tok0: e.dma_start_transpose(
                        out=hT[:, k, :], in_=x_src[tok0:tok0 + T, k * 128:(k + 1) * 128])),
                        writes=[hT_kbufs[t % 2][k]], dma=True)
            for n in range(NG):
                gs = min(512, ncols - n * 512)
                wa, wb = w_ring.next()
                S.add("pool", (lambda e, wa=wa, n=n, gs=gs: e.dma_start(
                    out=wa[:, :, 0:gs], in_=W_v[:, :, n * 512:n * 512 + gs])), writes=[wb], dma=True)
                for j in range(NJ):
                    pa, pb = C.psum.next()

                    def mm(e, wa=wa, hT=hT, j=j, gs=gs, ps=pa):
                        ins = None
                        for k in range(KD):
                            ins = e.matmul(ps[:, 0:gs], lhsT=hT[:, k, j * 128:(j + 1) * 128], rhs=wa[:, k, 0:gs],
                                           start=(k == 0), stop=(k == KD - 1))
                        return ins
                    S.add("pe", mm, reads=[wb] + ([b_hT[j]] if nt is not None else hT_kbufs[t % 2]), writes=[pb])
                    epilogue(t, j, n, gs, pa, pb, tok0 + j * 128)
            if post is not None:
                post(t, tok0, NJ)
        S.emit()


class PlainEpi:
    def __init__(self, C, stack, pfx, dst, silu_groups=(), col0=0):
        self.C = C
        self.dst = dst
        self.silu = set(silu_groups)
        self.col0 = col0
        t = stack.enter_context(_sbt(C.nc, pfx + "_stg", [128, 4, 512], BF16))
        self.ring = Ring([t[:, i, :] for i in range(4)])

    def __call__(self, t, j, n, gs, pa, pb, r0):
        C, S = self.C, self.C.S
        sa, sb = self.ring.next()
        if n in self.silu:
            S.add("act", (lambda e: e.activation(out=sa[:, 0:gs], in_=pa[:, 0:gs], func=AF.Silu)),
                  reads=[pb], writes=[sb])
        else:
            evac_copy(C, sa[:, 0:gs], pa[:, 0:gs], [pb], [sb])
        dst, c0 = self.dst, self.col0 + n * 512
        S.add("sp", (lambda e: e.dma_start(out=dst[r0:r0 + 128, c0:c0 + gs], in_=sa[:, 0:gs])),
              reads=[sb], dma=True)


def simple_linear(C, pfx, ntok, x_src, gain, W, ncols, dst, T=512):
    epi = {}

    def pre(stack):
        epi["e"] = PlainEpi(C, stack, pfx, dst)

    linear_phase(C, None, pfx, ntok, x_src, W, ncols, lambda *a: epi["e"](*a), norm_gain=gain, T=T, pre=pre)


def inproj_a_phase(C, x_src, gain, W, P, pos, invf):
    nc, S = C.nc, C.S
    R = {}
    NSUB = NTOK // 128
    PI = float(np.pi)

    def pre(stack):
        posi = stack.enter_context(_sbt(nc, "ia_posi", [128, NSUB], I32))
        posf = stack.enter_context(_sbt(nc, "ia_posf", [128, NSUB], F32))
        inv = stack.enter_context(_sbt(nc, "ia_inv", [128, 64], F32))
        ang = stack.enter_context(_sbt(nc, "ia_ang", [128, NSUB, 64], F32))
        tmp = stack.enter_context(_sbt(nc, "ia_tmp", [128, NSUB, 64], F32))
        cs = stack.enter_context(_sbt(nc, "ia_cs", [128, 4, NSUB, 64], F32))
        cst = stack.enter_context(_sbt(nc, "ia_c", [128, 2], F32))
        rt = stack.enter_context(_sbt(nc, "ia_rt", [128, 2, 4, 256], F32))
        R["cs"] = cs
        R["rt"] = Ring([rt[:, i] for i in range(2)])
        R["epi"] = PlainEpi(C, stack, "ia", P, silu_groups=(9, 10, 11))
        b_pos, b_inv, b_ang, b_tmp, b_c = Buf(), Buf(), Buf(), Buf(), Buf()
        R["b_cs"] = Buf()
        S.add("sp", lambda e: e.dma_start(out=posi[:], in_=pos), writes=[b_pos], dma=True)
        S.add("sp", lambda e: e.dma_start(out=inv[:], in_=invf.partition_broadcast(128)), writes=[b_inv], dma=True)
        S.add("dve", lambda e: e.tensor_copy(out=posf[:], in_=posi[:]), reads=[b_pos], writes=[b_pos])
        for n in range(NSUB):
            S.add("dve", (lambda e, n=n: e.tensor_scalar(out=ang[:, n, :], in0=inv[:], scalar1=posf[:, n:n + 1],
                                                         scalar2=None, op0=ALU.mult)),
                  reads=[b_pos, b_inv], writes=[b_ang])
        angf = ang[:].rearrange("p n i -> p (n i)")
        tmpf = tmp[:].rearrange("p n i -> p (n i)")
        MAGIC = 12582912.0
        C1 = 6.28125
        C2 = float(2 * np.pi - 6.28125)
        PIC = 3.141592
        S.add("dve", lambda e: e.memset(cst[:, 1:2], PI / 2), writes=[b_c])
        S.add("dve", lambda e: e.tensor_scalar(out=tmpf, in0=angf, scalar1=float(1.0 / (2 * np.pi)), scalar2=MAGIC,
                                               op0=ALU.mult, op1=ALU.add), reads=[b_ang], writes=[b_tmp])
        S.add("dve", lambda e: e.tensor_scalar(out=tmpf, in0=tmpf, scalar1=MAGIC, scalar2=None, op0=ALU.subtract),
              writes=[b_tmp])
        S.add("dve", lambda e: e.scalar_tensor_tensor(out=angf, in0=tmpf, scalar=-C1, in1=angf, op0=ALU.mult,
                                                      op1=ALU.add), reads=[b_tmp], writes=[b_ang])
        S.add("dve", lambda e: e.scalar_tensor_tensor(out=angf, in0=tmpf, scalar=-C2, in1=angf, op0=ALU.mult,
                                                      op1=ALU.add), reads=[b_tmp], writes=[b_ang])
        S.add("dve", lambda e: e.tensor_scalar(out=tmpf, in0=angf, scalar1=PIC, scalar2=None, op0=ALU.is_gt),
              reads=[b_ang], writes=[b_tmp])
        S.add("dve", lambda e: e.scalar_tensor_tensor(out=angf, in0=tmpf, scalar=float(-2 * np.pi), in1=angf,
                                                      op0=ALU.mult, op1=ALU.add), reads=[b_tmp], writes=[b_ang])
        S.add("dve", lambda e: e.tensor_scalar(out=angf, in0=angf, scalar1=-PIC, scalar2=PIC, op0=ALU.max,
                                               op1=ALU.min), writes=[b_ang])
        sinv = cs[:, 1].rearrange("p n i -> p (n i)")
        cosv = cs[:, 0].rearrange("p n i -> p (n i)")
        S.add("act", lambda e: e.activation(out=sinv, in_=angf, func=AF.Sin), reads=[b_ang], writes=[R["b_cs"]])
        S.add("act", lambda e: e.activation(out=tmpf, in_=angf, func=AF.Abs), reads=[b_ang], writes=[b_tmp])
        S.add("act", lambda e: e.activation(out=cosv, in_=tmpf, func=AF.Sin, bias=cst[:, 1:2], scale=-1.0),
              reads=[b_tmp, b_c], writes=[R["b_cs"]])
        for which in (0, 1):
            srcv = cs[:, which].rearrange("p n i -> p (n i)")
            dstv = cs[:, 2 + which].rearrange("p n i -> p (n i)")
            S.add("dve", (lambda e, srcv=srcv, dstv=dstv: e.tensor_scalar(
                out=dstv, in0=srcv, scalar1=float(HD ** -0.5), scalar2=None, op0=ALU.mult)),
                writes=[R["b_cs"]])

    def epilogue(t, j, n, gs, pa, pb, r0):
        if n >= 6:
            return R["epi"](t, j, n, gs, pa, pb, r0)
        nn = r0 // 128
        cs = R["cs"]
        koff = 0 if n < 3 else 2
        cosb = cs[:, koff + 0, nn:nn + 1, :].to_broadcast([128, 4, 64])
        sinb = cs[:, koff + 1, nn:nn + 1, :].to_broadcast([128, 4, 64])
        ps4 = pa.rearrange("p (h two i) -> p h two i", h=4, two=2)
        t1, t2 = ps4[:, :, 0, :], ps4[:, :, 1, :]
        rta, rtb = R["rt"].next()
        tv = [rta[:, i].rearrange("p (h i) -> p h i", h=4) for i in range(4)]
        sa, sb = R["epi"].ring.next()
        so = sa.rearrange("p (h two i) -> p h two i", h=4, two=2)
        bcs = R["b_cs"]
        ba, bb2, bc, bd = Buf(), Buf(), Buf(), Buf()
        S.add("dve", lambda e: e.tensor_tensor(out=tv[0], in0=t1, in1=cosb, op=ALU.mult), reads=[pb, bcs], writes=[rtb, ba])
        S.add("dve", lambda e: e.tensor_tensor(out=tv[1], in0=t2, in1=sinb, op=ALU.mult), reads=[pb, bcs], writes=[bb2])
        S.add("dve", lambda e: e.tensor_tensor(out=tv[2], in0=t2, in1=cosb, op=ALU.mult), reads=[pb, bcs], writes=[bc])
        S.add("dve", lambda e: e.tensor_tensor(out=tv[3], in0=t1, in1=sinb, op=ALU.mult), reads=[pb, bcs], writes=[bd])
        S.add("dve", lambda e: e.tensor_tensor(out=so[:, :, 0, :], in0=tv[0], in1=tv[1], op=ALU.subtract),
              reads=[ba, bb2], writes=[sb])
        S.add("dve", lambda e: e.tensor_tensor(out=so[:, :, 1, :], in0=tv[2], in1=tv[3], op=ALU.add),
              reads=[bc, bd, rtb], writes=[sb])
        c0 = n * 512
        S.add("sp", (lambda e: e.dma_start(out=P[r0:r0 + 128, c0:c0 + 512], in_=sa)), reads=[sb, rtb], dma=True)

    linear_phase(C, None, "ia", NTOK, x_src, W, 4 * MIXW + MEMW, epilogue, norm_gain=gain, pre=pre)


def load_T(C, dstT, src, tok0, ntok, col0, writes):
    S = C.S
    for r in range(ntok // 512):
        S.add("sp", (lambda e, r=r: e.dma_start_transpose(
            out=dstT[:, r * 512:(r + 1) * 512], in_=src[tok0 + r * 512:tok0 + (r + 1) * 512, col0:col0 + 128])),
            writes=[writes[r]], dma=True)


def retention_phase(C, P, ycat, dmaskT_d, qdec_d, kdec_d):
    nc, S = C.nc, C.S
    NC = SEQ // 128
    lg = [float(np.log1p(-2.0 ** (-5.0 - h))) for h in range(NH)]
    cdec = [float(np.exp(np.float32(l) * 128)) for l in lg]
    with ExitStackCompat() as stack:
        T_ = lambda name, shape, dt: stack.enter_context(_sbt(nc, "rt_" + name, shape, dt))
        dmask = T_("dmask", [128, NH, 128], F32)
        qdec = T_("qdec", [128, NH, 128], F32)
        kdec = T_("kdec", [128, NH], F32)
        qT = T_("qT", [128, 2, SEQ], BF16)
        kT = T_("kT", [128, 2, SEQ], BF16)
        qTd = T_("qTd", [128, 2, SEQ], BF16)
        ktok = T_("ktok", [128, 2, NC, 128], BF16)
        vtok = T_("vtok", [128, 2, NC, 128], BF16)
        gtok = T_("gtok", [128, 2, NC, 128], BF16)
        osb = T_("osb", [128, 2, NC, 128], F32)
        yo = T_("yo", [128, 2, NC, 128], BF16)
        state = T_("state", [128, 2, 128], F32)
        stbf = T_("stbf", [128, 2, 128], BF16)
        stm = T_("stm", [128, 3, 128], BF16)
        junk = T_("junk", [128, 128], BF16)
        st = T_("st", [128, 2, 5, NC], F32)
        cst = T_("cst", [128, 1], F32)
        b_const = Buf()
        S.add("sp", lambda e: e.dma_start(out=dmask[:], in_=dmaskT_d), writes=[b_const], dma=True)
        S.add("sp", lambda e: e.dma_start(out=qdec[:], in_=qdec_d), writes=[b_const], dma=True)
        S.add("sp", lambda e: e.dma_start(out=kdec[:], in_=kdec_d), writes=[b_const], dma=True)
        b_eps = Buf()
        S.add("dve", lambda e: e.memset(cst[:, 0:1], EPS), writes=[b_eps])
        b_ycst = Buf()
        stm_ring = Ring([stm[:, i, :] for i in range(3)])
        stbf_ring = Ring([stbf[:, i, :] for i in range(2)])
        slot_b = [{k: Buf() for k in ("qT0", "qT1", "qT2", "qT3", "kT0", "kT1", "kT2", "kT3", "qTd", "ktok",
                                      "vtok", "gtok", "osb", "yo", "state", "sum", "sq", "stat")} for _ in range(2)]
        it = 0
        for s in range(NSEQ):
            for h in range(NH):
                sl = it % 2
                it += 1
                B = slot_b[sl]
                tok0 = s * SEQ
                bq = [B["qT%d" % r] for r in range(4)]
                bk = [B["kT%d" % r] for r in range(4)]
                load_T(C, qT[:, sl, :], P, tok0, SEQ, h * HD, bq)
                load_T(C, kT[:, sl, :], P, tok0, SEQ, MIXW + h * HD, bk)
                for name, col, tl in (("ktok", MIXW + h * HD, ktok), ("vtok", 2 * MIXW + h * HD, vtok),
                                      ("gtok", 3 * MIXW + h * HD, gtok)):
                    S.add("sp", (lambda e, tl=tl, col=col, sl=sl, tok0=tok0: e.dma_start(
                        out=tl[:, sl], in_=P[tok0:tok0 + SEQ, col:col + HD].rearrange("(n p) d -> p n d", p=128))),
                        writes=[B[name]], dma=True)
                S.add("dve", (lambda e, sl=sl, h=h: e.tensor_tensor(
                    out=qTd[:, sl, :].rearrange("p (n c) -> p n c", c=128),
                    in0=qT[:, sl, :].rearrange("p (n c) -> p n c", c=128),
                    in1=qdec[:, h:h + 1, :].to_broadcast([128, NC, 128]), op=ALU.mult)),
                    reads=bq + [b_const], writes=[B["qTd"]])
                S.add("act", (lambda e, sl=sl, h=h: e.activation(
                    out=ktok[:, sl].rearrange("p n d -> p (n d)"), in_=ktok[:, sl].rearrange("p n d -> p (n d)"),
                    func=AF.Copy, scale=kdec[:, h:h + 1])),
                    reads=[b_const], writes=[B["ktok"]])
                sbf_prev = None
                for n in range(NC):
                    csl = slice(n * 128, (n + 1) * 128)
                    r = n // 4
                    p1, p1b = C.psum.next()
                    S.add("pe", (lambda e, p1=p1, sl=sl, csl=csl: e.matmul(
                        p1[:, 0:128], lhsT=kT[:, sl, csl], rhs=qT[:, sl, csl], start=True, stop=True)),
                        reads=[bk[r], bq[r]], writes=[p1b])
                    ma, mb = stm_ring.next()
                    S.add("dve", (lambda e, ma=ma, p1=p1, h=h: e.tensor_tensor(
                        out=ma, in0=p1[:, 0:128], in1=dmask[:, h, :], op=ALU.mult)),
                        reads=[p1b, b_const], writes=[mb])
                    p2, p2b = C.psum.next()

                    def mm_o(e, p2=p2, ma=ma, sl=sl, n=n, csl=csl, sbf=sbf_prev):
                        ins = e.matmul(p2[:, 0:128], lhsT=ma, rhs=vtok[:, sl, n, :], start=True, stop=(n == 0))
                        if n > 0:
                            ins = e.matmul(p2[:, 0:128], lhsT=qTd[:, sl, csl], rhs=sbf[0], start=False, stop=True)
                        return ins
                    rd = [mb, B["vtok"], B["qTd"]] + ([sbf_prev[1]] if n > 0 else [])
                    S.add("pe", mm_o, reads=rd, writes=[p2b])
                    S.add("act", (lambda e, p2=p2, sl=sl, n=n: e.activation(
                        out=osb[:, sl, n, :], in_=p2[:, 0:128], func=AF.Copy, accum_out=st[:, sl, 0, n:n + 1])),
                        reads=[p2b], writes=[B["osb"], B["sum"]])
                    S.add("act", (lambda e, p2=p2, sl=sl, n=n: e.activation(
                        out=junk[:], in_=p2[:, 0:128], func=AF.Square, accum_out=st[:, sl, 1, n:n + 1])),
                        reads=[p2b], writes=[B["sq"]])
                    if n < NC - 1:
                        p3, p3b = C.psum.next()
                        S.add("pe", (lambda e, p3=p3, sl=sl, n=n: e.matmul(
                            p3[:, 0:128], lhsT=ktok[:, sl, n, :], rhs=vtok[:, sl, n, :], start=True, stop=True)),
                            reads=[B["ktok"], B["vtok"]], writes=[p3b])
                        if n == 0:
                            S.add("dve", (lambda e, p3=p3, sl=sl: e.tensor_copy(out=state[:, sl, :], in_=p3[:, 0:128])),
                                  reads=[p3b], writes=[B["state"]])
                        else:
                            S.add("dve", (lambda e, p3=p3, sl=sl, h=h: e.scalar_tensor_tensor(
                                out=state[:, sl, :], in0=state[:, sl, :], scalar=cdec[h], in1=p3[:, 0:128],
                                op0=ALU.mult, op1=ALU.add)),
                                reads=[p3b], writes=[B["state"]])
                        sa, sb = stbf_ring.next()
                        S.add("act", (lambda e, sa=sa, sl=sl: e.activation(out=sa, in_=state[:, sl, :], func=AF.Copy)),
                              reads=[B["state"]], writes=[sb])
                        sbf_prev = (sa, sb)
                c_sum, c_sq = st[:, sl, 0, :], st[:, sl, 1, :]
                c_mean, c_var, c_rstd = st[:, sl, 2, :], st[:, sl, 3, :], st[:, sl, 4, :]
                bs = B["stat"]
                S.add("dve", (lambda e, c_mean=c_mean, c_sum=c_sum: e.tensor_scalar(
                    out=c_mean, in0=c_sum, scalar1=1.0 / HD, scalar2=None, op0=ALU.mult)),
                    reads=[B["sum"]], writes=[bs])
                S.add("dve", (lambda e, c_mean=c_mean, c_rstd=c_rstd: e.tensor_tensor(
                    out=c_rstd, in0=c_mean, in1=c_mean, op=ALU.mult)), writes=[bs])
                S.add("dve", (lambda e, c_var=c_var, c_sq=c_sq, c_rstd=c_rstd: e.scalar_tensor_tensor(
                    out=c_var, in0=c_sq, scalar=1.0 / HD, in1=c_rstd, op0=ALU.mult, op1=ALU.subtract)),
                    reads=[B["sq"]], writes=[bs])
                S.add("act", (lambda e, c_var=c_var: e.activation(out=c_var, in_=c_var, func=AF.Sqrt,
                                                                  bias=cst[:, 0:1], scale=1.0)),
                      reads=[b_eps], writes=[bs])
                S.add("dve", (lambda e, c_var=c_var, c_rstd=c_rstd: e.reciprocal(out=c_rstd, in_=c_var)), writes=[bs])
                o3 = osb[:, sl]
                S.add("dve", (lambda e, o3=o3, c_mean=c_mean: e.tensor_tensor(
                    out=o3, in0=o3, in1=c_mean.unsqueeze(2).to_broadcast([128, NC, 128]), op=ALU.subtract)),
                    reads=[bs], writes=[B["osb"]])
                S.add("dve", (lambda e, o3=o3, c_rstd=c_rstd: e.tensor_tensor(
                    out=o3, in0=o3, in1=c_rstd.unsqueeze(2).to_broadcast([128, NC, 128]), op=ALU.mult)),
                    reads=[bs], writes=[B["osb"]])
                S.add("dve", (lambda e, o3=o3, sl=sl: e.tensor_tensor(
                    out=yo[:, sl], in0=o3, in1=gtok[:, sl], op=ALU.mult)),
                    reads=[B["osb"], B["gtok"]], writes=[B["yo"]])
                S.add("sp", (lambda e, sl=sl, tok0=tok0, h=h: e.dma_start(
                    out=ycat[tok0:tok0 + SEQ, h * HD:(h + 1) * HD].rearrange("(n p) d -> p n d", p=128),
                    in_=yo[:, sl])), reads=[B["yo"]], writes=[b_ycst], dma=True)
        S.emit()


def memattn_phase(C, Q, qcol0, MKV, ycat):
    nc, S = C.nc, C.S
    NTL = SEQ // 128
    scale = float(HD ** -0.5)
    with ExitStackCompat() as stack:
        T_ = lambda name, shape, dt: stack.enter_context(_sbt(nc, "ma_" + name, shape, dt))
        mkT = T_("mkT", [128, NMH, NMEM], BF16)
        mv = T_("mv", [128, 2, MEMW], BF16)
        qmT = T_("qmT", [128, NMH, SEQ], BF16)
        p_sb = T_("p", [128, 3, NMEM], BF16)
        pT = T_("pT", [128, 3, 2, 128], BF16)
        yo = T_("yo", [128, 3, MEMW], BF16)
        st = T_("st", [128, 4, 4], F32)
        p_ring = Ring([p_sb[:, i, :] for i in range(3)])
        pT_ring = Ring([pT[:, i] for i in range(3)])
        yo_ring = Ring([yo[:, i, :] for i in range(3)])
        st_ring = Ring([st[:, i, :] for i in range(4)])
        b_mk, b_mv = Buf(), Buf()
        b_q = [[Buf() for _ in range(4)] for _ in range(NMH)]
        for s in range(NSEQ):
            tok0 = s * SEQ
            for hm in range(NMH):
                S.add("sp", (lambda e, hm=hm, s=s: e.dma_start_transpose(
                    out=mkT[:, hm, :], in_=MKV[s * NMEM:(s + 1) * NMEM, hm * HD:(hm + 1) * HD])),
                    writes=[b_mk], dma=True)
                load_T(C, qmT[:, hm, :], Q, tok0, SEQ, qcol0 + hm * HD, b_q[hm])
            S.add("sp", (lambda e, s=s: e.dma_start(
                out=mv[:], in_=MKV[s * NMEM:(s + 1) * NMEM, MEMW:2 * MEMW].rearrange("(c p) d -> p c d", p=128))),
                writes=[b_mv], dma=True)
            for j in range(NTL):
                ya, yb = yo_ring.next()
                for hm in range(NMH):
                    ps, psb = C.psum.next()
                    S.add("pe", (lambda e, ps=ps, hm=hm, j=j: e.matmul(
                        ps[:, 0:NMEM], lhsT=qmT[:, hm, j * 128:(j + 1) * 128], rhs=mkT[:, hm, :],
                        start=True, stop=True)), reads=[b_q[hm][j // 4], b_mk], writes=[psb])
                    sa, sb = st_ring.next()
                    S.add("dve", (lambda e, ps=ps, sa=sa: e.tensor_reduce(
                        out=sa[:, 0:1], in_=ps[:, 0:NMEM], axis=AX.X, op=ALU.max)), reads=[psb], writes=[sb])
                    S.add("dve", (lambda e, sa=sa: e.tensor_scalar(
                        out=sa[:, 1:2], in0=sa[:, 0:1], scalar1=-scale, scalar2=None, op0=ALU.mult)), writes=[sb])
                    pa, pb = p_ring.next()
                    S.add("act", (lambda e, ps=ps, sa=sa, pa=pa: e.activation(
                        out=pa, in_=ps[:, 0:NMEM], func=AF.Exp, bias=sa[:, 1:2], scale=scale,
                        accum_out=sa[:, 2:3])), reads=[psb], writes=[pb, sb])
                    S.add("dve", (lambda e, sa=sa: e.reciprocal(out=sa[:, 3:4], in_=sa[:, 2:3])), writes=[sb])
                    pt, ptb = C.psum.next()
                    ptv = pt.bitcast(BF16)

                    def tr(e, ptv=ptv, pa=pa):
                        ins = None
                        for c in range(2):
                            ins = e.transpose(out=ptv[:, c * 128:(c + 1) * 128], in_=pa[:, c * 128:(c + 1) * 128],
                                              identity=C.ident[:])
                        return ins
                    S.add("pe", tr, reads=[pb, C.b_ident], writes=[ptb])
                    ta, tb = pT_ring.next()
                    evac_copy(C, ta, ptv[:, 0:256].rearrange("p (c t) -> p c t", c=2), [ptb], [tb])
                    po, pob = C.psum.next()

                    def mm(e, po=po, ta=ta, hm=hm):
                        ins = None
                        for c in range(2):
                            ins = e.matmul(po[:, 0:128], lhsT=ta[:, c, :], rhs=mv[:, c, hm * HD:(hm + 1) * HD],
                                           start=(c == 0), stop=(c == 1))
                        return ins
                    S.add("pe", mm, reads=[tb, b_mv], writes=[pob])
                    S.add("act", (lambda e, po=po, ya=ya, hm=hm, sa=sa: e.activation(
                        out=ya[:, hm * HD:(hm + 1) * HD], in_=po[:, 0:128], func=AF.Copy, scale=sa[:, 3:4])),
                        reads=[pob, sb], writes=[yb])
                r0 = tok0 + j * 128
                S.add("sp", (lambda e, ya=ya, r0=r0: e.dma_start(out=ycat[r0:r0 + 128, MIXW:MIXW + MEMW], in_=ya)),
                      reads=[yb], dma=True)
        S.emit()


class PostNorm:
    def __init__(self, C, stack, pfx, gain_dram, x_in, x_out, wres):
        nc = C.nc
        self.C, self.x_in, self.x_out, self.wres = C, x_in, x_out, wres
        self.ys = stack.enter_context(_sbt(nc, pfx + "_ys", [128, 3, 512], F32))
        self.xy = stack.enter_context(_sbt(nc, pfx + "_xy", [128, 4, D], F32))
        self.g = stack.enter_context(_sbt(nc, pfx + "_gp", [128, D], F32))
        self.junk = stack.enter_context(_sbt(nc, pfx + "_pj", [128, 512], BF16))
        self.st = stack.enter_context(_sbt(nc, pfx + "_pst", [128, 2, 24], F32))
        self.cst = stack.enter_context(_sbt(nc, pfx + "_pc", [128, 1], F32))
        self.ys_ring = Ring([self.ys[:, i, :] for i in range(3)])
        self.xy_ring = Ring([self.xy[:, i, :] for i in range(4)])
        self.b_g, self.b_eps = Buf(), Buf()
        self.b_st = [[Buf(), Buf()], [Buf(), Buf()]]
        self.b_y = {}
        g, cst = self.g, self.cst
        C.S.add("sp", lambda e: e.dma_start(out=g[:], in_=gain_dram.partition_broadcast(128)),
                writes=[self.b_g], dma=True)
        C.S.add("dve", lambda e: e.memset(cst[:, 0:1], EPS), writes=[self.b_eps])

    def epilogue(self, t, j, n, gs, pa, pb, r0):
        C, S = self.C, self.C.S
        ya, yb = self.ys_ring.next()
        st = self.st
        b_ss = self.b_st[t % 2][0]
        x_out = self.x_out
        junk = self.junk
        S.add("act", (lambda e: e.activation(out=ya, in_=pa, func=AF.Copy)), reads=[pb], writes=[yb])
        S.add("act", (lambda e: e.activation(out=junk[:], in_=pa, func=AF.Square,
                                             accum_out=st[:, t % 2, j * 4 + n:j * 4 + n + 1])),
              reads=[pb], writes=[b_ss])
        by = Buf()
        self.b_y.setdefault((t, j), []).append(by)
        S.add("sp", (lambda e: e.dma_start(out=x_out[r0:r0 + 128, n * 512:(n + 1) * 512], in_=ya)),
              reads=[yb], writes=[by], dma=True)

    def finalize(self, t, tok0, NJ):
        C, S = self.C, self.C.S
        st, gt, cst = self.st, self.g, self.cst
        b_ss, b_r = self.b_st[t % 2]
        c_ss = st[:, t % 2, 0:16]
        c_r = st[:, t % 2, 16:16 + NJ]
        x_in, x_out, wres = self.x_in, self.x_out, self.wres
        S.add("dve", lambda e: e.tensor_reduce(out=c_r, in_=c_ss[:, 0:NJ * 4].rearrange("p (j n) -> p j n", n=4),
                                               axis=AX.X, op=ALU.add), reads=[b_ss], writes=[b_r])
        S.add("act", lambda e: e.activation(out=c_r, in_=c_r, func=AF.Sqrt, scale=1.0 / D, bias=cst[:, 0:1]),
              reads=[self.b_eps], writes=[b_r])
        S.add("dve", lambda e: e.reciprocal(out=c_r, in_=c_r), writes=[b_r])
        for j in range(NJ):
            r0 = tok0 + j * 128
            ya, yb = self.xy_ring.next()
            xa, xb = self.xy_ring.next()
            bys = self.b_y.pop((t, j))
            S.add("sp", (lambda e, ya=ya, r0=r0: e.dma_start(out=ya, in_=x_out[r0:r0 + 128, :])),
                  reads=bys, writes=[yb], dma=True)
            S.add("sp", (lambda e, xa=xa, r0=r0: e.dma_start(out=xa, in_=x_in[r0:r0 + 128, :])),
                  writes=[xb], dma=True)
            S.add("dve", (lambda e, ya=ya, j=j: e.scalar_tensor_tensor(
                out=ya, in0=ya, scalar=c_r[:, j:j + 1], in1=gt[:], op0=ALU.mult, op1=ALU.mult)),
                reads=[b_r, self.b_g], writes=[yb])
            S.add("dve", (lambda e, ya=ya, xa=xa: e.scalar_tensor_tensor(
                out=ya, in0=ya, scalar=wres, in1=xa, op0=ALU.mult, op1=ALU.add)),
                reads=[xb], writes=[yb])
            S.add("sp", (lambda e, ya=ya, r0=r0: e.dma_start(out=x_out[r0:r0 + 128, :], in_=ya)),
                  reads=[yb], writes=bys, dma=True)


def outproj_phase(C, ycat, W, gain, x_in, x_out):
    R = {}

    def pre(stack):
        R["pn"] = PostNorm(C, stack, "op", gain, x_in, x_out, 1.0)

    def epilogue(t, j, n, gs, pa, pb, r0):
        R["pn"].epilogue(t, j, n, gs, pa, pb, r0)

    def post(t, tok0, NJ):
        R["pn"].finalize(t, tok0, NJ)

    linear_phase(C, None, "op", NTOK, ycat, W, D, epilogue, norm_gain=None, pre=pre, post=post)


def stickbreak_phase(C, Q, KV, ycat, tril_f_d, tril_b_d):
    nc, S = C.nc, C.S
    NB = SEQ // 128
    scale = float(HD ** -0.5)
    with ExitStackCompat() as stack:
        T_ = lambda name, shape, dt: stack.enter_context(_sbt(nc, "sb_" + name, shape, dt))
        qT = T_("qT", [128, 4, SEQ], BF16)
        kT = T_("kT", [128, 4, SEQ], BF16)
        vtok = T_("vtok", [128, 4, NB, 128], BF16)
        yo = T_("yo", [128, 4, NB, 128], BF16)
        Et = T_("E", [128, 2, SEQ], F32)
        Lt = T_("L", [128, 2, SEQ], F32)
        Ct = T_("C", [128, 2, SEQ], F32)
        At = T_("A", [128, 2, SEQ], BF16)
        ATt = T_("AT", [128, 2, NB, 128], BF16)
        trf = T_("trf", [128, 128], F32)
        trb = T_("trb", [128, 128], BF16)
        ones = T_("ones", [128, 1], F32)
        b_const = Buf()
        S.add("sp", lambda e: e.dma_start(out=trf[:], in_=tril_f_d), writes=[b_const], dma=True)
        S.add("sp", lambda e: e.dma_start(out=trb[:], in_=tril_b_d), writes=[b_const], dma=True)
        S.add("dve", lambda e: e.memset(ones[:], 1.0), writes=[b_const])
        b_ycst = Buf()
        hb = [{k: Buf() for k in ("q0", "q1", "q2", "q3", "k0", "k1", "k2", "k3", "v", "yo")} for _ in range(4)]
        wb = [{k: Buf() for k in ("E", "L", "C", "A", "AT")} for _ in range(2)]

        def block_stages(st, sl, i):
            B, W_ = hb[sl], wb[st]
            bq = [B["q%d" % r] for r in range(4)]
            bk = [B["k%d" % r] for r in range(4)]
            L = (i + 1) * 128
            E, Lb, Cb, A, AT = Et[:, st, :], Lt[:, st, :], Ct[:, st, :], At[:, st, :], ATt[:, st]
            nbk = (L + 511) // 512
            zb = []

            def s0():
                for c in range(nbk):
                    zb.append(C.psum.next())
                for c in range(nbk):
                    w = min(512, L - c * 512)
                    S.add("pe", (lambda e, c=c, w=w: e.matmul(
                        zb[c][0][:, 0:w], lhsT=qT[:, sl, i * 128:(i + 1) * 128], rhs=kT[:, sl, c * 512:c * 512 + w],
                        start=True, stop=True)), reads=[bq[i // 4], bk[c]], writes=[zb[c][1]])

            def s1():
                for c in range(nbk):
                    w = min(512, L - c * 512)
                    S.add("act", (lambda e, c=c, w=w: e.activation(
                        out=E[:, c * 512:c * 512 + w], in_=zb[c][0][:, 0:w], func=AF.Exp, scale=-scale)),
                        reads=[zb[c][1]], writes=[W_["E"]])
                if SB_OLDMATH:
                    return
                for c in range(nbk):
                    w = min(512, L - c * 512)
                    S.add("dve", (lambda e, c=c, w=w: e.tensor_scalar(
                        out=Lb[:, c * 512:c * 512 + w], in0=zb[c][0][:, 0:w], scalar1=-scale, scalar2=None,
                        op0=ALU.mult)), reads=[zb[c][1]], writes=[W_["L"]])

            def s2():
                S.add("act", (lambda e: e.activation(out=E[:, 0:L], in_=E[:, 0:L], func=AF.Ln,
                                                     bias=ones[:, 0:1], scale=1.0)),
                      reads=[b_const], writes=[W_["E"]])

            def s3():
                if SB_OLDMATH:
                    for c in range(nbk):
                        w = min(512, L - c * 512)
                        S.add("dve", (lambda e, c=c, w=w: e.scalar_tensor_tensor(
                            out=Lb[:, c * 512:c * 512 + w], in0=zb[c][0][:, 0:w], scalar=-scale,
                            in1=E[:, c * 512:c * 512 + w], op0=ALU.mult, op1=ALU.subtract)),
                            reads=[zb[c][1], W_["E"]], writes=[W_["L"]])
                else:
                    S.add(SB_POOL_ENG, (lambda e: e.tensor_tensor(out=Lb[:, 0:L], in0=Lb[:, 0:L], in1=E[:, 0:L],
                                                                  op=ALU.subtract)),
                          reads=[W_["E"]], writes=[W_["L"]])
                S.add(SB_POOL_ENG, (lambda e: e.tensor_tensor(out=Lb[:, L - 128:L], in0=Lb[:, L - 128:L], in1=trf[:],
                                                         op=ALU.mult)),
                      reads=[b_const], writes=[W_["L"]])

            def s4():
                S.add("dve", (lambda e: e.tensor_tensor_scan(
                    out=Cb[:, 0:L], data0=ones[:, 0:1].to_broadcast([128, L]), data1=Lb[:, 0:L], initial=0.0,
                    op0=ALU.mult, op1=ALU.add)), reads=[W_["L"], b_const], writes=[W_["C"]])

            def s5():
                S.add(SB_POOL_ENG, (lambda e: e.tensor_tensor(out=Lb[:, 0:L], in0=Cb[:, 0:L], in1=E[:, 0:L], op=ALU.add)),
                      reads=[W_["C"], W_["E"]], writes=[W_["L"]])

            def s6():
                S.add("act", (lambda e: e.activation(out=A[:, 0:L], in_=Lb[:, 0:L], func=AF.Exp,
                                                     bias=Cb[:, L - 1:L], scale=-1.0)),
                      reads=[W_["L"], W_["C"]], writes=[W_["A"]])

            def s7():
                S.add(SB_POOL_ENG, (lambda e: e.tensor_tensor(out=A[:, L - 128:L], in0=A[:, L - 128:L], in1=trb[:],
                                                         op=ALU.mult)),
                      reads=[b_const], writes=[W_["A"]])

            def s8():
                ntb = (i + 1 + 7) // 8
                for tb_ in range(ntb):
                    nblk = min(8, i + 1 - tb_ * 8)
                    pt, ptb = C.psum.next()
                    ptv = pt.bitcast(BF16)

                    def tr(e, ptv=ptv, tb_=tb_, nblk=nblk):
                        ins = None
                        for bb in range(nblk):
                            blk = tb_ * 8 + bb
                            ins = e.transpose(out=ptv[:, bb * 128:(bb + 1) * 128],
                                              in_=A[:, blk * 128:(blk + 1) * 128], identity=C.ident[:])
                        return ins
                    S.add("pe", tr, reads=[W_["A"], C.b_ident], writes=[ptb])
                    evac_copy(C, AT[:, tb_ * 8:tb_ * 8 + nblk, :],
                              ptv[:, 0:nblk * 128].rearrange("p (b t) -> p b t", b=nblk), [ptb], [W_["AT"]],
                              eng="dve")

            def s9():
                po, pob = C.psum.next()

                def mm(e):
                    ins = None
                    for bb in range(i + 1):
                        ins = e.matmul(po[:, 0:128], lhsT=AT[:, bb, :], rhs=vtok[:, sl, bb, :],
                                       start=(bb == 0), stop=(bb == i))
                    return ins
                S.add("pe", mm, reads=[W_["AT"], B["v"]], writes=[pob])
                evac_copy(C, yo[:, sl, i, :], po[:, 0:128], [pob], [B["yo"]], eng="dve")

            return [s0, s1, s2, s3, s4, s5, s6, s7, s8, s9]

        pair = 0
        for s in range(NSEQ):
            tok0 = s * SEQ
            for hp in range(NH // 2):
                slots = [(pair % 2) * 2 + st for st in range(2)]
                pair += 1
                for st in range(2):
                    h = hp * 2 + st
                    sl = slots[st]
                    B = hb[sl]
                    load_T(C, qT[:, sl, :], Q, tok0, SEQ, h * HD, [B["q%d" % r] for r in range(4)])
                    load_T(C, kT[:, sl, :], KV, tok0, SEQ, h * HD, [B["k%d" % r] for r in range(4)])
                    S.add("sp", (lambda e, sl=sl, tok0=tok0, h=h: e.dma_start(
                        out=vtok[:, sl], in_=KV[tok0:tok0 + SEQ, MIXW + h * HD:MIXW + (h + 1) * HD].rearrange(
                            "(n p) d -> p n d", p=128))), writes=[B["v"]], dma=True)
                for i in range(NB):
                    a = block_stages(0, slots[0], i)
                    b = block_stages(1, slots[1], i)
                    order = [a[0], a[1], a[2], b[0], b[1], b[2], a[3], a[4], a[5], b[3], b[4], b[5],
                             a[6], a[7], b[6], b[7], a[8], a[9], b[8], b[9]]
                    if SB_SEQUENTIAL:
                        order = a + b
                    for f in order:
                        f()
                for st in range(2):
                    h = hp * 2 + st
                    sl = slots[st]
                    S.add("sp", (lambda e, sl=sl, tok0=tok0, h=h: e.dma_start(
                        out=ycat[tok0:tok0 + SEQ, h * HD:(h + 1) * HD].rearrange("(n p) d -> p n d", p=128),
                        in_=yo[:, sl])), reads=[hb[sl]["yo"]], writes=[b_ycst], dma=True)
        S.emit()


W_SPECS = [
    ("ffn1_norm_pre", [2, D]), ("ffn1_norm_post", [2, D]),
    ("ffn1_w_gate", [2, D, DFF]), ("ffn1_w_up", [2, D, DFF]), ("ffn1_w_down", [2, DFF, D]),
    ("mix_norm_pre", [2, D]), ("mix_norm_post", [2, D]), ("mem_norm", [2, D]),
    ("w_mem_kv", [2, D, 2 * MEMW]), ("w_o", [2, D, D]), ("ret_w_in", [1, D, 4 * MIXW + MEMW]),
    ("kv_norm", [D]), ("w_kv_shared", [D, 2 * MIXW]), ("sb_w_in", [1, D, MIXW + MEMW]),
    ("ffn2_norm_pre", [2, D]), ("ffn2_norm_post", [2, D]),
    ("ffn2_w_gate", [2, D, DFF]), ("ffn2_w_up", [2, D, DFF]), ("ffn2_w_down", [2, DFF, D]),
]


def make_consts():
    h = np.arange(NH, dtype=np.float32)
    lg = np.log1p(-np.exp2(-5.0 - h)).astype(np.float32)
    idx = np.arange(128, dtype=np.float32)
    diff = idx[None, :] - idx[:, None]
    dm = np.where(diff >= 0, np.exp(lg[:, None, None] * np.maximum(diff, 0.0)[None]), 0.0).astype(np.float32)
    dmaskT = np.ascontiguousarray(dm.transpose(1, 0, 2))
    qd = np.exp(lg[:, None] * (idx + 1.0)[None, :]).astype(np.float32)
    qdec = np.ascontiguousarray(np.broadcast_to(qd[None], (128, NH, 128))).astype(np.float32)
    kdec = np.ascontiguousarray(np.exp(lg[None, :] * (127.0 - idx)[:, None]).astype(np.float32))
    invf = (10000.0 ** (-np.arange(0, HD, 2, dtype=np.float32) / HD)).astype(np.float32)
    tril = (idx[None, :] < idx[:, None]).astype(np.float32)
    return {
        "c_ident": np.eye(128, dtype=np.float32).astype(ml_dtypes.bfloat16),
        "c_invf": invf, "c_dmaskT": dmaskT, "c_qdec": qdec, "c_kdec": kdec,
        "c_tril_f": tril, "c_tril_b": tril.astype(ml_dtypes.bfloat16),
    }


def build_program(phases=None):
    nc = bass.Bass("TRN2", target_bir_lowering=False)
    IN = lambda name, shape, dt=F32: nc.dram_tensor(name, shape, dt, kind="ExternalInput").ap()
    SCR = lambda name, shape, dt: nc.dram_tensor(name, shape, dt, kind="Internal").ap()
    x = IN("x", [NTOK, D])
    mem = IN("mem", [NSEQ * NMEM, D])
    pos = IN("positions", [128, NTOK // 128], I32)
    Wt = {name: IN(name, shape) for name, shape in W_SPECS}
    cst = {
        "c_ident": IN("c_ident", [128, 128], BF16), "c_invf": IN("c_invf", [64]),
        "c_dmaskT": IN("c_dmaskT", [128, NH, 128]), "c_qdec": IN("c_qdec", [128, NH, 128]),
        "c_kdec": IN("c_kdec", [128, NH]), "c_tril_f": IN("c_tril_f", [128, 128]),
        "c_tril_b": IN("c_tril_b", [128, 128], BF16),
    }
    out = nc.dram_tensor("out", [NTOK, D], F32, kind="ExternalOutput").ap()
    xa = SCR("s_xa", [NTOK, D], F32)
    xb = SCR("s_xb", [NTOK, D], F32)
    P = SCR("s_P", [NTOK, 4 * MIXW + MEMW], BF16)
    KVs = SCR("s_KV", [NTOK, 2 * MIXW], BF16)
    ycat = SCR("s_ycat", [NTOK, D], BF16)
    MKV = SCR("s_MKV", [NSEQ * NMEM, 2 * MEMW], BF16)
    with ExitStackCompat() as stack:
        C = make_ctx(nc, stack, cst["c_ident"])

        def ffn(l, which, src, dst):
            p = "ffn%d_" % which
            ffn_phase(C, src, dst, Wt[p + "w_gate"][l], Wt[p + "w_up"][l], Wt[p + "w_down"][l],
                      Wt[p + "norm_pre"][l], Wt[p + "norm_post"][l], NTOK)

        ffn(0, 1, x, xa)
        simple_linear(C, "mk0", NSEQ * NMEM, mem, Wt["mem_norm"][0], Wt["w_mem_kv"][0], 2 * MEMW, MKV)
        inproj_a_phase(C, xa, Wt["mix_norm_pre"][0], Wt["ret_w_in"][0], P, pos, cst["c_invf"])
        retention_phase(C, P, ycat, cst["c_dmaskT"], cst["c_qdec"], cst["c_kdec"])
        memattn_phase(C, P, 4 * MIXW, MKV, ycat)
        outproj_phase(C, ycat, Wt["w_o"][0], Wt["mix_norm_post"][0], xa, xb)
        ffn(0, 2, xb, xa)
        simple_linear(C, "kvs", NTOK, xa, Wt["kv_norm"], Wt["w_kv_shared"], 2 * MIXW, KVs)
        ffn(1, 1, xa, xb)
        simple_linear(C, "mk1", NSEQ * NMEM, mem, Wt["mem_norm"][1], Wt["w_mem_kv"][1], 2 * MEMW, MKV)
        simple_linear(C, "ib", NTOK, xb, Wt["mix_norm_pre"][1], Wt["sb_w_in"][0], MIXW + MEMW, P)
        stickbreak_phase(C, P, KVs, ycat, cst["c_tril_f"], cst["c_tril_b"])
        memattn_phase(C, P, MIXW, MKV, ycat)
        outproj_phase(C, ycat, Wt["w_o"][1], Wt["mix_norm_post"][1], xb, xa)
        ffn(1, 2, xa, out)
        C.S.close()
    return nc


_CACHE = {}


def kernel(**inputs):
    if "nc" not in _CACHE:
        _CACHE["nc"] = build_program()
        _CACHE["consts"] = make_consts()
    nc = _CACHE["nc"]
    consts = _CACHE["consts"]
    x = np.ascontiguousarray(inputs["x"], dtype=np.float32)
    mem = np.ascontiguousarray(inputs["mem"], dtype=np.float32)
    pos = np.ascontiguousarray(inputs["positions"], dtype=np.int32)
    shared = {name: np.ascontiguousarray(inputs[name], dtype=np.float32) for name, _ in W_SPECS}
    shared.update(consts)
    in_maps = []
    for c in range(N_CORES):
        m = dict(shared)
        m["x"] = x[c * NSEQ:(c + 1) * NSEQ].reshape(NTOK, D)
        m["mem"] = mem[c * NSEQ:(c + 1) * NSEQ].reshape(NSEQ * NMEM, D)
        m["positions"] = np.ascontiguousarray(pos[c * NSEQ:(c + 1) * NSEQ].reshape(NTOK // 128, 128).T)
        in_maps.append(m)
    res = run_bass_kernel_spmd(nc, in_maps, core_ids=list(range(N_CORES)))
    outs = [np.asarray(r["out"]).reshape(NSEQ, SEQ, D) for r in res.results]
    return np.concatenate(outs, axis=0).astype(np.float32)
```

```python
import numpy as np
import ml_dtypes
from contextlib import ExitStack as ExitStackCompat
import concourse.bass as bass
import concourse.mybir as mybir
from concourse.bass_utils import run_bass_kernel_spmd

F32 = mybir.dt.float32
BF16 = mybir.dt.bfloat16
I32 = mybir.dt.int32
AF = mybir.ActivationFunctionType
ALU = mybir.AluOpType
AX = mybir.AxisListType

D = 2048
DFF = 5632
NFF = DFF // 128
KD = D // 128
SEQ = 2048
NSEQ = 2
NTOK = SEQ * NSEQ
NMEM = 256
HD = 128
NH = 12
NMH = 4
MIXW = NH * HD
MEMW = NMH * HD
EPS = 1e-6
N_CORES = 8

SB_SEQUENTIAL = False
SB_OLDMATH = True
SB_POOL_ENG = "pool"
N_DSEM = 12
CSEM_CAP = 30000


_UID = [0]


def _sbt(nc, name, shape, dt):
    _UID[0] += 1
    return nc.sbuf_tensor("%s_%d" % (name, _UID[0]), shape, dt)


class Buf:
    __slots__ = ("name", "w", "r")

    def __init__(self, name=""):
        self.name = name
        self.w = None
        self.r = {}


class Op:
    __slots__ = ("q", "fn", "deps", "need", "no", "dma")


class Sched:
    QUEUES = ("pe", "act", "dve", "pool", "sp")

    def __init__(self, nc):
        self.nc = nc
        self.ops = {q: [] for q in self.QUEUES}
        self.count = {}
        self.sems = {}
        self.waited = {q: {} for q in self.QUEUES}
        self._semctx = []

    def _sem(self, fam, idx):
        key = (fam, idx)
        if key not in self.sems:
            cm = self.nc.semaphore("s_%s_%s_%d" % (fam[0], "d" if fam[1] else "c", idx))
            h = cm.__enter__()
            self._semctx.append(cm)
            self.sems[key] = h
        return self.sems[key]

    def close(self):
        for cm in reversed(self._semctx):
            cm.__exit__(None, None, None)
        self._semctx = []

    def add(self, q, fn, reads=(), writes=(), dma=False):
        op = Op()
        op.q = q
        op.fn = fn
        op.dma = dma
        op.need = dma
        op.no = None
        deps = []
        for b in reads:
            if b.w is not None:
                deps.append(b.w)
        for b in writes:
            if b.w is not None:
                deps.append(b.w)
            deps.extend(b.r.values())
        fam = (q, dma)
        for b in reads:
            b.r[fam] = op
        for b in writes:
            b.w = op
            b.r = {}
        out = []
        seen = set()
        for d in deps:
            if d is op or id(d) in seen:
                continue
            seen.add(id(d))
            if d.q == "pe" and q == "pe" and not d.dma and not dma:
                continue
            d.need = True
            out.append(d)
        op.deps = out
        self.ops[q].append(op)
        return op

    def _semval(self, d):
        fam = (d.q, d.dma)
        if d.dma:
            idx = (d.no - 1) % N_DSEM
            return self._sem(fam, idx), 16 * ((d.no - 1) // N_DSEM + 1)
        idx = (d.no - 1) // CSEM_CAP
        v = (d.no - 1) % CSEM_CAP + 1
        return self._sem(fam, idx), v

    def emit(self, barrier=True):
        nc = self.nc
        if barrier:
            for q in self.QUEUES:
                for op in reversed(self.ops[q]):
                    if not op.dma:
                        op.need = True
                        break
        for q in self.QUEUES:
            for op in self.ops[q]:
                if op.need and op.no is None:
                    fam = (q, op.dma)
                    self.count[fam] = self.count.get(fam, 0) + 1
                    op.no = self.count[fam]
        finals = []
        if barrier:
            for fam, n in self.count.items():
                if fam[1]:
                    for idx in range(min(n, N_DSEM)):
                        last_i = n - ((n - 1 - idx) % N_DSEM)
                        finals.append((self._sem(fam, idx), 16 * ((last_i - 1) // N_DSEM + 1)))
                else:
                    idx = (n - 1) // CSEM_CAP
                    v = (n - 1) % CSEM_CAP + 1
                    finals.append((self._sem(fam, idx), v))
        for q in self.QUEUES:
            for op in self.ops[q]:
                for d in op.deps:
                    self._semval(d)
                if op.need:
                    self._semval(op)

        def run_queue(q, e):
            waited = self.waited[q]
            for op in self.ops[q]:
                for d in op.deps:
                    sem, val = self._semval(d)
                    if waited.get(sem, 0) < val:
                        e.wait_ge(sem, val)
                        waited[sem] = val
                if op.dma and op.no > N_DSEM:
                    sem, val = self._semval(op)
                    if waited.get(sem, 0) < val - 16:
                        e.wait_ge(sem, val - 16)
                        waited[sem] = val - 16
                ins = op.fn(e)
                if op.need:
                    sem, _ = self._semval(op)
                    ins.then_inc(sem, 16 if op.dma else 1)
            for sem, val in finals:
                if waited.get(sem, 0) < val:
                    e.wait_ge(sem, val)
                    waited[sem] = val

        with nc.Block() as block:
            @block.tensor
            def _(e):
                run_queue("pe", e)

            @block.scalar
            def _(e):
                run_queue("act", e)

            @block.vector
            def _(e):
                run_queue("dve", e)

            @block.gpsimd
            def _(e):
                run_queue("pool", e)

            @block.sync
            def _(e):
                run_queue("sp", e)
        self.ops = {q: [] for q in self.QUEUES}


class Ring:
    def __init__(self, aps, name=""):
        self.aps = list(aps)
        self.bufs = [Buf("%s%d" % (name, i)) for i in range(len(self.aps))]
        self.i = 0

    def next(self):
        k = self.i % len(self.aps)
        self.i += 1
        return self.aps[k], self.bufs[k]


class Ctx:
    pass


def ffn_phase(C, x_in, x_out, wg, wu, wd, g_pre, g_post, ntok, dbg=None):
    nc, S = C.nc, C.S
    T = 512
    NJ = T // 128
    NT = ntok // T
    WC = 256
    NWC = DFF // WC
    QC = 11
    NQ = NFF // QC
    wg_v = wg.rearrange("(k p) c -> p k c", p=128)
    wu_v = wu.rearrange("(k p) c -> p k c", p=128)
    wd_v = wd.rearrange("(c p) n -> p c n", p=128)
    with (
        _sbt(nc, "f_hT", [128, KD, T], BF16) as hT,
        _sbt(nc, "f_aT", [128, NFF, T], BF16) as aT,
        _sbt(nc, "f_wg", [128, 2, KD, WC], BF16) as wgt,
        _sbt(nc, "f_wu", [128, 2, KD, WC], BF16) as wut,
        _sbt(nc, "f_wd", [128, 2, QC, 512], BF16) as wdt,
        _sbt(nc, "f_xs", [128, 4, D], F32) as xs,
        _sbt(nc, "f_xn", [128, 2, D], BF16) as xn,
        _sbt(nc, "f_ys", [128, 3, 512], F32) as ys,
        _sbt(nc, "f_sg", [128, 2, 512], F32) as sg,
        _sbt(nc, "f_gpre", [128, D], F32) as gpre,
        _sbt(nc, "f_gpost", [128, D], F32) as gpost,
        _sbt(nc, "f_junk", [128, D], BF16) as junk,
        _sbt(nc, "f_st", [128, 64], F32) as st,
    ):
        xs_ring = Ring([xs[:, i, :] for i in range(4)], "xs")
        xn_ring = Ring([xn[:, i, :] for i in range(2)], "xn")
        ys_ring = Ring([ys[:, i, :] for i in range(3)], "ys")
        sg_ring = Ring([sg[:, i, :] for i in range(2)], "sg")
        wg_ring = Ring([wgt[:, i] for i in range(2)], "wg")
        wu_ring = Ring([wut[:, i] for i in range(2)], "wu")
        wd_ring = Ring([wdt[:, i] for i in range(2)], "wd")
        b_gpre, b_gpost, b_eps = Buf(), Buf(), Buf()
        b_junk = Buf()
        S.add("sp", lambda e: e.dma_start(out=gpre[:], in_=g_pre.partition_broadcast(128)),
              writes=[b_gpre], dma=True)
        S.add("sp", lambda e: e.dma_start(out=gpost[:], in_=g_post.partition_broadcast(128)),
              writes=[b_gpost], dma=True)
        S.add("dve", lambda e: e.memset(st[:, 0:1], EPS), writes=[b_eps])

        b_hT = [Buf() for _ in range(NJ)]
        b_aT = [Buf() for _ in range(NFF)]
        b_stat = [[Buf() for _ in range(4)] for _ in range(2)]
        def tile_stages(t):
            base = 4 + (t % 2) * 28
            c_sspre = st[:, base:base + 4]
            c_rpre = st[:, base + 4:base + 8]
            c_sspost = st[:, base + 8:base + 24]
            c_rpost = st[:, base + 24:base + 28]
            b_sspre, b_rpre, b_sspost, b_rpost = b_stat[t % 2]
            tok0 = t * T
            b_y = [[Buf() for _ in range(4)] for _ in range(NJ)]

            def s1():
                x_tiles = []
                for j in range(NJ):
                    xa, xb = xs_ring.next()
                    r0 = tok0 + j * 128
                    S.add("sp", (lambda e, xa=xa, r0=r0: e.dma_start(out=xa, in_=x_in[r0:r0 + 128, :])),
                          writes=[xb], dma=True)
                    S.add("act", (lambda e, xa=xa, j=j: e.activation(out=junk[:], in_=xa, func=AF.Square,
                                                                       accum_out=c_sspre[:, j:j + 1])),
                          reads=[xb], writes=[b_sspre])
                    x_tiles.append((xa, xb))
                S.add("act", lambda e: e.activation(out=c_rpre, in_=c_sspre, func=AF.Sqrt,
                                                    scale=1.0 / D, bias=st[:, 0:1]),
                      reads=[b_sspre, b_eps], writes=[b_rpre])
                S.add("dve", lambda e: e.reciprocal(out=c_rpre, in_=c_rpre), writes=[b_rpre])
                for j in range(NJ):
                    xa, xb = x_tiles[j]
                    na, nb = xn_ring.next()
                    S.add("dve", (lambda e, xa=xa, na=na, j=j: e.scalar_tensor_tensor(
                        out=na, in0=xa, scalar=c_rpre[:, j:j + 1], in1=gpre[:], op0=ALU.mult, op1=ALU.mult)),
                        reads=[xb, b_rpre, b_gpre], writes=[nb])
                    for half in range(2):
                        pa, pb = C.psum.next()
                        pv = pa.bitcast(BF16)

                        def tr(e, pv=pv, na=na, half=half):
                            ins = None
                            for i in range(8):
                                k = half * 8 + i
                                ins = e.transpose(out=pv[:, i * 128:(i + 1) * 128],
                                                  in_=na[:, k * 128:(k + 1) * 128], identity=C.ident[:])
                            return ins
                        S.add("pe", tr, reads=[nb, C.b_ident], writes=[pb])
                        eng = "act" if half == 0 else "dve"
                        dst = hT[:, half * 8:(half + 1) * 8, j * 128:(j + 1) * 128]
                        src = pv.rearrange("p (i t) -> p i t", i=8)
                        if eng == "act":
                            S.add("act", (lambda e, dst=dst, src=src: e.activation(out=dst, in_=src, func=AF.Copy)),
                                  reads=[pb], writes=[b_hT[j]])
                        else:
                            S.add("dve", (lambda e, dst=dst, src=src: e.tensor_copy(out=dst, in_=src)),
                                  reads=[pb], writes=[b_hT[j]])
            def s2():
                for wc in range(NWC):
                    ga, gb = wg_ring.next()
                    ua, ub = wu_ring.next()
                    c0 = wc * WC
                    S.add("pool", (lambda e, ga=ga, c0=c0: e.dma_start(out=ga, in_=wg_v[:, :, c0:c0 + WC])),
                          writes=[gb], dma=True)
                    S.add("pool", (lambda e, ua=ua, c0=c0: e.dma_start(out=ua, in_=wu_v[:, :, c0:c0 + WC])),
                          writes=[ub], dma=True)
                    for cl in range(WC // 128):
                        c = wc * (WC // 128) + cl
                        pg, pgb = C.psum.next()
                        pu, pub = C.psum.next()

                        def mm(e, w=ga, cl=cl, ps=pg):
                            ins = None
                            for k in range(KD):
                                ins = e.matmul(ps, lhsT=w[:, k, cl * 128:(cl + 1) * 128], rhs=hT[:, k, :],
                                               start=(k == 0), stop=(k == KD - 1))
                            return ins
                        S.add("pe", mm, reads=[gb] + b_hT, writes=[pgb])

                        def mm2(e, w=ua, cl=cl, ps=pu):
                            ins = None
                            for k in range(KD):
                                ins = e.matmul(ps, lhsT=w[:, k, cl * 128:(cl + 1) * 128], rhs=hT[:, k, :],
                                               start=(k == 0), stop=(k == KD - 1))
                            return ins
                        S.add("pe", mm2, reads=[ub] + b_hT, writes=[pub])
                        sa, sb = sg_ring.next()
                        S.add("act", (lambda e, sa=sa, pg=pg: e.activation(out=sa, in_=pg, func=AF.Silu)),
                              reads=[pgb], writes=[sb])
                        S.add("dve", (lambda e, sa=sa, pu=pu, c=c: e.tensor_tensor(out=aT[:, c, :], in0=pu, in1=sa,
                                                                                   op=ALU.mult)),
                              reads=[pub, sb], writes=[b_aT[c]])
                if dbg is not None and t == 0:
                    S.add("sp", lambda e: e.dma_start(out=dbg["hT"], in_=hT[:]), reads=b_hT, dma=True)
                    S.add("sp", lambda e: e.dma_start(out=dbg["aT"], in_=aT[:]), reads=b_aT, dma=True)
                    S.add("sp", lambda e: e.dma_start(out=dbg["st"], in_=st[:]), reads=[b_rpre], dma=True)
            def s3():
                for n in range(4):
                    accs = [C.psum.next() for _ in range(NJ)]
                    for qi in range(NQ):
                        wa, wb = wd_ring.next()
                        S.add("pool", (lambda e, wa=wa, qi=qi, n=n: e.dma_start(
                            out=wa, in_=wd_v[:, qi * QC:(qi + 1) * QC, n * 512:(n + 1) * 512])),
                            writes=[wb], dma=True)
                        for j in range(NJ):
                            def mm3(e, wa=wa, qi=qi, j=j, ps=accs[j][0]):
                                ins = None
                                for ci in range(QC):
                                    c = qi * QC + ci
                                    ins = e.matmul(ps, lhsT=aT[:, c, j * 128:(j + 1) * 128], rhs=wa[:, ci, :],
                                                   start=(c == 0), stop=(c == NFF - 1))
                                return ins
                            S.add("pe", mm3, reads=[wb] + b_aT[qi * QC:(qi + 1) * QC], writes=[accs[j][1]])
                    for j in range(NJ):
                        ya, yb = ys_ring.next()
                        ps, psb = accs[j]
                        S.add("act", (lambda e, ya=ya, ps=ps: e.activation(out=ya, in_=ps, func=AF.Copy)),
                              reads=[psb], writes=[yb])
                        S.add("act", (lambda e, ps=ps, j=j, n=n: e.activation(
                            out=junk[:, 0:512], in_=ps, func=AF.Square,
                            accum_out=c_sspost[:, j * 4 + n:j * 4 + n + 1])),
                            reads=[psb], writes=[b_sspost])
                        r0 = tok0 + j * 128
                        S.add("sp", (lambda e, ya=ya, r0=r0, n=n: e.dma_start(
                            out=x_out[r0:r0 + 128, n * 512:(n + 1) * 512], in_=ya)),
                            reads=[yb], writes=[b_y[j][n]], dma=True)
            def s4():
                S.add("dve", lambda e: e.tensor_reduce(out=c_rpost, in_=c_sspost.rearrange("p (j n) -> p j n", n=4),
                                                       axis=AX.X, op=ALU.add),
                      reads=[b_sspost], writes=[b_rpost])
                S.add("act", lambda e: e.activation(out=c_rpost, in_=c_rpost, func=AF.Sqrt,
                                                    scale=1.0 / D, bias=st[:, 0:1]),
                      reads=[b_eps], writes=[b_rpost])
                S.add("dve", lambda e: e.reciprocal(out=c_rpost, in_=c_rpost), writes=[b_rpost])
                for j in range(NJ):
                    r0 = tok0 + j * 128
                    ya, yb = xs_ring.next()
                    xa, xb = xs_ring.next()
                    S.add("sp", (lambda e, ya=ya, r0=r0: e.dma_start(out=ya, in_=x_out[r0:r0 + 128, :])),
                          reads=b_y[j], writes=[yb], dma=True)
                    S.add("sp", (lambda e, xa=xa, r0=r0: e.dma_start(out=xa, in_=x_in[r0:r0 + 128, :])),
                          writes=[xb], dma=True)
                    S.add("dve", (lambda e, ya=ya, j=j: e.scalar_tensor_tensor(
                        out=ya, in0=ya, scalar=c_rpost[:, j:j + 1], in1=gpost[:], op0=ALU.mult, op1=ALU.mult)),
                        reads=[b_rpost, b_gpost], writes=[yb])
                    S.add("dve", (lambda e, ya=ya, xa=xa: e.scalar_tensor_tensor(
                        out=ya, in0=ya, scalar=0.5, in1=xa, op0=ALU.mult, op1=ALU.add)),
                        reads=[xb], writes=[yb])
                    S.add("sp", (lambda e, ya=ya, r0=r0: e.dma_start(out=x_out[r0:r0 + 128, :], in_=ya)),
                          reads=[yb], writes=b_y[j], dma=True)

            return s1, s2, s3, s4

        stg = [tile_stages(t) for t in range(NT)]
        stg[0][0]()
        for t in range(NT):
            stg[t][1]()
            if t + 1 < NT:
                stg[t + 1][0]()
            stg[t][2]()
            stg[t][3]()
        S.emit()


def make_ctx(nc, stack, ident_dram):
    C = Ctx()
    C.nc = nc
    C.S = Sched(nc)
    banks = []
    for i in range(8):
        h = stack.enter_context(nc.psum_tensor("psb%d" % i, [128, 512], F32))
        banks.append(h[:])
    C.psum = Ring(banks, "ps")
    C.ident = stack.enter_context(_sbt(nc, "ident_sb", [128, 128], BF16))
    C.b_ident = Buf("ident")
    C.S.add("sp", lambda e: e.dma_start(out=C.ident[:], in_=ident_dram), writes=[C.b_ident], dma=True)
    C.flip = 0
    return C


def evac_copy(C, dst, src, reads, writes, eng=None):
    S = C.S
    if eng is None:
        eng = "act" if (C.flip % 2 == 0) else "dve"
        C.flip += 1
    if eng == "act":
        return S.add("act", (lambda e: e.activation(out=dst, in_=src, func=AF.Copy)), reads=reads, writes=writes)
    return S.add("dve", (lambda e: e.tensor_copy(out=dst, in_=src)), reads=reads, writes=writes)


class NormT:
    def __init__(self, C, stack, pfx, gain_dram):
        nc = C.nc
        self.C = C
        self.xs = stack.enter_context(_sbt(nc, pfx + "_xs", [128, 4, D], F32))
        self.xn = stack.enter_context(_sbt(nc, pfx + "_xn", [128, 2, D], BF16))
        self.g = stack.enter_context(_sbt(nc, pfx + "_g", [128, D], F32))
        self.junk = stack.enter_context(_sbt(nc, pfx + "_junk", [128, D], BF16))
        self.st = stack.enter_context(_sbt(nc, pfx + "_st", [128, 20], F32))
        self.xs_ring = Ring([self.xs[:, i, :] for i in range(4)])
        self.xn_ring = Ring([self.xn[:, i, :] for i in range(2)])
        self.b_g, self.b_eps = Buf(), Buf()
        self.b_stat = [[Buf(), Buf()], [Buf(), Buf()]]
        self.n = 0
        g = self.g
        C.S.add("sp", lambda e: e.dma_start(out=g[:], in_=gain_dram.partition_broadcast(128)),
                writes=[self.b_g], dma=True)
        st = self.st
        C.S.add("dve", lambda e: e.memset(st[:, 0:1], EPS), writes=[self.b_eps])

    def tile(self, x_in, tok0, NJ, hT, b_hT):
        C, S = self.C, self.C.S
        st, junk, gt = self.st, self.junk, self.g
        base = 4 + (self.n % 2) * 8
        b_ss, b_r = self.b_stat[self.n % 2]
        self.n += 1
        c_ss = st[:, base:base + NJ]
        c_r = st[:, base + 4:base + 4 + NJ]
        x_tiles = []
        for j in range(NJ):
            xa, xb = self.xs_ring.next()
            r0 = tok0 + j * 128
            S.add("sp", (lambda e, xa=xa, r0=r0: e.dma_start(out=xa, in_=x_in[r0:r0 + 128, :])),
                  writes=[xb], dma=True)
            S.add("act", (lambda e, xa=xa, j=j: e.activation(out=junk[:], in_=xa, func=AF.Square,
                                                               accum_out=c_ss[:, j:j + 1])),
                  reads=[xb], writes=[b_ss])
            x_tiles.append((xa, xb))
        S.add("act", lambda e: e.activation(out=c_r, in_=c_ss, func=AF.Sqrt, scale=1.0 / D, bias=st[:, 0:1]),
              reads=[b_ss, self.b_eps], writes=[b_r])
        S.add("dve", lambda e: e.reciprocal(out=c_r, in_=c_r), writes=[b_r])
        for j in range(NJ):
            xa, xb = x_tiles[j]
            na, nb = self.xn_ring.next()
            S.add("dve", (lambda e, xa=xa, na=na, j=j: e.scalar_tensor_tensor(
                out=na, in0=xa, scalar=c_r[:, j:j + 1], in1=gt[:], op0=ALU.mult, op1=ALU.mult)),
                reads=[xb, b_r, self.b_g], writes=[nb])
            for half in range(2):
                pa, pb = C.psum.next()
                pv = pa.bitcast(BF16)

                def tr(e, pv=pv, na=na, half=half):
                    ins = None
                    for i in range(8):
                        k = half * 8 + i
                        ins = e.transpose(out=pv[:, i * 128:(i + 1) * 128],
                                          in_=na[:, k * 128:(k + 1) * 128], identity=C.ident[:])
                    return ins
                S.add("pe", tr, reads=[nb, C.b_ident], writes=[pb])
                dst = hT[:, half * 8:(half + 1) * 8, j * 128:(j + 1) * 128]
                src = pv.rearrange("p (i t) -> p i t", i=8)
                evac_copy(C, dst, src, [pb], [b_hT[j]], eng=("act" if half == 0 else "dve"))


def linear_phase(C, stack_outer, pfx, ntok, x_src, W, ncols, epilogue, norm_gain=None, T=512, pre=None, post=None):
    nc, S = C.nc, C.S
    NJ = T // 128
    NT = ntok // T
    NG = (ncols + 511) // 512
    W_v = W.rearrange("(k p) c -> p k c", p=128)
    with ExitStackCompat() as stack:
        hTt = stack.enter_context(_sbt(nc, pfx + "_hT", [128, 2, KD, T], BF16))
        wt = stack.enter_context(_sbt(nc, pfx + "_w", [128, 3, KD, 512], BF16))
        w_ring = Ring([wt[:, i] for i in range(3)])
        hT_bufs = [[Buf() for _ in range(NJ)] for _ in range(2)]
        hT_kbufs = [[Buf() for _ in range(KD)] for _ in range(2)]
        nt = NormT(C, stack, pfx, norm_gain) if norm_gain is not None else None
        if pre is not None:
            pre(stack)
        for t in range(NT):
            tok0 = t * T
            hT = hTt[:, t % 2]
            b_hT = hT_bufs[t % 2]
            if nt is not None:
                nt.tile(x_src, tok0, NJ, hT, b_hT)
            else:
                for k in range(KD):
                    S.add("sp", (lambda e, hT=hT, k=k, tok0=tok0: e.dma_start_transpose(
                        out=hT[:, k, :], in_=x_src[tok0:tok0 + T, k * 128:(k + 1) * 128])),
                        writes=[hT_kbufs[t % 2][k]], dma=True)
            for n in range(NG):
                gs = min(512, ncols - n * 512)
                wa, wb = w_ring.next()
                S.add("pool", (lambda e, wa=wa, n=n, gs=gs: e.dma_start(
                    out=wa[:, :, 0:gs], in_=W_v[:, :, n * 512:n * 512 + gs])), writes=[wb], dma=True)
                for j in range(NJ):
                    pa, pb = C.psum.next()

                    def mm(e, wa=wa, hT=hT, j=j, gs=gs, ps=pa):
                        ins = None
                        for k in range(KD):
                            ins = e.matmul(ps[:, 0:gs], lhsT=hT[:, k, j * 128:(j + 1) * 128], rhs=wa[:, k, 0:gs],
                                           start=(k == 0), stop=(k == KD - 1))
                        return ins
                    S.add("pe", mm, reads=[wb] + ([b_hT[j]] if nt is not None else hT_kbufs[t % 2]), writes=[pb])
                    epilogue(t, j, n, gs, pa, pb, tok0 + j * 128)
            if post is not None:
                post(t, tok0, NJ)
        S.emit()


class PlainEpi:
    def __init__(self, C, stack, pfx, dst, silu_groups=(), col0=0):
        self.C = C
        self.dst = dst
        self.silu = set(silu_groups)
        self.col0 = col0
        t = stack.enter_context(_sbt(C.nc, pfx + "_stg", [128, 4, 512], BF16))
        self.ring = Ring([t[:, i, :] for i in range(4)])

    def __call__(self, t, j, n, gs, pa, pb, r0):
        C, S = self.C, self.C.S
        sa, sb = self.ring.next()
        if n in self.silu:
            S.add("act", (lambda e: e.activation(out=sa[:, 0:gs], in_=pa[:, 0:gs], func=AF.Silu)),
                  reads=[pb], writes=[sb])
        else:
            evac_copy(C, sa[:, 0:gs], pa[:, 0:gs], [pb], [sb])
        dst, c0 = self.dst, self.col0 + n * 512
        S.add("sp", (lambda e: e.dma_start(out=dst[r0:r0 + 128, c0:c0 + gs], in_=sa[:, 0:gs])),
              reads=[sb], dma=True)


def simple_linear(C, pfx, ntok, x_src, gain, W, ncols, dst, T=512):
    epi = {}

    def pre(stack):
        epi["e"] = PlainEpi(C, stack, pfx, dst)

    linear_phase(C, None, pfx, ntok, x_src, W, ncols, lambda *a: epi["e"](*a), norm_gain=gain, T=T, pre=pre)


def inproj_a_phase(C, x_src, gain, W, P, pos, invf):
    nc, S = C.nc, C.S
    R = {}
    NSUB = NTOK // 128
    PI = float(np.pi)

    def pre(stack):
        posi = stack.enter_context(_sbt(nc, "ia_posi", [128, NSUB], I32))
        posf = stack.enter_context(_sbt(nc, "ia_posf", [128, NSUB], F32))
        inv = stack.enter_context(_sbt(nc, "ia_inv", [128, 64], F32))
        ang = stack.enter_context(_sbt(nc, "ia_ang", [128, NSUB, 64], F32))
        tmp = stack.enter_context(_sbt(nc, "ia_tmp", [128, NSUB, 64], F32))
        cs = stack.enter_context(_sbt(nc, "ia_cs", [128, 4, NSUB, 64], F32))
        cst = stack.enter_context(_sbt(nc, "ia_c", [128, 2], F32))
        rt = stack.enter_context(_sbt(nc, "ia_rt", [128, 2, 4, 256], F32))
        R["cs"] = cs
        R["rt"] = Ring([rt[:, i] for i in range(2)])
        R["epi"] = PlainEpi(C, stack, "ia", P, silu_groups=(9, 10, 11))
        b_pos, b_inv, b_ang, b_tmp, b_c = Buf(), Buf(), Buf(), Buf(), Buf()
        R["b_cs"] = Buf()
        S.add("sp", lambda e: e.dma_start(out=posi[:], in_=pos), writes=[b_pos], dma=True)
        S.add("sp", lambda e: e.dma_start(out=inv[:], in_=invf.partition_broadcast(128)), writes=[b_inv], dma=True)
        S.add("dve", lambda e: e.tensor_copy(out=posf[:], in_=posi[:]), reads=[b_pos], writes=[b_pos])
        for n in range(NSUB):
            S.add("dve", (lambda e, n=n: e.tensor_scalar(out=ang[:, n, :], in0=inv[:], scalar1=posf[:, n:n + 1],
                                                         scalar2=None, op0=ALU.mult)),
                  reads=[b_pos, b_inv], writes=[b_ang])
        angf = ang[:].rearrange("p n i -> p (n i)")
        tmpf = tmp[:].rearrange("p n i -> p (n i)")
        MAGIC = 12582912.0
        C1 = 6.28125
        C2 = float(2 * np.pi - 6.28125)
        PIC = 3.141592
        S.add("dve", lambda e: e.memset(cst[:, 1:2], PI / 2), writes=[b_c])
        S.add("dve", lambda e: e.tensor_scalar(out=tmpf, in0=angf, scalar1=float(1.0 / (2 * np.pi)), scalar2=MAGIC,
                                               op0=ALU.mult, op1=ALU.add), reads=[b_ang], writes=[b_tmp])
        S.add("dve", lambda e: e.tensor_scalar(out=tmpf, in0=tmpf, scalar1=MAGIC, scalar2=None, op0=ALU.subtract),
              writes=[b_tmp])
        S.add("dve", lambda e: e.scalar_tensor_tensor(out=angf, in0=tmpf, scalar=-C1, in1=angf, op0=ALU.mult,
                                                      op1=ALU.add), reads=[b_tmp], writes=[b_ang])
        S.add("dve", lambda e: e.scalar_tensor_tensor(out=angf, in0=tmpf, scalar=-C2, in1=angf, op0=ALU.mult,
                                                      op1=ALU.add), reads=[b_tmp], writes=[b_ang])
        S.add("dve", lambda e: e.tensor_scalar(out=tmpf, in0=angf, scalar1=PIC, scalar2=None, op0=ALU.is_gt),
              reads=[b_ang], writes=[b_tmp])
        S.add("dve", lambda e: e.scalar_tensor_tensor(out=angf, in0=tmpf, scalar=float(-2 * np.pi), in1=angf,
                                                      op0=ALU.mult, op1=ALU.add), reads=[b_tmp], writes=[b_ang])
        S.add("dve", lambda e: e.tensor_scalar(out=angf, in0=angf, scalar1=-PIC, scalar2=PIC, op0=ALU.max,
                                               op1=ALU.min), writes=[b_ang])
        sinv = cs[:, 1].rearrange("p n i -> p (n i)")
        cosv = cs[:, 0].rearrange("p n i -> p (n i)")
        S.add("act", lambda e: e.activation(out=sinv, in_=angf, func=AF.Sin), reads=[b_ang], writes=[R["b_cs"]])
        S.add("act", lambda e: e.activation(out=tmpf, in_=angf, func=AF.Abs), reads=[b_ang], writes=[b_tmp])
        S.add("act", lambda e: e.activation(out=cosv, in_=tmpf, func=AF.Sin, bias=cst[:, 1:2], scale=-1.0),
              reads=[b_tmp, b_c], writes=[R["b_cs"]])
        for which in (0, 1):
            srcv = cs[:, which].rearrange("p n i -> p (n i)")
            dstv = cs[:, 2 + which].rearrange("p n i -> p (n i)")
            S.add("dve", (lambda e, srcv=srcv, dstv=dstv: e.tensor_scalar(
                out=dstv, in0=srcv, scalar1=float(HD ** -0.5), scalar2=None, op0=ALU.mult)),
                writes=[R["b_cs"]])

    def epilogue(t, j, n, gs, pa, pb, r0):
        if n >= 6:
            return R["epi"](t, j, n, gs, pa, pb, r0)
        nn = r0 // 128
        cs = R["cs"]
        koff = 0 if n < 3 else 2
        cosb = cs[:, koff + 0, nn:nn + 1, :].to_broadcast([128, 4, 64])
        sinb = cs[:, koff + 1, nn:nn + 1, :].to_broadcast([128, 4, 64])
        ps4 = pa.rearrange("p (h two i) -> p h two i", h=4, two=2)
        t1, t2 = ps4[:, :, 0, :], ps4[:, :, 1, :]
        rta, rtb = R["rt"].next()
        tv = [rta[:, i].rearrange("p (h i) -> p h i", h=4) for i in range(4)]
        sa, sb = R["epi"].ring.next()
        so = sa.rearrange("p (h two i) -> p h two i", h=4, two=2)
        bcs = R["b_cs"]
        ba, bb2, bc, bd = Buf(), Buf(), Buf(), Buf()
        S.add("dve", lambda e: e.tensor_tensor(out=tv[0], in0=t1, in1=cosb, op=ALU.mult), reads=[pb, bcs], writes=[rtb, ba])
        S.add("dve", lambda e: e.tensor_tensor(out=tv[1], in0=t2, in1=sinb, op=ALU.mult), reads=[pb, bcs], writes=[bb2])
        S.add("dve", lambda e: e.tensor_tensor(out=tv[2], in0=t2, in1=cosb, op=ALU.mult), reads=[pb, bcs], writes=[bc])
        S.add("dve", lambda e: e.tensor_tensor(out=tv[3], in0=t1, in1=sinb, op=ALU.mult), reads=[pb, bcs], writes=[bd])
        S.add("dve", lambda e: e.tensor_tensor(out=so[:, :, 0, :], in0=tv[0], in1=tv[1], op=ALU.subtract),
              reads=[ba, bb2], writes=[sb])
        S.add("dve", lambda e: e.tensor_tensor(out=so[:, :, 1, :], in0=tv[2], in1=tv[3], op=ALU.add),
              reads=[bc, bd, rtb], writes=[sb])
        c0 = n * 512
        S.add("sp", (lambda e: e.dma_start(out=P[r0:r0 + 128, c0:c0 + 512], in_=sa)), reads=[sb, rtb], dma=True)

    linear_phase(C, None, "ia", NTOK, x_src, W, 4 * MIXW + MEMW, epilogue, norm_gain=gain, pre=pre)


def load_T(C, dstT, src, tok0, ntok, col0, writes):
    S = C.S
    for r in range(ntok // 512):
        S.add("sp", (lambda e, r=r: e.dma_start_transpose(
            out=dstT[:, r * 512:(r + 1) * 512], in_=src[tok0 + r * 512:tok0 + (r + 1) * 512, col0:col0 + 128])),
            writes=[writes[r]], dma=True)


def retention_phase(C, P, ycat, dmaskT_d, qdec_d, kdec_d):
    nc, S = C.nc, C.S
    NC = SEQ // 128
    lg = [float(np.log1p(-2.0 ** (-5.0 - h))) for h in range(NH)]
    cdec = [float(np.exp(np.float32(l) * 128)) for l in lg]
    with ExitStackCompat() as stack:
        T_ = lambda name, shape, dt: stack.enter_context(_sbt(nc, "rt_" + name, shape, dt))
        dmask = T_("dmask", [128, NH, 128], F32)
        qdec = T_("qdec", [128, NH, 128], F32)
        kdec = T_("kdec", [128, NH], F32)
        qT = T_("qT", [128, 2, SEQ], BF16)
        kT = T_("kT", [128, 2, SEQ], BF16)
        qTd = T_("qTd", [128, 2, SEQ], BF16)
        ktok = T_("ktok", [128, 2, NC, 128], BF16)
        vtok = T_("vtok", [128, 2, NC, 128], BF16)
        gtok = T_("gtok", [128, 2, NC, 128], BF16)
        osb = T_("osb", [128, 2, NC, 128], F32)
        yo = T_("yo", [128, 2, NC, 128], BF16)
        state = T_("state", [128, 2, 128], F32)
        stbf = T_("stbf", [128, 2, 128], BF16)
        stm = T_("stm", [128, 3, 128], BF16)
        junk = T_("junk", [128, 128], BF16)
        st = T_("st", [128, 2, 5, NC], F32)
        cst = T_("cst", [128, 1], F32)
        b_const = Buf()
        S.add("sp", lambda e: e.dma_start(out=dmask[:], in_=dmaskT_d), writes=[b_const], dma=True)
        S.add("sp", lambda e: e.dma_start(out=qdec[:], in_=qdec_d), writes=[b_const], dma=True)
        S.add("sp", lambda e: e.dma_start(out=kdec[:], in_=kdec_d), writes=[b_const], dma=True)
        b_eps = Buf()
        S.add("dve", lambda e: e.memset(cst[:, 0:1], EPS), writes=[b_eps])
        b_ycst = Buf()
        stm_ring = Ring([stm[:, i, :] for i in range(3)])
        stbf_ring = Ring([stbf[:, i, :] for i in range(2)])
        slot_b = [{k: Buf() for k in ("qT0", "qT1", "qT2", "qT3", "kT0", "kT1", "kT2", "kT3", "qTd", "ktok",
                                      "vtok", "gtok", "osb", "yo", "state", "sum", "sq", "stat")} for _ in range(2)]
        it = 0
        for s in range(NSEQ):
            for h in range(NH):
                sl = it % 2
                it += 1
                B = slot_b[sl]
                tok0 = s * SEQ
                bq = [B["qT%d" % r] for r in range(4)]
                bk = [B["kT%d" % r] for r in range(4)]
                load_T(C, qT[:, sl, :], P, tok0, SEQ, h * HD, bq)
                load_T(C, kT[:, sl, :], P, tok0, SEQ, MIXW + h * HD, bk)
                for name, col, tl in (("ktok", MIXW + h * HD, ktok), ("vtok", 2 * MIXW + h * HD, vtok),
                                      ("gtok", 3 * MIXW + h * HD, gtok)):
                    S.add("sp", (lambda e, tl=tl, col=col, sl=sl, tok0=tok0: e.dma_start(
                        out=tl[:, sl], in_=P[tok0:tok0 + SEQ, col:col + HD].rearrange("(n p) d -> p n d", p=128))),
                        writes=[B[name]], dma=True)
                S.add("dve", (lambda e, sl=sl, h=h: e.tensor_tensor(
                    out=qTd[:, sl, :].rearrange("p (n c) -> p n c", c=128),
                    in0=qT[:, sl, :].rearrange("p (n c) -> p n c", c=128),
                    in1=qdec[:, h:h + 1, :].to_broadcast([128, NC, 128]), op=ALU.mult)),
                    reads=bq + [b_const], writes=[B["qTd"]])
                S.add("act", (lambda e, sl=sl, h=h: e.activation(
                    out=ktok[:, sl].rearrange("p n d -> p (n d)"), in_=ktok[:, sl].rearrange("p n d -> p (n d)"),
                    func=AF.Copy, scale=kdec[:, h:h + 1])),
                    reads=[b_const], writes=[B["ktok"]])
                sbf_prev = None
                for n in range(NC):
                    csl = slice(n * 128, (n + 1) * 128)
                    r = n // 4
                    p1, p1b = C.psum.next()
                    S.add("pe", (lambda e, p1=p1, sl=sl, csl=csl: e.matmul(
                        p1[:, 0:128], lhsT=kT[:, sl, csl], rhs=qT[:, sl, csl], start=True, stop=True)),
                        reads=[bk[r], bq[r]], writes=[p1b])
                    ma, mb = stm_ring.next()
                    S.add("dve", (lambda e, ma=ma, p1=p1, h=h: e.tensor_tensor(
                        out=ma, in0=p1[:, 0:128], in1=dmask[:, h, :], op=ALU.mult)),
                        reads=[p1b, b_const], writes=[mb])
                    p2, p2b = C.psum.next()

                    def mm_o(e, p2=p2, ma=ma, sl=sl, n=n, csl=csl, sbf=sbf_prev):
                        ins = e.matmul(p2[:, 0:128], lhsT=ma, rhs=vtok[:, sl, n, :], start=True, stop=(n == 0))
                        if n > 0:
                            ins = e.matmul(p2[:, 0:128], lhsT=qTd[:, sl, csl], rhs=sbf[0], start=False, stop=True)
                        return ins
                    rd = [mb, B["vtok"], B["qTd"]] + ([sbf_prev[1]] if n > 0 else [])
                    S.add("pe", mm_o, reads=rd, writes=[p2b])
                    S.add("act", (lambda e, p2=p2, sl=sl, n=n: e.activation(
                        out=osb[:, sl, n, :], in_=p2[:, 0:128], func=AF.Copy, accum_out=st[:, sl, 0, n:n + 1])),
                        reads=[p2b], writes=[B["osb"], B["sum"]])
                    S.add("act", (lambda e, p2=p2, sl=sl, n=n: e.activation(
                        out=junk[:], in_=p2[:, 0:128], func=AF.Square, accum_out=st[:, sl, 1, n:n + 1])),
                        reads=[p2b], writes=[B["sq"]])
                    if n < NC - 1:
                        p3, p3b = C.psum.next()
                        S.add("pe", (lambda e, p3=p3, sl=sl, n=n: e.matmul(
                            p3[:, 0:128], lhsT=ktok[:, sl, n, :], rhs=vtok[:, sl, n, :], start=True, stop=True)),
                            reads=[B["ktok"], B["vtok"]], writes=[p3b])
                        if n == 0:
                            S.add("dve", (lambda e, p3=p3, sl=sl: e.tensor_copy(out=state[:, sl, :], in_=p3[:, 0:128])),
                                  reads=[p3b], writes=[B["state"]])
                        else:
                            S.add("dve", (lambda e, p3=p3, sl=sl, h=h: e.scalar_tensor_tensor(
                                out=state[:, sl, :], in0=state[:, sl, :], scalar=cdec[h], in1=p3[:, 0:128],
                                op0=ALU.mult, op1=ALU.add)),
                                reads=[p3b], writes=[B["state"]])
                        sa, sb = stbf_ring.next()
                        S.add("act", (lambda e, sa=sa, sl=sl: e.activation(out=sa, in_=state[:, sl, :], func=AF.Copy)),
                              reads=[B["state"]], writes=[sb])
                        sbf_prev = (sa, sb)
                c_sum, c_sq = st[:, sl, 0, :], st[:, sl, 1, :]
                c_mean, c_var, c_rstd = st[:, sl, 2, :], st[:, sl, 3, :], st[:, sl, 4, :]
                bs = B["stat"]
                S.add("dve", (lambda e, c_mean=c_mean, c_sum=c_sum: e.tensor_scalar(
                    out=c_mean, in0=c_sum, scalar1=1.0 / HD, scalar2=None, op0=ALU.mult)),
                    reads=[B["sum"]], writes=[bs])
                S.add("dve", (lambda e, c_mean=c_mean, c_rstd=c_rstd: e.tensor_tensor(
                    out=c_rstd, in0=c_mean, in1=c_mean, op=ALU.mult)), writes=[bs])
                S.add("dve", (lambda e, c_var=c_var, c_sq=c_sq, c_rstd=c_rstd: e.scalar_tensor_tensor(
                    out=c_var, in0=c_sq, scalar=1.0 / HD, in1=c_rstd, op0=ALU.mult, op1=ALU.subtract)),
                    reads=[B["sq"]], writes=[bs])
                S.add("act", (lambda e, c_var=c_var: e.activation(out=c_var, in_=c_var, func=AF.Sqrt,
                                                                  bias=cst[:, 0:1], scale=1.0)),
                      reads=[b_eps], writes=[bs])
                S.add("dve", (lambda e, c_var=c_var, c_rstd=c_rstd: e.reciprocal(out=c_rstd, in_=c_var)), writes=[bs])
                o3 = osb[:, sl]
                S.add("dve", (lambda e, o3=o3, c_mean=c_mean: e.tensor_tensor(
                    out=o3, in0=o3, in1=c_mean.unsqueeze(2).to_broadcast([128, NC, 128]), op=ALU.subtract)),
                    reads=[bs], writes=[B["osb"]])
                S.add("dve", (lambda e, o3=o3, c_rstd=c_rstd: e.tensor_tensor(
                    out=o3, in0=o3, in1=c_rstd.unsqueeze(2).to_broadcast([128, NC, 128]), op=ALU.mult)),
                    reads=[bs], writes=[B["osb"]])
                S.add("dve", (lambda e, o3=o3, sl=sl: e.tensor_tensor(
                    out=yo[:, sl], in0=o3, in1=gtok[:, sl], op=ALU.mult)),
                    reads=[B["osb"], B["gtok"]], writes=[B["yo"]])
                S.add("sp", (lambda e, sl=sl, tok0=tok0, h=h: e.dma_start(
                    out=ycat[tok0:tok0 + SEQ, h * HD:(h + 1) * HD].rearrange("(n p) d -> p n d", p=128),
                    in_=yo[:, sl])), reads=[B["yo"]], writes=[b_ycst], dma=True)
        S.emit()


def memattn_phase(C, Q, qcol0, MKV, ycat):
    nc, S = C.nc, C.S
    NTL = SEQ // 128
    scale = float(HD ** -0.5)
    with ExitStackCompat() as stack:
        T_ = lambda name, shape, dt: stack.enter_context(_sbt(nc, "ma_" + name, shape, dt))
        mkT = T_("mkT", [128, NMH, NMEM], BF16)
        mv = T_("mv", [128, 2, MEMW], BF16)
        qmT = T_("qmT", [128, NMH, SEQ], BF16)
        p_sb = T_("p", [128, 3, NMEM], BF16)
        pT = T_("pT", [128, 3, 2, 128], BF16)
        yo = T_("yo", [128, 3, MEMW], BF16)
        st = T_("st", [128, 4, 4], F32)
        p_ring = Ring([p_sb[:, i, :] for i in range(3)])
        pT_ring = Ring([pT[:, i] for i in range(3)])
        yo_ring = Ring([yo[:, i, :] for i in range(3)])
        st_ring = Ring([st[:, i, :] for i in range(4)])
        b_mk, b_mv = Buf(), Buf()
        b_q = [[Buf() for _ in range(4)] for _ in range(NMH)]
        for s in range(NSEQ):
            tok0 = s * SEQ
            for hm in range(NMH):
                S.add("sp", (lambda e, hm=hm, s=s: e.dma_start_transpose(
                    out=mkT[:, hm, :], in_=MKV[s * NMEM:(s + 1) * NMEM, hm * HD:(hm + 1) * HD])),
                    writes=[b_mk], dma=True)
                load_T(C, qmT[:, hm, :], Q, tok0, SEQ, qcol0 + hm * HD, b_q[hm])
            S.add("sp", (lambda e, s=s: e.dma_start(
                out=mv[:], in_=MKV[s * NMEM:(s + 1) * NMEM, MEMW:2 * MEMW].rearrange("(c p) d -> p c d", p=128))),
                writes=[b_mv], dma=True)
            for j in range(NTL):
                ya, yb = yo_ring.next()
                for hm in range(NMH):
                    ps, psb = C.psum.next()
                    S.add("pe", (lambda e, ps=ps, hm=hm, j=j: e.matmul(
                        ps[:, 0:NMEM], lhsT=qmT[:, hm, j * 128:(j + 1) * 128], rhs=mkT[:, hm, :],
                        start=True, stop=True)), reads=[b_q[hm][j // 4], b_mk], writes=[psb])
                    sa, sb = st_ring.next()
                    S.add("dve", (lambda e, ps=ps, sa=sa: e.tensor_reduce(
                        out=sa[:, 0:1], in_=ps[:, 0:NMEM], axis=AX.X, op=ALU.max)), reads=[psb], writes=[sb])
                    S.add("dve", (lambda e, sa=sa: e.tensor_scalar(
                        out=sa[:, 1:2], in0=sa[:, 0:1], scalar1=-scale, scalar2=None, op0=ALU.mult)), writes=[sb])
                    pa, pb = p_ring.next()
                    S.add("act", (lambda e, ps=ps, sa=sa, pa=pa: e.activation(
                        out=pa, in_=ps[:, 0:NMEM], func=AF.Exp, bias=sa[:, 1:2], scale=scale,
                        accum_out=sa[:, 2:3])), reads=[psb], writes=[pb, sb])
                    S.add("dve", (lambda e, sa=sa: e.reciprocal(out=sa[:, 3:4], in_=sa[:, 2:3])), writes=[sb])
                    pt, ptb = C.psum.next()
                    ptv = pt.bitcast(BF16)

                    def tr(e, ptv=ptv, pa=pa):
                        ins = None
                        for c in range(2):
                            ins = e.transpose(out=ptv[:, c * 128:(c + 1) * 128], in_=pa[:, c * 128:(c + 1) * 128],
                                              identity=C.ident[:])
                        return ins
                    S.add("pe", tr, reads=[pb, C.b_ident], writes=[ptb])
                    ta, tb = pT_ring.next()
                    evac_copy(C, ta, ptv[:, 0:256].rearrange("p (c t) -> p c t", c=2), [ptb], [tb])
                    po, pob = C.psum.next()

                    def mm(e, po=po, ta=ta, hm=hm):
                        ins = None
                        for c in range(2):
                            ins = e.matmul(po[:, 0:128], lhsT=ta[:, c, :], rhs=mv[:, c, hm * HD:(hm + 1) * HD],
                                           start=(c == 0), stop=(c == 1))
                        return ins
                    S.add("pe", mm, reads=[tb, b_mv], writes=[pob])
                    S.add("act", (lambda e, po=po, ya=ya, hm=hm, sa=sa: e.activation(
                        out=ya[:, hm * HD:(hm + 1) * HD], in_=po[:, 0:128], func=AF.Copy, scale=sa[:, 3:4])),
                        reads=[pob, sb], writes=[yb])
                r0 = tok0 + j * 128
                S.add("sp", (lambda e, ya=ya, r0=r0: e.dma_start(out=ycat[r0:r0 + 128, MIXW:MIXW + MEMW], in_=ya)),
                      reads=[yb], dma=True)
        S.emit()


class PostNorm:
    def __init__(self, C, stack, pfx, gain_dram, x_in, x_out, wres):
        nc = C.nc
        self.C, self.x_in, self.x_out, self.wres = C, x_in, x_out, wres
        self.ys = stack.enter_context(_sbt(nc, pfx + "_ys", [128, 3, 512], F32))
        self.xy = stack.enter_context(_sbt(nc, pfx + "_xy", [128, 4, D], F32))
        self.g = stack.enter_context(_sbt(nc, pfx + "_gp", [128, D], F32))
        self.junk = stack.enter_context(_sbt(nc, pfx + "_pj", [128, 512], BF16))
        self.st = stack.enter_context(_sbt(nc, pfx + "_pst", [128, 2, 24], F32))
        self.cst = stack.enter_context(_sbt(nc, pfx + "_pc", [128, 1], F32))
        self.ys_ring = Ring([self.ys[:, i, :] for i in range(3)])
        self.xy_ring = Ring([self.xy[:, i, :] for i in range(4)])
        self.b_g, self.b_eps = Buf(), Buf()
        self.b_st = [[Buf(), Buf()], [Buf(), Buf()]]
        self.b_y = {}
        g, cst = self.g, self.cst
        C.S.add("sp", lambda e: e.dma_start(out=g[:], in_=gain_dram.partition_broadcast(128)),
                writes=[self.b_g], dma=True)
        C.S.add("dve", lambda e: e.memset(cst[:, 0:1], EPS), writes=[self.b_eps])

    def epilogue(self, t, j, n, gs, pa, pb, r0):
        C, S = self.C, self.C.S
        ya, yb = self.ys_ring.next()
        st = self.st
        b_ss = self.b_st[t % 2][0]
        x_out = self.x_out
        junk = self.junk
        S.add("act", (lambda e: e.activation(out=ya, in_=pa, func=AF.Copy)), reads=[pb], writes=[yb])
        S.add("act", (lambda e: e.activation(out=junk[:], in_=pa, func=AF.Square,
                                             accum_out=st[:, t % 2, j * 4 + n:j * 4 + n + 1])),
              reads=[pb], writes=[b_ss])
        by = Buf()
        self.b_y.setdefault((t, j), []).append(by)
        S.add("sp", (lambda e: e.dma_start(out=x_out[r0:r0 + 128, n * 512:(n + 1) * 512], in_=ya)),
              reads=[yb], writes=[by], dma=True)

    def finalize(self, t, tok0, NJ):
        C, S = self.C, self.C.S
        st, gt, cst = self.st, self.g, self.cst
        b_ss, b_r = self.b_st[t % 2]
        c_ss = st[:, t % 2, 0:16]
        c_r = st[:, t % 2, 16:16 + NJ]
        x_in, x_out, wres = self.x_in, self.x_out, self.wres
        S.add("dve", lambda e: e.tensor_reduce(out=c_r, in_=c_ss[:, 0:NJ * 4].rearrange("p (j n) -> p j n", n=4),
                                               axis=AX.X, op=ALU.add), reads=[b_ss], writes=[b_r])
        S.add("act", lambda e: e.activation(out=c_r, in_=c_r, func=AF.Sqrt, scale=1.0 / D, bias=cst[:, 0:1]),
              reads=[self.b_eps], writes=[b_r])
        S.add("dve", lambda e: e.reciprocal(out=c_r, in_=c_r), writes=[b_r])
        for j in range(NJ):
            r0 = tok0 + j * 128
            ya, yb = self.xy_ring.next()
            xa, xb = self.xy_ring.next()
            bys = self.b_y.pop((t, j))
            S.add("sp", (lambda e, ya=ya, r0=r0: e.dma_start(out=ya, in_=x_out[r0:r0 + 128, :])),
                  reads=bys, writes=[yb], dma=True)
            S.add("sp", (lambda e, xa=xa, r0=r0: e.dma_start(out=xa, in_=x_in[r0:r0 + 128, :])),
                  writes=[xb], dma=True)
            S.add("dve", (lambda e, ya=ya, j=j: e.scalar_tensor_tensor(
                out=ya, in0=ya, scalar=c_r[:, j:j + 1], in1=gt[:], op0=ALU.mult, op1=ALU.mult)),
                reads=[b_r, self.b_g], writes=[yb])
            S.add("dve", (lambda e, ya=ya, xa=xa: e.scalar_tensor_tensor(
                out=ya, in0=ya, scalar=wres, in1=xa, op0=ALU.mult, op1=ALU.add)),
                reads=[xb], writes=[yb])
            S.add("sp", (lambda e, ya=ya, r0=r0: e.dma_start(out=x_out[r0:r0 + 128, :], in_=ya)),
                  reads=[yb], writes=bys, dma=True)


def outproj_phase(C, ycat, W, gain, x_in, x_out):
    R = {}

    def pre(stack):
        R["pn"] = PostNorm(C, stack, "op", gain, x_in, x_out, 1.0)

    def epilogue(t, j, n, gs, pa, pb, r0):
        R["pn"].epilogue(t, j, n, gs, pa, pb, r0)

    def post(t, tok0, NJ):
        R["pn"].finalize(t, tok0, NJ)

    linear_phase(C, None, "op", NTOK, ycat, W, D, epilogue, norm_gain=None, pre=pre, post=post)


def stickbreak_phase(C, Q, KV, ycat, tril_f_d, tril_b_d):
    nc, S = C.nc, C.S
    NB = SEQ // 128
    scale = float(HD ** -0.5)
    with ExitStackCompat() as stack:
        T_ = lambda name, shape, dt: stack.enter_context(_sbt(nc, "sb_" + name, shape, dt))
        qT = T_("qT", [128, 4, SEQ], BF16)
        kT = T_("kT", [128, 4, SEQ], BF16)
        vtok = T_("vtok", [128, 4, NB, 128], BF16)
        yo = T_("yo", [128, 4, NB, 128], BF16)
        Et = T_("E", [128, 2, SEQ], F32)
        Lt = T_("L", [128, 2, SEQ], F32)
        Ct = T_("C", [128, 2, SEQ], F32)
        At = T_("A", [128, 2, SEQ], BF16)
        ATt = T_("AT", [128, 2, NB, 128], BF16)
        trf = T_("trf", [128, 128], F32)
        trb = T_("trb", [128, 128], BF16)
        ones = T_("ones", [128, 1], F32)
        b_const = Buf()
        S.add("sp", lambda e: e.dma_start(out=trf[:], in_=tril_f_d), writes=[b_const], dma=True)
        S.add("sp", lambda e: e.dma_start(out=trb[:], in_=tril_b_d), writes=[b_const], dma=True)
        S.add("dve", lambda e: e.memset(ones[:], 1.0), writes=[b_const])
        b_ycst = Buf()
        hb = [{k: Buf() for k in ("q0", "q1", "q2", "q3", "k0", "k1", "k2", "k3", "v", "yo")} for _ in range(4)]
        wb = [{k: Buf() for k in ("E", "L", "C", "A", "AT")} for _ in range(2)]

        def block_stages(st, sl, i):
            B, W_ = hb[sl], wb[st]
            bq = [B["q%d" % r] for r in range(4)]
            bk = [B["k%d" % r] for r in range(4)]
            L = (i + 1) * 128
            E, Lb, Cb, A, AT = Et[:, st, :], Lt[:, st, :], Ct[:, st, :], At[:, st, :], ATt[:, st]
            nbk = (L + 511) // 512
            zb = []

            def s0():
                for c in range(nbk):
                    zb.append(C.psum.next())
                for c in range(nbk):
                    w = min(512, L - c * 512)
                    S.add("pe", (lambda e, c=c, w=w: e.matmul(
                        zb[c][0][:, 0:w], lhsT=qT[:, sl, i * 128:(i + 1) * 128], rhs=kT[:, sl, c * 512:c * 512 + w],
                        start=True, stop=True)), reads=[bq[i // 4], bk[c]], writes=[zb[c][1]])

            def s1():
                for c in range(nbk):
                    w = min(512, L - c * 512)
                    S.add("act", (lambda e, c=c, w=w: e.activation(
                        out=E[:, c * 512:c * 512 + w], in_=zb[c][0][:, 0:w], func=AF.Exp, scale=-scale)),
                        reads=[zb[c][1]], writes=[W_["E"]])
                if SB_OLDMATH:
                    return
                for c in range(nbk):
                    w = min(512, L - c * 512)
                    S.add("dve", (lambda e, c=c, w=w: e.tensor_scalar(
                        out=Lb[:, c * 512:c * 512 + w], in0=zb[c][0][:, 0:w], scalar1=-scale, scalar2=None,
                        op0=ALU.mult)), reads=[zb[c][1]], writes=[W_["L"]])

            def s2():
                S.add("act", (lambda e: e.activation(out=E[:, 0:L], in_=E[:, 0:L], func=AF.Ln,
                                                     bias=ones[:, 0:1], scale=1.0)),
                      reads=[b_const], writes=[W_["E"]])

            def s3():
                if SB_OLDMATH:
                    for c in range(nbk):
                        w = min(512, L - c * 512)
                        S.add("dve", (lambda e, c=c, w=w: e.scalar_tensor_tensor(
                            out=Lb[:, c * 512:c * 512 + w], in0=zb[c][0][:, 0:w], scalar=-scale,
                            in1=E[:, c * 512:c * 512 + w], op0=ALU.mult, op1=ALU.subtract)),
                            reads=[zb[c][1], W_["E"]], writes=[W_["L"]])
                else:
                    S.add(SB_POOL_ENG, (lambda e: e.tensor_tensor(out=Lb[:, 0:L], in0=Lb[:, 0:L], in1=E[:, 0:L],
                                                                  op=ALU.subtract)),
                          reads=[W_["E"]], writes=[W_["L"]])
                S.add(SB_POOL_ENG, (lambda e: e.tensor_tensor(out=Lb[:, L - 128:L], in0=Lb[:, L - 128:L], in1=trf[:],
                                                         op=ALU.mult)),
                      reads=[b_const], writes=[W_["L"]])

            def s4():
                S.add("dve", (lambda e: e.tensor_tensor_scan(
                    out=Cb[:, 0:L], data0=ones[:, 0:1].to_broadcast([128, L]), data1=Lb[:, 0:L], initial=0.0,
                    op0=ALU.mult, op1=ALU.add)), reads=[W_["L"], b_const], writes=[W_["C"]])

            def s5():
                S.add(SB_POOL_ENG, (lambda e: e.tensor_tensor(out=Lb[:, 0:L], in0=Cb[:, 0:L], in1=E[:, 0:L], op=ALU.add)),
                      reads=[W_["C"], W_["E"]], writes=[W_["L"]])

            def s6():
                S.add("act", (lambda e: e.activation(out=A[:, 0:L], in_=Lb[:, 0:L], func=AF.Exp,
                                                     bias=Cb[:, L - 1:L], scale=-1.0)),
                      reads=[W_["L"], W_["C"]], writes=[W_["A"]])

            def s7():
                S.add(SB_POOL_ENG, (lambda e: e.tensor_tensor(out=A[:, L - 128:L], in0=A[:, L - 128:L], in1=trb[:],
                                                         op=ALU.mult)),
                      reads=[b_const], writes=[W_["A"]])

            def s8():
                ntb = (i + 1 + 7) // 8
                for tb_ in range(ntb):
                    nblk = min(8, i + 1 - tb_ * 8)
                    pt, ptb = C.psum.next()
                    ptv = pt.bitcast(BF16)

                    def tr(e, ptv=ptv, tb_=tb_, nblk=nblk):
                        ins = None
                        for bb in range(nblk):
                            blk = tb_ * 8 + bb
                            ins = e.transpose(out=ptv[:, bb * 128:(bb + 1) * 128],
                                              in_=A[:, blk * 128:(blk + 1) * 128], identity=C.ident[:])
                        return ins
                    S.add("pe", tr, reads=[W_["A"], C.b_ident], writes=[ptb])
                    evac_copy(C, AT[:, tb_ * 8:tb_ * 8 + nblk, :],
                              ptv[:, 0:nblk * 128].rearrange("p (b t) -> p b t", b=nblk), [ptb], [W_["AT"]],
                              eng="dve")

            def s9():
                po, pob = C.psum.next()

                def mm(e):
                    ins = None
                    for bb in range(i + 1):
                        ins = e.matmul(po[:, 0:128], lhsT=AT[:, bb, :], rhs=vtok[:, sl, bb, :],
                                       start=(bb == 0), stop=(bb == i))
                    return ins
                S.add("pe", mm, reads=[W_["AT"], B["v"]], writes=[pob])
                evac_copy(C, yo[:, sl, i, :], po[:, 0:128], [pob], [B["yo"]], eng="dve")

            return [s0, s1, s2, s3, s4, s5, s6, s7, s8, s9]

        pair = 0
        for s in range(NSEQ):
            tok0 = s * SEQ
            for hp in range(NH // 2):
                slots = [(pair % 2) * 2 + st for st in range(2)]
                pair += 1
                for st in range(2):
                    h = hp * 2 + st
                    sl = slots[st]
                    B = hb[sl]
                    load_T(C, qT[:, sl, :], Q, tok0, SEQ, h * HD, [B["q%d" % r] for r in range(4)])
                    load_T(C, kT[:, sl, :], KV, tok0, SEQ, h * HD, [B["k%d" % r] for r in range(4)])
                    S.add("sp", (lambda e, sl=sl, tok0=tok0, h=h: e.dma_start(
                        out=vtok[:, sl], in_=KV[tok0:tok0 + SEQ, MIXW + h * HD:MIXW + (h + 1) * HD].rearrange(
                            "(n p) d -> p n d", p=128))), writes=[B["v"]], dma=True)
                for i in range(NB):
                    a = block_stages(0, slots[0], i)
                    b = block_stages(1, slots[1], i)
                    order = [a[0], a[1], a[2], b[0], b[1], b[2], a[3], a[4], a[5], b[3], b[4], b[5],
                             a[6], a[7], b[6], b[7], a[8], a[9], b[8], b[9]]
                    if SB_SEQUENTIAL:
                        order = a + b
                    for f in order:
                        f()
                for st in range(2):
                    h = hp * 2 + st
                    sl = slots[st]
                    S.add("sp", (lambda e, sl=sl, tok0=tok0, h=h: e.dma_start(
                        out=ycat[tok0:tok0 + SEQ, h * HD:(h + 1) * HD].rearrange("(n p) d -> p n d", p=128),
                        in_=yo[:, sl])), reads=[hb[sl]["yo"]], writes=[b_ycst], dma=True)
        S.emit()


W_SPECS = [
    ("ffn1_norm_pre", [2, D]), ("ffn1_norm_post", [2, D]),
    ("ffn1_w_gate", [2, D, DFF]), ("ffn1_w_up", [2, D, DFF]), ("ffn1_w_down", [2, DFF, D]),
    ("mix_norm_pre", [2, D]), ("mix_norm_post", [2, D]), ("mem_norm", [2, D]),
    ("w_mem_kv", [2, D, 2 * MEMW]), ("w_o", [2, D, D]), ("ret_w_in", [1, D, 4 * MIXW + MEMW]),
    ("kv_norm", [D]), ("w_kv_shared", [D, 2 * MIXW]), ("sb_w_in", [1, D, MIXW + MEMW]),
    ("ffn2_norm_pre", [2, D]), ("ffn2_norm_post", [2, D]),
    ("ffn2_w_gate", [2, D, DFF]), ("ffn2_w_up", [2, D, DFF]), ("ffn2_w_down", [2, DFF, D]),
]


def make_consts():
    h = np.arange(NH, dtype=np.float32)
    lg = np.log1p(-np.exp2(-5.0 - h)).astype(np.float32)
    idx = np.arange(128, dtype=np.float32)
    diff = idx[None, :] - idx[:, None]
    dm = np.where(diff >= 0, np.exp(lg[:, None, None] * np.maximum(diff, 0.0)[None]), 0.0).astype(np.float32)
    dmaskT = np.ascontiguousarray(dm.transpose(1, 0, 2))
    qd = np.exp(lg[:, None] * (idx + 1.0)[None, :]).astype(np.float32)
    qdec = np.ascontiguousarray(np.broadcast_to(qd[None], (128, NH, 128))).astype(np.float32)
    kdec = np.ascontiguousarray(np.exp(lg[None, :] * (127.0 - idx)[:, None]).astype(np.float32))
    invf = (10000.0 ** (-np.arange(0, HD, 2, dtype=np.float32) / HD)).astype(np.float32)
    tril = (idx[None, :] < idx[:, None]).astype(np.float32)
    return {
        "c_ident": np.eye(128, dtype=np.float32).astype(ml_dtypes.bfloat16),
        "c_invf": invf, "c_dmaskT": dmaskT, "c_qdec": qdec, "c_kdec": kdec,
        "c_tril_f": tril, "c_tril_b": tril.astype(ml_dtypes.bfloat16),
    }


def build_program(phases=None):
    nc = bass.Bass("TRN2", target_bir_lowering=False)
    IN = lambda name, shape, dt=F32: nc.dram_tensor(name, shape, dt, kind="ExternalInput").ap()
    SCR = lambda name, shape, dt: nc.dram_tensor(name, shape, dt, kind="Internal").ap()
    x = IN("x", [NTOK, D])
    mem = IN("mem", [NSEQ * NMEM, D])
    pos = IN("positions", [128, NTOK // 128], I32)
    Wt = {name: IN(name, shape) for name, shape in W_SPECS}
    cst = {
        "c_ident": IN("c_ident", [128, 128], BF16), "c_invf": IN("c_invf", [64]),
        "c_dmaskT": IN("c_dmaskT", [128, NH, 128]), "c_qdec": IN("c_qdec", [128, NH, 128]),
        "c_kdec": IN("c_kdec", [128, NH]), "c_tril_f": IN("c_tril_f", [128, 128]),
        "c_tril_b": IN("c_tril_b", [128, 128], BF16),
    }
    out = nc.dram_tensor("out", [NTOK, D], F32, kind="ExternalOutput").ap()
    xa = SCR("s_xa", [NTOK, D], F32)
    xb = SCR("s_xb", [NTOK, D], F32)
    P = SCR("s_P", [NTOK, 4 * MIXW + MEMW], BF16)
    KVs = SCR("s_KV", [NTOK, 2 * MIXW], BF16)
    ycat = SCR("s_ycat", [NTOK, D], BF16)
    MKV = SCR("s_MKV", [NSEQ * NMEM, 2 * MEMW], BF16)
    wbf = {}
    for key in ((0, 2), (1, 1), (1, 2)):
        wbf[key] = (SCR("s_wg%d%d" % key, [D, DFF], BF16), SCR("s_wu%d%d" % key, [D, DFF], BF16),
                    SCR("s_wd%d%d" % key, [DFF, D], BF16))
    with ExitStackCompat() as stack:
        C = make_ctx(nc, stack, cst["c_ident"])

        cast_bufs = [Buf(), Buf()]
        cast_n = [0]

        def cast_ffn(l, which):
            p = "ffn%d_" % which
            dg, du, dd = wbf[(l, which)]
            for src, dst, rows, cw in ((Wt[p + "w_gate"][l], dg, D, DFF // 2), (Wt[p + "w_up"][l], du, D, DFF // 2),
                                       (Wt[p + "w_down"][l], dd, DFF, D)):
                cols = DFF if rows == D else D
                for r0 in range(0, rows, 128):
                    for c0 in range(0, cols, cw):
                        C.S.add("pool", (lambda e, src=src, dst=dst, r0=r0, c0=c0, cw=cw: e.dma_start(
                            out=dst[r0:r0 + 128, c0:c0 + cw], in_=src[r0:r0 + 128, c0:c0 + cw])),
                            writes=[cast_bufs[cast_n[0] % 2]], dma=True)
                        cast_n[0] += 1

        def ffn(l, which, src, dst):
            p = "ffn%d_" % which
            if (l, which) in wbf:
                wg_, wu_, wd_ = wbf[(l, which)]
            else:
                wg_, wu_, wd_ = Wt[p + "w_gate"][l], Wt[p + "w_up"][l], Wt[p + "w_down"][l]
            ffn_phase(C, src, dst, wg_, wu_, wd_, Wt[p + "norm_pre"][l], Wt[p + "norm_post"][l], NTOK)

        ffn(0, 1, x, xa)
        simple_linear(C, "mk0", NSEQ * NMEM, mem, Wt["mem_norm"][0], Wt["w_mem_kv"][0], 2 * MEMW, MKV)
        inproj_a_phase(C, xa, Wt["mix_norm_pre"][0], Wt["ret_w_in"][0], P, pos, cst["c_invf"])
        cast_ffn(0, 2)
        cast_ffn(1, 1)
        retention_phase(C, P, ycat, cst["c_dmaskT"], cst["c_qdec"], cst["c_kdec"])
        memattn_phase(C, P, 4 * MIXW, MKV, ycat)
        outproj_phase(C, ycat, Wt["w_o"][0], Wt["mix_norm_post"][0], xa, xb)
        ffn(0, 2, xb, xa)
        simple_linear(C, "kvs", NTOK, xa, Wt["kv_norm"], Wt["w_kv_shared"], 2 * MIXW, KVs)
        ffn(1, 1, xa, xb)
        simple_linear(C, "mk1", NSEQ * NMEM, mem, Wt["mem_norm"][1], Wt["w_mem_kv"][1], 2 * MEMW, MKV)
        simple_linear(C, "ib", NTOK, xb, Wt["mix_norm_pre"][1], Wt["sb_w_in"][0], MIXW + MEMW, P)
        stickbreak_phase(C, P, KVs, ycat, cst["c_tril_f"], cst["c_tril_b"])
        cast_ffn(1, 2)
        memattn_phase(C, P, MIXW, MKV, ycat)
        outproj_phase(C, ycat, Wt["w_o"][1], Wt["mix_norm_post"][1], xb, xa)
        ffn(1, 2, xa, out)
        C.S.close()
    return nc


_CACHE = {}


def kernel(**inputs):
    if "nc" not in _CACHE:
        _CACHE["nc"] = build_program()
        _CACHE["consts"] = make_consts()
    nc = _CACHE["nc"]
    consts = _CACHE["consts"]
    x = np.ascontiguousarray(inputs["x"], dtype=np.float32)
    mem = np.ascontiguousarray(inputs["mem"], dtype=np.float32)
    pos = np.ascontiguousarray(inputs["positions"], dtype=np.int32)
    shared = {name: np.ascontiguousarray(inputs[name], dtype=np.float32) for name, _ in W_SPECS}
    shared.update(consts)
    in_maps = []
    for c in range(N_CORES):
        m = dict(shared)
        m["x"] = x[c * NSEQ:(c + 1) * NSEQ].reshape(NTOK, D)
        m["mem"] = mem[c * NSEQ:(c + 1) * NSEQ].reshape(NSEQ * NMEM, D)
        m["positions"] = np.ascontiguousarray(pos[c * NSEQ:(c + 1) * NSEQ].reshape(NTOK // 128, 128).T)
        in_maps.append(m)
    res = run_bass_kernel_spmd(nc, in_maps, core_ids=list(range(N_CORES)))
    outs = [np.asarray(r["out"]).reshape(NSEQ, SEQ, D) for r in res.results]
    return np.concatenate(outs, axis=0).astype(np.float32)
```

```python
import numpy as np
import ml_dtypes
from contextlib import ExitStack as ExitStackCompat
import concourse.bass as bass
import concourse.mybir as mybir
from concourse.bass_utils import run_bass_kernel_spmd

F32 = mybir.dt.float32
BF16 = mybir.dt.bfloat16
I32 = mybir.dt.int32
AF = mybir.ActivationFunctionType
ALU = mybir.AluOpType
AX = mybir.AxisListType

D = 2048
DFF = 5632
NFF = DFF // 128
KD = D // 128
SEQ = 2048
NSEQ = 2
NTOK = SEQ * NSEQ
NMEM = 256
HD = 128
NH = 12
NMH = 4
MIXW = NH * HD
MEMW = NMH * HD
EPS = 1e-6
N_CORES = 8

SB_SEQUENTIAL = False
SB_OLDMATH = True
SB_POOL_ENG = "pool"
N_DSEM = 12
CSEM_CAP = 30000


_UID = [0]


def _sbt(nc, name, shape, dt):
    _UID[0] += 1
    return nc.sbuf_tensor("%s_%d" % (name, _UID[0]), shape, dt)


class Buf:
    __slots__ = ("name", "w", "r")

    def __init__(self, name=""):
        self.name = name
        self.w = None
        self.r = {}


class Op:
    __slots__ = ("q", "fn", "deps", "need", "no", "dma")


class Sched:
    QUEUES = ("pe", "act", "dve", "pool", "sp")

    def __init__(self, nc):
        self.nc = nc
        self.ops = {q: [] for q in self.QUEUES}
        self.count = {}
        self.sems = {}
        self.waited = {q: {} for q in self.QUEUES}
        self._semctx = []

    def _sem(self, fam, idx):
        key = (fam, idx)
        if key not in self.sems:
            cm = self.nc.semaphore("s_%s_%s_%d" % (fam[0], "d" if fam[1] else "c", idx))
            h = cm.__enter__()
            self._semctx.append(cm)
            self.sems[key] = h
        return self.sems[key]

    def close(self):
        for cm in reversed(self._semctx):
            cm.__exit__(None, None, None)
        self._semctx = []

    def add(self, q, fn, reads=(), writes=(), dma=False):
        op = Op()
        op.q = q
        op.fn = fn
        op.dma = dma
        op.need = dma
        op.no = None
        deps = []
        for b in reads:
            if b.w is not None:
                deps.append(b.w)
        for b in writes:
            if b.w is not None:
                deps.append(b.w)
            deps.extend(b.r.values())
        fam = (q, dma)
        for b in reads:
            b.r[fam] = op
        for b in writes:
            b.w = op
            b.r = {}
        out = []
        seen = set()
        for d in deps:
            if d is op or id(d) in seen:
                continue
            seen.add(id(d))
            if d.q == "pe" and q == "pe" and not d.dma and not dma:
                continue
            d.need = True
            out.append(d)
        op.deps = out
        self.ops[q].append(op)
        return op

    def _semval(self, d):
        fam = (d.q, d.dma)
        if d.dma:
            idx = (d.no - 1) % N_DSEM
            return self._sem(fam, idx), 16 * ((d.no - 1) // N_DSEM + 1)
        idx = (d.no - 1) // CSEM_CAP
        v = (d.no - 1) % CSEM_CAP + 1
        return self._sem(fam, idx), v

    def emit(self, barrier=True):
        nc = self.nc
        if barrier:
            for q in self.QUEUES:
                for op in reversed(self.ops[q]):
                    if not op.dma:
                        op.need = True
                        break
        for q in self.QUEUES:
            for op in self.ops[q]:
                if op.need and op.no is None:
                    fam = (q, op.dma)
                    self.count[fam] = self.count.get(fam, 0) + 1
                    op.no = self.count[fam]
        finals = []
        if barrier:
            for fam, n in self.count.items():
                if fam[1]:
                    for idx in range(min(n, N_DSEM)):
                        last_i = n - ((n - 1 - idx) % N_DSEM)
                        finals.append((self._sem(fam, idx), 16 * ((last_i - 1) // N_DSEM + 1)))
                else:
                    idx = (n - 1) // CSEM_CAP
                    v = (n - 1) % CSEM_CAP + 1
                    finals.append((self._sem(fam, idx), v))
        for q in self.QUEUES:
            for op in self.ops[q]:
                for d in op.deps:
                    self._semval(d)
                if op.need:
                    self._semval(op)

        def run_queue(q, e):
            waited = self.waited[q]
            for op in self.ops[q]:
                for d in op.deps:
                    sem, val = self._semval(d)
                    if waited.get(sem, 0) < val:
                        e.wait_ge(sem, val)
                        waited[sem] = val
                if op.dma and op.no > N_DSEM:
                    sem, val = self._semval(op)
                    if waited.get(sem, 0) < val - 16:
                        e.wait_ge(sem, val - 16)
                        waited[sem] = val - 16
                ins = op.fn(e)
                if op.need:
                    sem, _ = self._semval(op)
                    ins.then_inc(sem, 16 if op.dma else 1)
            for sem, val in finals:
                if waited.get(sem, 0) < val:
                    e.wait_ge(sem, val)
                    waited[sem] = val

        with nc.Block() as block:
            @block.tensor
            def _(e):
                run_queue("pe", e)

            @block.scalar
            def _(e):
                run_queue("act", e)

            @block.vector
            def _(e):
                run_queue("dve", e)

            @block.gpsimd
            def _(e):
                run_queue("pool", e)

            @block.sync
            def _(e):
                run_queue("sp", e)
        self.ops = {q: [] for q in self.QUEUES}


class Ring:
    def __init__(self, aps, name=""):
        self.aps = list(aps)
        self.bufs = [Buf("%s%d" % (name, i)) for i in range(len(self.aps))]
        self.i = 0

    def next(self):
        k = self.i % len(self.aps)
        self.i += 1
        return self.aps[k], self.bufs[k]


class Ctx:
    pass


def ffn_phase(C, x_in, x_out, wg, wu, wd, g_pre, g_post, ntok, dbg=None):
    nc, S = C.nc, C.S
    T = 512
    NJ = T // 128
    NT = ntok // T
    WC = 256
    NWC = DFF // WC
    QC = 11
    NQ = NFF // QC
    wg_v = wg.rearrange("(k p) c -> p k c", p=128)
    wu_v = wu.rearrange("(k p) c -> p k c", p=128)
    wd_v = wd.rearrange("(c p) n -> p c n", p=128)
    with (
        _sbt(nc, "f_hT", [128, KD, T], BF16) as hT,
        _sbt(nc, "f_aT", [128, NFF, T], BF16) as aT,
        _sbt(nc, "f_wg", [128, 2, KD, WC], BF16) as wgt,
        _sbt(nc, "f_wu", [128, 2, KD, WC], BF16) as wut,
        _sbt(nc, "f_wd", [128, 2, QC, 512], BF16) as wdt,
        _sbt(nc, "f_xs", [128, 4, D], F32) as xs,
        _sbt(nc, "f_xn", [128, 2, D], BF16) as xn,
        _sbt(nc, "f_ys", [128, 3, 512], F32) as ys,
        _sbt(nc, "f_sg", [128, 2, 512], F32) as sg,
        _sbt(nc, "f_gpre", [128, D], F32) as gpre,
        _sbt(nc, "f_gpost", [128, D], F32) as gpost,
        _sbt(nc, "f_junk", [128, D], BF16) as junk,
        _sbt(nc, "f_st", [128, 64], F32) as st,
    ):
        xs_ring = Ring([xs[:, i, :] for i in range(4)], "xs")
        xn_ring = Ring([xn[:, i, :] for i in range(2)], "xn")
        ys_ring = Ring([ys[:, i, :] for i in range(3)], "ys")
        sg_ring = Ring([sg[:, i, :] for i in range(2)], "sg")
        wg_ring = Ring([wgt[:, i] for i in range(2)], "wg")
        wu_ring = Ring([wut[:, i] for i in range(2)], "wu")
        wd_ring = Ring([wdt[:, i] for i in range(2)], "wd")
        b_gpre, b_gpost, b_eps = Buf(), Buf(), Buf()
        b_junk = Buf()
        S.add("sp", lambda e: e.dma_start(out=gpre[:], in_=g_pre.partition_broadcast(128)),
              writes=[b_gpre], dma=True)
        S.add("sp", lambda e: e.dma_start(out=gpost[:], in_=g_post.partition_broadcast(128)),
              writes=[b_gpost], dma=True)
        S.add("dve", lambda e: e.memset(st[:, 0:1], EPS), writes=[b_eps])

        b_hT = [Buf() for _ in range(NJ)]
        b_aT = [Buf() for _ in range(NFF)]
        b_stat = [[Buf() for _ in range(4)] for _ in range(2)]
        def tile_stages(t):
            base = 4 + (t % 2) * 28
            c_sspre = st[:, base:base + 4]
            c_rpre = st[:, base + 4:base + 8]
            c_sspost = st[:, base + 8:base + 24]
            c_rpost = st[:, base + 24:base + 28]
            b_sspre, b_rpre, b_sspost, b_rpost = b_stat[t % 2]
            tok0 = t * T
            b_y = [[Buf() for _ in range(4)] for _ in range(NJ)]

            def s1():
                x_tiles = []
                for j in range(NJ):
                    xa, xb = xs_ring.next()
                    r0 = tok0 + j * 128
                    S.add("sp", (lambda e, xa=xa, r0=r0: e.dma_start(out=xa, in_=x_in[r0:r0 + 128, :])),
                          writes=[xb], dma=True)
                    S.add("act", (lambda e, xa=xa, j=j: e.activation(out=junk[:], in_=xa, func=AF.Square,
                                                                       accum_out=c_sspre[:, j:j + 1])),
                          reads=[xb], writes=[b_sspre])
                    x_tiles.append((xa, xb))
                S.add("act", lambda e: e.activation(out=c_rpre, in_=c_sspre, func=AF.Sqrt,
                                                    scale=1.0 / D, bias=st[:, 0:1]),
                      reads=[b_sspre, b_eps], writes=[b_rpre])
                S.add("dve", lambda e: e.reciprocal(out=c_rpre, in_=c_rpre), writes=[b_rpre])
                for j in range(NJ):
                    xa, xb = x_tiles[j]
                    na, nb = xn_ring.next()
                    S.add("dve", (lambda e, xa=xa, na=na, j=j: e.scalar_tensor_tensor(
                        out=na, in0=xa, scalar=c_rpre[:, j:j + 1], in1=gpre[:], op0=ALU.mult, op1=ALU.mult)),
                        reads=[xb, b_rpre, b_gpre], writes=[nb])
                    for half in range(2):
                        pa, pb = C.psum.next()
                        pv = pa.bitcast(BF16)

                        def tr(e, pv=pv, na=na, half=half):
                            ins = None
                            for i in range(8):
                                k = half * 8 + i
                                ins = e.transpose(out=pv[:, i * 128:(i + 1) * 128],
                                                  in_=na[:, k * 128:(k + 1) * 128], identity=C.ident[:])
                            return ins
                        S.add("pe", tr, reads=[nb, C.b_ident], writes=[pb])
                        eng = "act" if half == 0 else "dve"
                        dst = hT[:, half * 8:(half + 1) * 8, j * 128:(j + 1) * 128]
                        src = pv.rearrange("p (i t) -> p i t", i=8)
                        if eng == "act":
                            S.add("act", (lambda e, dst=dst, src=src: e.activation(out=dst, in_=src, func=AF.Copy)),
                                  reads=[pb], writes=[b_hT[j]])
                        else:
                            S.add("dve", (lambda e, dst=dst, src=src: e.tensor_copy(out=dst, in_=src)),
                                  reads=[pb], writes=[b_hT[j]])
            def s2():
                for wc in range(NWC):
                    ga, gb = wg_ring.next()
                    ua, ub = wu_ring.next()
                    c0 = wc * WC
                    S.add("pool", (lambda e, ga=ga, c0=c0: e.dma_start(out=ga, in_=wg_v[:, :, c0:c0 + WC])),
                          writes=[gb], dma=True)
                    S.add("pool", (lambda e, ua=ua, c0=c0: e.dma_start(out=ua, in_=wu_v[:, :, c0:c0 + WC])),
                          writes=[ub], dma=True)
                    for cl in range(WC // 128):
                        c = wc * (WC // 128) + cl
                        pg, pgb = C.psum.next()
                        pu, pub = C.psum.next()

                        def mm(e, w=ga, cl=cl, ps=pg):
                            ins = None
                            for k in range(KD):
                                ins = e.matmul(ps, lhsT=w[:, k, cl * 128:(cl + 1) * 128], rhs=hT[:, k, :],
                                               start=(k == 0), stop=(k == KD - 1))
                            return ins
                        S.add("pe", mm, reads=[gb] + b_hT, writes=[pgb])

                        def mm2(e, w=ua, cl=cl, ps=pu):
                            ins = None
                            for k in range(KD):
                                ins = e.matmul(ps, lhsT=w[:, k, cl * 128:(cl + 1) * 128], rhs=hT[:, k, :],
                                               start=(k == 0), stop=(k == KD - 1))
                            return ins
                        S.add("pe", mm2, reads=[ub] + b_hT, writes=[pub])
                        sa, sb = sg_ring.next()
                        S.add("act", (lambda e, sa=sa, pg=pg: e.activation(out=sa, in_=pg, func=AF.Silu)),
                              reads=[pgb], writes=[sb])
                        S.add("dve", (lambda e, sa=sa, pu=pu, c=c: e.tensor_tensor(out=aT[:, c, :], in0=pu, in1=sa,
                                                                                   op=ALU.mult)),
                              reads=[pub, sb], writes=[b_aT[c]])
                if dbg is not None and t == 0:
                    S.add("sp", lambda e: e.dma_start(out=dbg["hT"], in_=hT[:]), reads=b_hT, dma=True)
                    S.add("sp", lambda e: e.dma_start(out=dbg["aT"], in_=aT[:]), reads=b_aT, dma=True)
                    S.add("sp", lambda e: e.dma_start(out=dbg["st"], in_=st[:]), reads=[b_rpre], dma=True)
            def s3():
                for n in range(4):
                    accs = [C.psum.next() for _ in range(NJ)]
                    for qi in range(NQ):
                        wa, wb = wd_ring.next()
                        S.add("pool", (lambda e, wa=wa, qi=qi, n=n: e.dma_start(
                            out=wa, in_=wd_v[:, qi * QC:(qi + 1) * QC, n * 512:(n + 1) * 512])),
                            writes=[wb], dma=True)
                        for j in range(NJ):
                            def mm3(e, wa=wa, qi=qi, j=j, ps=accs[j][0]):
                                ins = None
                                for ci in range(QC):
                                    c = qi * QC + ci
                                    ins = e.matmul(ps, lhsT=aT[:, c, j * 128:(j + 1) * 128], rhs=wa[:, ci, :],
                                                   start=(c == 0), stop=(c == NFF - 1))
                                return ins
                            S.add("pe", mm3, reads=[wb] + b_aT[qi * QC:(qi + 1) * QC], writes=[accs[j][1]])
                    for j in range(NJ):
                        ya, yb = ys_ring.next()
                        ps, psb = accs[j]
                        S.add("act", (lambda e, ya=ya, ps=ps: e.activation(out=ya, in_=ps, func=AF.Copy)),
                              reads=[psb], writes=[yb])
                        S.add("act", (lambda e, ps=ps, j=j, n=n: e.activation(
                            out=junk[:, 0:512], in_=ps, func=AF.Square,
                            accum_out=c_sspost[:, j * 4 + n:j * 4 + n + 1])),
                            reads=[psb], writes=[b_sspost])
                        r0 = tok0 + j * 128
                        S.add("sp", (lambda e, ya=ya, r0=r0, n=n: e.dma_start(
                            out=x_out[r0:r0 + 128, n * 512:(n + 1) * 512], in_=ya)),
                            reads=[yb], writes=[b_y[j][n]], dma=True)
            def s4():
                S.add("dve", lambda e: e.tensor_reduce(out=c_rpost, in_=c_sspost.rearrange("p (j n) -> p j n", n=4),
                                                       axis=AX.X, op=ALU.add),
                      reads=[b_sspost], writes=[b_rpost])
                S.add("act", lambda e: e.activation(out=c_rpost, in_=c_rpost, func=AF.Sqrt,
                                                    scale=1.0 / D, bias=st[:, 0:1]),
                      reads=[b_eps], writes=[b_rpost])
                S.add("dve", lambda e: e.reciprocal(out=c_rpost, in_=c_rpost), writes=[b_rpost])
                for j in range(NJ):
                    r0 = tok0 + j * 128
                    ya, yb = xs_ring.next()
                    xa, xb = xs_ring.next()
                    S.add("sp", (lambda e, ya=ya, r0=r0: e.dma_start(out=ya, in_=x_out[r0:r0 + 128, :])),
                          reads=b_y[j], writes=[yb], dma=True)
                    S.add("sp", (lambda e, xa=xa, r0=r0: e.dma_start(out=xa, in_=x_in[r0:r0 + 128, :])),
                          writes=[xb], dma=True)
                    S.add("dve", (lambda e, ya=ya, j=j: e.scalar_tensor_tensor(
                        out=ya, in0=ya, scalar=c_rpost[:, j:j + 1], in1=gpost[:], op0=ALU.mult, op1=ALU.mult)),
                        reads=[b_rpost, b_gpost], writes=[yb])
                    S.add("dve", (lambda e, ya=ya, xa=xa: e.scalar_tensor_tensor(
                        out=ya, in0=ya, scalar=0.5, in1=xa, op0=ALU.mult, op1=ALU.add)),
                        reads=[xb], writes=[yb])
                    S.add("sp", (lambda e, ya=ya, r0=r0: e.dma_start(out=x_out[r0:r0 + 128, :], in_=ya)),
                          reads=[yb], writes=b_y[j], dma=True)

            return s1, s2, s3, s4

        stg = [tile_stages(t) for t in range(NT)]
        stg[0][0]()
        for t in range(NT):
            stg[t][1]()
            if t + 1 < NT:
                stg[t + 1][0]()
            stg[t][2]()
            stg[t][3]()
        S.emit()


def make_ctx(nc, stack, ident_dram):
    C = Ctx()
    C.nc = nc
    C.S = Sched(nc)
    banks = []
    for i in range(8):
        h = stack.enter_context(nc.psum_tensor("psb%d" % i, [128, 512], F32))
        banks.append(h[:])
    C.psum = Ring(banks, "ps")
    C.ident = stack.enter_context(_sbt(nc, "ident_sb", [128, 128], BF16))
    C.b_ident = Buf("ident")
    C.S.add("sp", lambda e: e.dma_start(out=C.ident[:], in_=ident_dram), writes=[C.b_ident], dma=True)
    C.flip = 0
    return C


def evac_copy(C, dst, src, reads, writes, eng=None):
    S = C.S
    if eng is None:
        eng = "act" if (C.flip % 2 == 0) else "dve"
        C.flip += 1
    if eng == "act":
        return S.add("act", (lambda e: e.activation(out=dst, in_=src, func=AF.Copy)), reads=reads, writes=writes)
    return S.add("dve", (lambda e: e.tensor_copy(out=dst, in_=src)), reads=reads, writes=writes)


class NormT:
    def __init__(self, C, stack, pfx, gain_dram):
        nc = C.nc
        self.C = C
        self.xs = stack.enter_context(_sbt(nc, pfx + "_xs", [128, 4, D], F32))
        self.xn = stack.enter_context(_sbt(nc, pfx + "_xn", [128, 2, D], BF16))
        self.g = stack.enter_context(_sbt(nc, pfx + "_g", [128, D], F32))
        self.junk = stack.enter_context(_sbt(nc, pfx + "_junk", [128, D], BF16))
        self.st = stack.enter_context(_sbt(nc, pfx + "_st", [128, 20], F32))
        self.xs_ring = Ring([self.xs[:, i, :] for i in range(4)])
        self.xn_ring = Ring([self.xn[:, i, :] for i in range(2)])
        self.b_g, self.b_eps = Buf(), Buf()
        self.b_stat = [[Buf(), Buf()], [Buf(), Buf()]]
        self.n = 0
        g = self.g
        C.S.add("sp", lambda e: e.dma_start(out=g[:], in_=gain_dram.partition_broadcast(128)),
                writes=[self.b_g], dma=True)
        st = self.st
        C.S.add("dve", lambda e: e.memset(st[:, 0:1], EPS), writes=[self.b_eps])

    def tile(self, x_in, tok0, NJ, hT, b_hT):
        C, S = self.C, self.C.S
        st, junk, gt = self.st, self.junk, self.g
        base = 4 + (self.n % 2) * 8
        b_ss, b_r = self.b_stat[self.n % 2]
        self.n += 1
        c_ss = st[:, base:base + NJ]
        c_r = st[:, base + 4:base + 4 + NJ]
        x_tiles = []
        for j in range(NJ):
            xa, xb = self.xs_ring.next()
            r0 = tok0 + j * 128
            S.add("sp", (lambda e, xa=xa, r0=r0: e.dma_start(out=xa, in_=x_in[r0:r0 + 128, :])),
                  writes=[xb], dma=True)
            S.add("act", (lambda e, xa=xa, j=j: e.activation(out=junk[:], in_=xa, func=AF.Square,
                                                               accum_out=c_ss[:, j:j + 1])),
                  reads=[xb], writes=[b_ss])
            x_tiles.append((xa, xb))
        S.add("act", lambda e: e.activation(out=c_r, in_=c_ss, func=AF.Sqrt, scale=1.0 / D, bias=st[:, 0:1]),
              reads=[b_ss, self.b_eps], writes=[b_r])
        S.add("dve", lambda e: e.reciprocal(out=c_r, in_=c_r), writes=[b_r])
        for j in range(NJ):
            xa, xb = x_tiles[j]
            na, nb = self.xn_ring.next()
            S.add("dve", (lambda e, xa=xa, na=na, j=j: e.scalar_tensor_tensor(
                out=na, in0=xa, scalar=c_r[:, j:j + 1], in1=gt[:], op0=ALU.mult, op1=ALU.mult)),
                reads=[xb, b_r, self.b_g], writes=[nb])
            for half in range(2):
                pa, pb = C.psum.next()
                pv = pa.bitcast(BF16)

                def tr(e, pv=pv, na=na, half=half):
                    ins = None
                    for i in range(8):
                        k = half * 8 + i
                        ins = e.transpose(out=pv[:, i * 128:(i + 1) * 128],
                                          in_=na[:, k * 128:(k + 1) * 128], identity=C.ident[:])
                    return ins
                S.add("pe", tr, reads=[nb, C.b_ident], writes=[pb])
                dst = hT[:, half * 8:(half + 1) * 8, j * 128:(j + 1) * 128]
                src = pv.rearrange("p (i t) -> p i t", i=8)
                evac_copy(C, dst, src, [pb], [b_hT[j]], eng=("act" if half == 0 else "dve"))


def linear_phase(C, stack_outer, pfx, ntok, x_src, W, ncols, epilogue, norm_gain=None, T=512, pre=None, post=None):
    nc, S = C.nc, C.S
    NJ = T // 128
    NT = ntok // T
    NG = (ncols + 511) // 512
    W_v = W.rearrange("(k p) c -> p k c", p=128)
    with ExitStackCompat() as stack:
        hTt = stack.enter_context(_sbt(nc, pfx + "_hT", [128, 2, KD, T], BF16))
        wt = stack.enter_context(_sbt(nc, pfx + "_w", [128, 3, KD, 512], BF16))
        w_ring = Ring([wt[:, i] for i in range(3)])
        hT_bufs = [[Buf() for _ in range(NJ)] for _ in range(2)]
        hT_kbufs = [[Buf() for _ in range(KD)] for _ in range(2)]
        nt = NormT(C, stack, pfx, norm_gain) if norm_gain is not None else None
        if pre is not None:
            pre(stack)
        for t in range(NT):
            tok0 = t * T
            hT = hTt[:, t % 2]
            b_hT = hT_bufs[t % 2]
            if nt is not None:
                nt.tile(x_src, tok0, NJ, hT, b_hT)
            else:
                for k in range(KD):
                    S.add("sp", (lambda e, hT=hT, k=k, tok0=tok0: e.dma_start_transpose(
                        out=hT[:, k, :], in_=x_src[tok0:tok0 + T, k * 128:(k + 1) * 128])),
                        writes=[hT_kbufs[t % 2][k]], dma=True)
            for n in range(NG):
                gs = min(512, ncols - n * 512)
                wa, wb = w_ring.next()
                S.add("pool", (lambda e, wa=wa, n=n, gs=gs: e.dma_start(
                    out=wa[:, :, 0:gs], in_=W_v[:, :, n * 512:n * 512 + gs])), writes=[wb], dma=True)
                for j in range(NJ):
                    pa, pb = C.psum.next()

                    def mm(e, wa=wa, hT=hT, j=j, gs=gs, ps=pa):
                        ins = None
                        for k in range(KD):
                            ins = e.matmul(ps[:, 0:gs], lhsT=hT[:, k, j * 128:(j + 1) * 128], rhs=wa[:, k, 0:gs],
                                           start=(k == 0), stop=(k == KD - 1))
                        return ins
                    S.add("pe", mm, reads=[wb] + ([b_hT[j]] if nt is not None else hT_kbufs[t % 2]), writes=[pb])
                    epilogue(t, j, n, gs, pa, pb, tok0 + j * 128)
                if n == 0 and post is not None and t > 0:
                    post(t - 1, tok0 - T, NJ)
        if post is not None:
            post(NT - 1, (NT - 1) * T, NJ)
        S.emit()


class PlainEpi:
    def __init__(self, C, stack, pfx, dst, silu_groups=(), col0=0):
        self.C = C
        self.dst = dst
        self.silu = set(silu_groups)
        self.col0 = col0
        t = stack.enter_context(_sbt(C.nc, pfx + "_stg", [128, 4, 512], BF16))
        self.ring = Ring([t[:, i, :] for i in range(4)])

    def __call__(self, t, j, n, gs, pa, pb, r0):
        C, S = self.C, self.C.S
        sa, sb = self.ring.next()
        if n in self.silu:
            S.add("act", (lambda e: e.activation(out=sa[:, 0:gs], in_=pa[:, 0:gs], func=AF.Silu)),
                  reads=[pb], writes=[sb])
        else:
            evac_copy(C, sa[:, 0:gs], pa[:, 0:gs], [pb], [sb])
        dst, c0 = self.dst, self.col0 + n * 512
        S.add("sp", (lambda e: e.dma_start(out=dst[r0:r0 + 128, c0:c0 + gs], in_=sa[:, 0:gs])),
              reads=[sb], dma=True)


def simple_linear(C, pfx, ntok, x_src, gain, W, ncols, dst, T=512):
    epi = {}

    def pre(stack):
        epi["e"] = PlainEpi(C, stack, pfx, dst)

    linear_phase(C, None, pfx, ntok, x_src, W, ncols, lambda *a: epi["e"](*a), norm_gain=gain, T=T, pre=pre)


def inproj_a_phase(C, x_src, gain, W, P, pos, invf):
    nc, S = C.nc, C.S
    R = {}
    NSUB = NTOK // 128
    PI = float(np.pi)

    def pre(stack):
        posi = stack.enter_context(_sbt(nc, "ia_posi", [128, NSUB], I32))
        posf = stack.enter_context(_sbt(nc, "ia_posf", [128, NSUB], F32))
        inv = stack.enter_context(_sbt(nc, "ia_inv", [128, 64], F32))
        ang = stack.enter_context(_sbt(nc, "ia_ang", [128, NSUB, 64], F32))
        tmp = stack.enter_context(_sbt(nc, "ia_tmp", [128, NSUB, 64], F32))
        cs = stack.enter_context(_sbt(nc, "ia_cs", [128, 4, NSUB, 64], F32))
        cst = stack.enter_context(_sbt(nc, "ia_c", [128, 2], F32))
        rt = stack.enter_context(_sbt(nc, "ia_rt", [128, 2, 4, 256], F32))
        R["cs"] = cs
        R["rt"] = Ring([rt[:, i] for i in range(2)])
        R["epi"] = PlainEpi(C, stack, "ia", P, silu_groups=(9, 10, 11))
        b_pos, b_inv, b_ang, b_tmp, b_c = Buf(), Buf(), Buf(), Buf(), Buf()
        R["b_cs"] = Buf()
        S.add("sp", lambda e: e.dma_start(out=posi[:], in_=pos), writes=[b_pos], dma=True)
        S.add("sp", lambda e: e.dma_start(out=inv[:], in_=invf.partition_broadcast(128)), writes=[b_inv], dma=True)
        S.add("dve", lambda e: e.tensor_copy(out=posf[:], in_=posi[:]), reads=[b_pos], writes=[b_pos])
        for n in range(NSUB):
            S.add("dve", (lambda e, n=n: e.tensor_scalar(out=ang[:, n, :], in0=inv[:], scalar1=posf[:, n:n + 1],
                                                         scalar2=None, op0=ALU.mult)),
                  reads=[b_pos, b_inv], writes=[b_ang])
        angf = ang[:].rearrange("p n i -> p (n i)")
        tmpf = tmp[:].rearrange("p n i -> p (n i)")
        MAGIC = 12582912.0
        C1 = 6.28125
        C2 = float(2 * np.pi - 6.28125)
        PIC = 3.141592
        S.add("dve", lambda e: e.memset(cst[:, 1:2], PI / 2), writes=[b_c])
        S.add("dve", lambda e: e.tensor_scalar(out=tmpf, in0=angf, scalar1=float(1.0 / (2 * np.pi)), scalar2=MAGIC,
                                               op0=ALU.mult, op1=ALU.add), reads=[b_ang], writes=[b_tmp])
        S.add("dve", lambda e: e.tensor_scalar(out=tmpf, in0=tmpf, scalar1=MAGIC, scalar2=None, op0=ALU.subtract),
              writes=[b_tmp])
        S.add("dve", lambda e: e.scalar_tensor_tensor(out=angf, in0=tmpf, scalar=-C1, in1=angf, op0=ALU.mult,
                                                      op1=ALU.add), reads=[b_tmp], writes=[b_ang])
        S.add("dve", lambda e: e.scalar_tensor_tensor(out=angf, in0=tmpf, scalar=-C2, in1=angf, op0=ALU.mult,
                                                      op1=ALU.add), reads=[b_tmp], writes=[b_ang])
        S.add("dve", lambda e: e.tensor_scalar(out=tmpf, in0=angf, scalar1=PIC, scalar2=None, op0=ALU.is_gt),
              reads=[b_ang], writes=[b_tmp])
        S.add("dve", lambda e: e.scalar_tensor_tensor(out=angf, in0=tmpf, scalar=float(-2 * np.pi), in1=angf,
                                                      op0=ALU.mult, op1=ALU.add), reads=[b_tmp], writes=[b_ang])
        S.add("dve", lambda e: e.tensor_scalar(out=angf, in0=angf, scalar1=-PIC, scalar2=PIC, op0=ALU.max,
                                               op1=ALU.min), writes=[b_ang])
        sinv = cs[:, 1].rearrange("p n i -> p (n i)")
        cosv = cs[:, 0].rearrange("p n i -> p (n i)")
        S.add("act", lambda e: e.activation(out=sinv, in_=angf, func=AF.Sin), reads=[b_ang], writes=[R["b_cs"]])
        S.add("act", lambda e: e.activation(out=tmpf, in_=angf, func=AF.Abs), reads=[b_ang], writes=[b_tmp])
        S.add("act", lambda e: e.activation(out=cosv, in_=tmpf, func=AF.Sin, bias=cst[:, 1:2], scale=-1.0),
              reads=[b_tmp, b_c], writes=[R["b_cs"]])
        for which in (0, 1):
            srcv = cs[:, which].rearrange("p n i -> p (n i)")
            dstv = cs[:, 2 + which].rearrange("p n i -> p (n i)")
            S.add("dve", (lambda e, srcv=srcv, dstv=dstv: e.tensor_scalar(
                out=dstv, in0=srcv, scalar1=float(HD ** -0.5), scalar2=None, op0=ALU.mult)),
                writes=[R["b_cs"]])

    def epilogue(t, j, n, gs, pa, pb, r0):
        if n >= 6:
            return R["epi"](t, j, n, gs, pa, pb, r0)
        nn = r0 // 128
        cs = R["cs"]
        koff = 0 if n < 3 else 2
        cosb = cs[:, koff + 0, nn:nn + 1, :].to_broadcast([128, 4, 64])
        sinb = cs[:, koff + 1, nn:nn + 1, :].to_broadcast([128, 4, 64])
        ps4 = pa.rearrange("p (h two i) -> p h two i", h=4, two=2)
        t1, t2 = ps4[:, :, 0, :], ps4[:, :, 1, :]
        rta, rtb = R["rt"].next()
        tv = [rta[:, i].rearrange("p (h i) -> p h i", h=4) for i in range(4)]
        sa, sb = R["epi"].ring.next()
        so = sa.rearrange("p (h two i) -> p h two i", h=4, two=2)
        bcs = R["b_cs"]
        ba, bb2, bc, bd = Buf(), Buf(), Buf(), Buf()
        S.add("dve", lambda e: e.tensor_tensor(out=tv[0], in0=t1, in1=cosb, op=ALU.mult), reads=[pb, bcs], writes=[rtb, ba])
        S.add("dve", lambda e: e.tensor_tensor(out=tv[1], in0=t2, in1=sinb, op=ALU.mult), reads=[pb, bcs], writes=[bb2])
        S.add("dve", lambda e: e.tensor_tensor(out=tv[2], in0=t2, in1=cosb, op=ALU.mult), reads=[pb, bcs], writes=[bc])
        S.add("dve", lambda e: e.tensor_tensor(out=tv[3], in0=t1, in1=sinb, op=ALU.mult), reads=[pb, bcs], writes=[bd])
        S.add("dve", lambda e: e.tensor_tensor(out=so[:, :, 0, :], in0=tv[0], in1=tv[1], op=ALU.subtract),
              reads=[ba, bb2], writes=[sb])
        S.add("dve", lambda e: e.tensor_tensor(out=so[:, :, 1, :], in0=tv[2], in1=tv[3], op=ALU.add),
              reads=[bc, bd, rtb], writes=[sb])
        c0 = n * 512
        S.add("sp", (lambda e: e.dma_start(out=P[r0:r0 + 128, c0:c0 + 512], in_=sa)), reads=[sb, rtb], dma=True)

    linear_phase(C, None, "ia", NTOK, x_src, W, 4 * MIXW + MEMW, epilogue, norm_gain=gain, pre=pre)


def load_T(C, dstT, src, tok0, ntok, col0, writes):
    S = C.S
    for r in range(ntok // 512):
        S.add("sp", (lambda e, r=r: e.dma_start_transpose(
            out=dstT[:, r * 512:(r + 1) * 512], in_=src[tok0 + r * 512:tok0 + (r + 1) * 512, col0:col0 + 128])),
            writes=[writes[r]], dma=True)


def retention_phase(C, P, ycat, dmaskT_d, qdec_d, kdec_d):
    nc, S = C.nc, C.S
    NC = SEQ // 128
    lg = [float(np.log1p(-2.0 ** (-5.0 - h))) for h in range(NH)]
    cdec = [float(np.exp(np.float32(l) * 128)) for l in lg]
    with ExitStackCompat() as stack:
        T_ = lambda name, shape, dt: stack.enter_context(_sbt(nc, "rt_" + name, shape, dt))
        dmask = T_("dmask", [128, NH, 128], F32)
        qdec = T_("qdec", [128, NH, 128], F32)
        kdec = T_("kdec", [128, NH], F32)
        qT = T_("qT", [128, 2, SEQ], BF16)
        kT = T_("kT", [128, 2, SEQ], BF16)
        qTd = T_("qTd", [128, 2, SEQ], BF16)
        ktok = T_("ktok", [128, 2, NC, 128], BF16)
        vtok = T_("vtok", [128, 2, NC, 128], BF16)
        gtok = T_("gtok", [128, 2, NC, 128], BF16)
        osb = T_("osb", [128, 2, NC, 128], F32)
        yo = T_("yo", [128, 2, NC, 128], BF16)
        state = T_("state", [128, 2, 128], F32)
        stbf = T_("stbf", [128, 2, 128], BF16)
        stm = T_("stm", [128, 3, 128], BF16)
        junk = T_("junk", [128, 128], BF16)
        st = T_("st", [128, 2, 5, NC], F32)
        cst = T_("cst", [128, 1], F32)
        b_const = Buf()
        S.add("sp", lambda e: e.dma_start(out=dmask[:], in_=dmaskT_d), writes=[b_const], dma=True)
        S.add("sp", lambda e: e.dma_start(out=qdec[:], in_=qdec_d), writes=[b_const], dma=True)
        S.add("sp", lambda e: e.dma_start(out=kdec[:], in_=kdec_d), writes=[b_const], dma=True)
        b_eps = Buf()
        S.add("dve", lambda e: e.memset(cst[:, 0:1], EPS), writes=[b_eps])
        b_ycst = Buf()
        stm_ring = Ring([stm[:, i, :] for i in range(3)])
        stbf_ring = Ring([stbf[:, i, :] for i in range(2)])
        slot_b = [{k: Buf() for k in ("qT0", "qT1", "qT2", "qT3", "kT0", "kT1", "kT2", "kT3", "qTd", "ktok",
                                      "vtok", "gtok", "osb", "yo", "state", "sum", "sq", "stat")} for _ in range(2)]
        it = 0
        for s in range(NSEQ):
            for h in range(NH):
                sl = it % 2
                it += 1
                B = slot_b[sl]
                tok0 = s * SEQ
                bq = [B["qT%d" % r] for r in range(4)]
                bk = [B["kT%d" % r] for r in range(4)]
                load_T(C, qT[:, sl, :], P, tok0, SEQ, h * HD, bq)
                load_T(C, kT[:, sl, :], P, tok0, SEQ, MIXW + h * HD, bk)
                for name, col, tl in (("ktok", MIXW + h * HD, ktok), ("vtok", 2 * MIXW + h * HD, vtok),
                                      ("gtok", 3 * MIXW + h * HD, gtok)):
                    S.add("sp", (lambda e, tl=tl, col=col, sl=sl, tok0=tok0: e.dma_start(
                        out=tl[:, sl], in_=P[tok0:tok0 + SEQ, col:col + HD].rearrange("(n p) d -> p n d", p=128))),
                        writes=[B[name]], dma=True)
                S.add("dve", (lambda e, sl=sl, h=h: e.tensor_tensor(
                    out=qTd[:, sl, :].rearrange("p (n c) -> p n c", c=128),
                    in0=qT[:, sl, :].rearrange("p (n c) -> p n c", c=128),
                    in1=qdec[:, h:h + 1, :].to_broadcast([128, NC, 128]), op=ALU.mult)),
                    reads=bq + [b_const], writes=[B["qTd"]])
                S.add("act", (lambda e, sl=sl, h=h: e.activation(
                    out=ktok[:, sl].rearrange("p n d -> p (n d)"), in_=ktok[:, sl].rearrange("p n d -> p (n d)"),
                    func=AF.Copy, scale=kdec[:, h:h + 1])),
                    reads=[b_const], writes=[B["ktok"]])
                sbf_prev = None
                for n in range(NC):
                    csl = slice(n * 128, (n + 1) * 128)
                    r = n // 4
                    p1, p1b = C.psum.next()
                    S.add("pe", (lambda e, p1=p1, sl=sl, csl=csl: e.matmul(
                        p1[:, 0:128], lhsT=kT[:, sl, csl], rhs=qT[:, sl, csl], start=True, stop=True)),
                        reads=[bk[r], bq[r]], writes=[p1b])
                    ma, mb = stm_ring.next()
                    S.add("dve", (lambda e, ma=ma, p1=p1, h=h: e.tensor_tensor(
                        out=ma, in0=p1[:, 0:128], in1=dmask[:, h, :], op=ALU.mult)),
                        reads=[p1b, b_const], writes=[mb])
                    p2, p2b = C.psum.next()

                    def mm_o(e, p2=p2, ma=ma, sl=sl, n=n, csl=csl, sbf=sbf_prev):
                        ins = e.matmul(p2[:, 0:128], lhsT=ma, rhs=vtok[:, sl, n, :], start=True, stop=(n == 0))
                        if n > 0:
                            ins = e.matmul(p2[:, 0:128], lhsT=qTd[:, sl, csl], rhs=sbf[0], start=False, stop=True)
                        return ins
                    rd = [mb, B["vtok"], B["qTd"]] + ([sbf_prev[1]] if n > 0 else [])
                    S.add("pe", mm_o, reads=rd, writes=[p2b])
                    S.add("act", (lambda e, p2=p2, sl=sl, n=n: e.activation(
                        out=osb[:, sl, n, :], in_=p2[:, 0:128], func=AF.Copy, accum_out=st[:, sl, 0, n:n + 1])),
                        reads=[p2b], writes=[B["osb"], B["sum"]])
                    S.add("act", (lambda e, p2=p2, sl=sl, n=n: e.activation(
                        out=junk[:], in_=p2[:, 0:128], func=AF.Square, accum_out=st[:, sl, 1, n:n + 1])),
                        reads=[p2b], writes=[B["sq"]])
                    if n < NC - 1:
                        p3, p3b = C.psum.next()
                        S.add("pe", (lambda e, p3=p3, sl=sl, n=n: e.matmul(
                            p3[:, 0:128], lhsT=ktok[:, sl, n, :], rhs=vtok[:, sl, n, :], start=True, stop=True)),
                            reads=[B["ktok"], B["vtok"]], writes=[p3b])
                        if n == 0:
                            S.add("dve", (lambda e, p3=p3, sl=sl: e.tensor_copy(out=state[:, sl, :], in_=p3[:, 0:128])),
                                  reads=[p3b], writes=[B["state"]])
                        else:
                            S.add("dve", (lambda e, p3=p3, sl=sl, h=h: e.scalar_tensor_tensor(
                                out=state[:, sl, :], in0=state[:, sl, :], scalar=cdec[h], in1=p3[:, 0:128],
                                op0=ALU.mult, op1=ALU.add)),
                                reads=[p3b], writes=[B["state"]])
                        sa, sb = stbf_ring.next()
                        S.add("act", (lambda e, sa=sa, sl=sl: e.activation(out=sa, in_=state[:, sl, :], func=AF.Copy)),
                              reads=[B["state"]], writes=[sb])
                        sbf_prev = (sa, sb)
                c_sum, c_sq = st[:, sl, 0, :], st[:, sl, 1, :]
                c_mean, c_var, c_rstd = st[:, sl, 2, :], st[:, sl, 3, :], st[:, sl, 4, :]
                bs = B["stat"]
                S.add("dve", (lambda e, c_mean=c_mean, c_sum=c_sum: e.tensor_scalar(
                    out=c_mean, in0=c_sum, scalar1=1.0 / HD, scalar2=None, op0=ALU.mult)),
                    reads=[B["sum"]], writes=[bs])
                S.add("dve", (lambda e, c_mean=c_mean, c_rstd=c_rstd: e.tensor_tensor(
                    out=c_rstd, in0=c_mean, in1=c_mean, op=ALU.mult)), writes=[bs])
                S.add("dve", (lambda e, c_var=c_var, c_sq=c_sq, c_rstd=c_rstd: e.scalar_tensor_tensor(
                    out=c_var, in0=c_sq, scalar=1.0 / HD, in1=c_rstd, op0=ALU.mult, op1=ALU.subtract)),
                    reads=[B["sq"]], writes=[bs])
                S.add("act", (lambda e, c_var=c_var: e.activation(out=c_var, in_=c_var, func=AF.Sqrt,
                                                                  bias=cst[:, 0:1], scale=1.0)),
                      reads=[b_eps], writes=[bs])
                S.add("dve", (lambda e, c_var=c_var, c_rstd=c_rstd: e.reciprocal(out=c_rstd, in_=c_var)), writes=[bs])
                o3 = osb[:, sl]
                S.add("dve", (lambda e, o3=o3, c_mean=c_mean: e.tensor_tensor(
                    out=o3, in0=o3, in1=c_mean.unsqueeze(2).to_broadcast([128, NC, 128]), op=ALU.subtract)),
                    reads=[bs], writes=[B["osb"]])
                S.add("dve", (lambda e, o3=o3, c_rstd=c_rstd: e.tensor_tensor(
                    out=o3, in0=o3, in1=c_rstd.unsqueeze(2).to_broadcast([128, NC, 128]), op=ALU.mult)),
                    reads=[bs], writes=[B["osb"]])
                S.add("dve", (lambda e, o3=o3, sl=sl: e.tensor_tensor(
                    out=yo[:, sl], in0=o3, in1=gtok[:, sl], op=ALU.mult)),
                    reads=[B["osb"], B["gtok"]], writes=[B["yo"]])
                S.add("sp", (lambda e, sl=sl, tok0=tok0, h=h: e.dma_start(
                    out=ycat[tok0:tok0 + SEQ, h * HD:(h + 1) * HD].rearrange("(n p) d -> p n d", p=128),
                    in_=yo[:, sl])), reads=[B["yo"]], writes=[b_ycst], dma=True)
        S.emit()


def memattn_phase(C, Q, qcol0, MKV, ycat):
    nc, S = C.nc, C.S
    NTL = SEQ // 128
    scale = float(HD ** -0.5)
    with ExitStackCompat() as stack:
        T_ = lambda name, shape, dt: stack.enter_context(_sbt(nc, "ma_" + name, shape, dt))
        mkT = T_("mkT", [128, NMH, NMEM], BF16)
        mv = T_("mv", [128, 2, MEMW], BF16)
        qmT = T_("qmT", [128, NMH, SEQ], BF16)
        p_sb = T_("p", [128, 3, NMEM], BF16)
        pT = T_("pT", [128, 3, 2, 128], BF16)
        yo = T_("yo", [128, 3, MEMW], BF16)
        st = T_("st", [128, 4, 4], F32)
        p_ring = Ring([p_sb[:, i, :] for i in range(3)])
        pT_ring = Ring([pT[:, i] for i in range(3)])
        yo_ring = Ring([yo[:, i, :] for i in range(3)])
        st_ring = Ring([st[:, i, :] for i in range(4)])
        b_mk, b_mv = Buf(), Buf()
        b_q = [[Buf() for _ in range(4)] for _ in range(NMH)]
        for s in range(NSEQ):
            tok0 = s * SEQ
            for hm in range(NMH):
                S.add("sp", (lambda e, hm=hm, s=s: e.dma_start_transpose(
                    out=mkT[:, hm, :], in_=MKV[s * NMEM:(s + 1) * NMEM, hm * HD:(hm + 1) * HD])),
                    writes=[b_mk], dma=True)
                load_T(C, qmT[:, hm, :], Q, tok0, SEQ, qcol0 + hm * HD, b_q[hm])
            S.add("sp", (lambda e, s=s: e.dma_start(
                out=mv[:], in_=MKV[s * NMEM:(s + 1) * NMEM, MEMW:2 * MEMW].rearrange("(c p) d -> p c d", p=128))),
                writes=[b_mv], dma=True)
            for j in range(NTL):
                ya, yb = yo_ring.next()
                for hm in range(NMH):
                    ps, psb = C.psum.next()
                    S.add("pe", (lambda e, ps=ps, hm=hm, j=j: e.matmul(
                        ps[:, 0:NMEM], lhsT=qmT[:, hm, j * 128:(j + 1) * 128], rhs=mkT[:, hm, :],
                        start=True, stop=True)), reads=[b_q[hm][j // 4], b_mk], writes=[psb])
                    sa, sb = st_ring.next()
                    S.add("dve", (lambda e, ps=ps, sa=sa: e.tensor_reduce(
                        out=sa[:, 0:1], in_=ps[:, 0:NMEM], axis=AX.X, op=ALU.max)), reads=[psb], writes=[sb])
                    S.add("dve", (lambda e, sa=sa: e.tensor_scalar(
                        out=sa[:, 1:2], in0=sa[:, 0:1], scalar1=-scale, scalar2=None, op0=ALU.mult)), writes=[sb])
                    pa, pb = p_ring.next()
                    S.add("act", (lambda e, ps=ps, sa=sa, pa=pa: e.activation(
                        out=pa, in_=ps[:, 0:NMEM], func=AF.Exp, bias=sa[:, 1:2], scale=scale,
                        accum_out=sa[:, 2:3])), reads=[psb], writes=[pb, sb])
                    S.add("dve", (lambda e, sa=sa: e.reciprocal(out=sa[:, 3:4], in_=sa[:, 2:3])), writes=[sb])
                    pt, ptb = C.psum.next()
                    ptv = pt.bitcast(BF16)

                    def tr(e, ptv=ptv, pa=pa):
                        ins = None
                        for c in range(2):
                            ins = e.transpose(out=ptv[:, c * 128:(c + 1) * 128], in_=pa[:, c * 128:(c + 1) * 128],
                                              identity=C.ident[:])
                        return ins
                    S.add("pe", tr, reads=[pb, C.b_ident], writes=[ptb])
                    ta, tb = pT_ring.next()
                    evac_copy(C, ta, ptv[:, 0:256].rearrange("p (c t) -> p c t", c=2), [ptb], [tb])
                    po, pob = C.psum.next()

                    def mm(e, po=po, ta=ta, hm=hm):
                        ins = None
                        for c in range(2):
                            ins = e.matmul(po[:, 0:128], lhsT=ta[:, c, :], rhs=mv[:, c, hm * HD:(hm + 1) * HD],
                                           start=(c == 0), stop=(c == 1))
                        return ins
                    S.add("pe", mm, reads=[tb, b_mv], writes=[pob])
                    S.add("act", (lambda e, po=po, ya=ya, hm=hm, sa=sa: e.activation(
                        out=ya[:, hm * HD:(hm + 1) * HD], in_=po[:, 0:128], func=AF.Copy, scale=sa[:, 3:4])),
                        reads=[pob, sb], writes=[yb])
                r0 = tok0 + j * 128
                S.add("sp", (lambda e, ya=ya, r0=r0: e.dma_start(out=ycat[r0:r0 + 128, MIXW:MIXW + MEMW], in_=ya)),
                      reads=[yb], dma=True)
        S.emit()


class PostNorm:
    def __init__(self, C, stack, pfx, gain_dram, x_in, x_out, wres):
        nc = C.nc
        self.C, self.x_in, self.x_out, self.wres = C, x_in, x_out, wres
        self.ys = stack.enter_context(_sbt(nc, pfx + "_ys", [128, 3, 512], F32))
        self.xy = stack.enter_context(_sbt(nc, pfx + "_xy", [128, 4, D], F32))
        self.g = stack.enter_context(_sbt(nc, pfx + "_gp", [128, D], F32))
        self.junk = stack.enter_context(_sbt(nc, pfx + "_pj", [128, 512], BF16))
        self.st = stack.enter_context(_sbt(nc, pfx + "_pst", [128, 2, 24], F32))
        self.cst = stack.enter_context(_sbt(nc, pfx + "_pc", [128, 1], F32))
        self.ys_ring = Ring([self.ys[:, i, :] for i in range(3)])
        self.xy_ring = Ring([self.xy[:, i, :] for i in range(4)])
        self.b_g, self.b_eps = Buf(), Buf()
        self.b_st = [[Buf(), Buf()], [Buf(), Buf()]]
        self.b_y = {}
        g, cst = self.g, self.cst
        C.S.add("sp", lambda e: e.dma_start(out=g[:], in_=gain_dram.partition_broadcast(128)),
                writes=[self.b_g], dma=True)
        C.S.add("dve", lambda e: e.memset(cst[:, 0:1], EPS), writes=[self.b_eps])

    def epilogue(self, t, j, n, gs, pa, pb, r0):
        C, S = self.C, self.C.S
        ya, yb = self.ys_ring.next()
        st = self.st
        b_ss = self.b_st[t % 2][0]
        x_out = self.x_out
        junk = self.junk
        S.add("act", (lambda e: e.activation(out=ya, in_=pa, func=AF.Copy)), reads=[pb], writes=[yb])
        S.add("act", (lambda e: e.activation(out=junk[:], in_=pa, func=AF.Square,
                                             accum_out=st[:, t % 2, j * 4 + n:j * 4 + n + 1])),
              reads=[pb], writes=[b_ss])
        by = Buf()
        self.b_y.setdefault((t, j), []).append(by)
        S.add("sp", (lambda e: e.dma_start(out=x_out[r0:r0 + 128, n * 512:(n + 1) * 512], in_=ya)),
              reads=[yb], writes=[by], dma=True)

    def finalize(self, t, tok0, NJ):
        C, S = self.C, self.C.S
        st, gt, cst = self.st, self.g, self.cst
        b_ss, b_r = self.b_st[t % 2]
        c_ss = st[:, t % 2, 0:16]
        c_r = st[:, t % 2, 16:16 + NJ]
        x_in, x_out, wres = self.x_in, self.x_out, self.wres
        S.add("dve", lambda e: e.tensor_reduce(out=c_r, in_=c_ss[:, 0:NJ * 4].rearrange("p (j n) -> p j n", n=4),
                                               axis=AX.X, op=ALU.add), reads=[b_ss], writes=[b_r])
        S.add("act", lambda e: e.activation(out=c_r, in_=c_r, func=AF.Sqrt, scale=1.0 / D, bias=cst[:, 0:1]),
              reads=[self.b_eps], writes=[b_r])
        S.add("dve", lambda e: e.reciprocal(out=c_r, in_=c_r), writes=[b_r])
        for j in range(NJ):
            r0 = tok0 + j * 128
            ya, yb = self.xy_ring.next()
            xa, xb = self.xy_ring.next()
            bys = self.b_y.pop((t, j))
            S.add("sp", (lambda e, ya=ya, r0=r0: e.dma_start(out=ya, in_=x_out[r0:r0 + 128, :])),
                  reads=bys, writes=[yb], dma=True)
            S.add("sp", (lambda e, xa=xa, r0=r0: e.dma_start(out=xa, in_=x_in[r0:r0 + 128, :])),
                  writes=[xb], dma=True)
            S.add("dve", (lambda e, ya=ya, j=j: e.scalar_tensor_tensor(
                out=ya, in0=ya, scalar=c_r[:, j:j + 1], in1=gt[:], op0=ALU.mult, op1=ALU.mult)),
                reads=[b_r, self.b_g], writes=[yb])
            S.add("dve", (lambda e, ya=ya, xa=xa: e.scalar_tensor_tensor(
                out=ya, in0=ya, scalar=wres, in1=xa, op0=ALU.mult, op1=ALU.add)),
                reads=[xb], writes=[yb])
            S.add("sp", (lambda e, ya=ya, r0=r0: e.dma_start(out=x_out[r0:r0 + 128, :], in_=ya)),
                  reads=[yb], writes=bys, dma=True)


def outproj_phase(C, ycat, W, gain, x_in, x_out):
    R = {}

    def pre(stack):
        R["pn"] = PostNorm(C, stack, "op", gain, x_in, x_out, 1.0)

    def epilogue(t, j, n, gs, pa, pb, r0):
        R["pn"].epilogue(t, j, n, gs, pa, pb, r0)

    def post(t, tok0, NJ):
        R["pn"].finalize(t, tok0, NJ)

    linear_phase(C, None, "op", NTOK, ycat, W, D, epilogue, norm_gain=None, pre=pre, post=post)


def stickbreak_phase(C, Q, KV, ycat, tril_f_d, tril_b_d):
    nc, S = C.nc, C.S
    NB = SEQ // 128
    scale = float(HD ** -0.5)
    with ExitStackCompat() as stack:
        T_ = lambda name, shape, dt: stack.enter_context(_sbt(nc, "sb_" + name, shape, dt))
        qT = T_("qT", [128, 4, SEQ], BF16)
        kT = T_("kT", [128, 4, SEQ], BF16)
        vtok = T_("vtok", [128, 4, NB, 128], BF16)
        yo = T_("yo", [128, 4, NB, 128], BF16)
        Et = T_("E", [128, 2, SEQ], F32)
        Lt = T_("L", [128, 2, SEQ], F32)
        Ct = T_("C", [128, 2, SEQ], F32)
        At = T_("A", [128, 2, SEQ], BF16)
        ATt = T_("AT", [128, 2, NB, 128], BF16)
        trf = T_("trf", [128, 128], F32)
        trb = T_("trb", [128, 128], BF16)
        ones = T_("ones", [128, 1], F32)
        b_const = Buf()
        S.add("sp", lambda e: e.dma_start(out=trf[:], in_=tril_f_d), writes=[b_const], dma=True)
        S.add("sp", lambda e: e.dma_start(out=trb[:], in_=tril_b_d), writes=[b_const], dma=True)
        S.add("dve", lambda e: e.memset(ones[:], 1.0), writes=[b_const])
        b_ycst = Buf()
        hb = [{k: Buf() for k in ("q0", "q1", "q2", "q3", "k0", "k1", "k2", "k3", "v", "yo")} for _ in range(4)]
        wb = [{k: Buf() for k in ("E", "L", "C", "A", "AT")} for _ in range(2)]

        def block_stages(st, sl, i):
            B, W_ = hb[sl], wb[st]
            bq = [B["q%d" % r] for r in range(4)]
            bk = [B["k%d" % r] for r in range(4)]
            L = (i + 1) * 128
            E, Lb, Cb, A, AT = Et[:, st, :], Lt[:, st, :], Ct[:, st, :], At[:, st, :], ATt[:, st]
            nbk = (L + 511) // 512
            zb = []

            def s0():
                for c in range(nbk):
                    zb.append(C.psum.next())
                for c in range(nbk):
                    w = min(512, L - c * 512)
                    S.add("pe", (lambda e, c=c, w=w: e.matmul(
                        zb[c][0][:, 0:w], lhsT=qT[:, sl, i * 128:(i + 1) * 128], rhs=kT[:, sl, c * 512:c * 512 + w],
                        start=True, stop=True)), reads=[bq[i // 4], bk[c]], writes=[zb[c][1]])

            def s1():
                for c in range(nbk):
                    w = min(512, L - c * 512)
                    S.add("act", (lambda e, c=c, w=w: e.activation(
                        out=E[:, c * 512:c * 512 + w], in_=zb[c][0][:, 0:w], func=AF.Exp, scale=-scale)),
                        reads=[zb[c][1]], writes=[W_["E"]])
                if SB_OLDMATH:
                    return
                for c in range(nbk):
                    w = min(512, L - c * 512)
                    S.add("dve", (lambda e, c=c, w=w: e.tensor_scalar(
                        out=Lb[:, c * 512:c * 512 + w], in0=zb[c][0][:, 0:w], scalar1=-scale, scalar2=None,
                        op0=ALU.mult)), reads=[zb[c][1]], writes=[W_["L"]])

            def s2():
                S.add("act", (lambda e: e.activation(out=E[:, 0:L], in_=E[:, 0:L], func=AF.Ln,
                                                     bias=ones[:, 0:1], scale=1.0)),
                      reads=[b_const], writes=[W_["E"]])

            def s3():
                if SB_OLDMATH:
                    for c in range(nbk):
                        w = min(512, L - c * 512)
                        S.add("dve", (lambda e, c=c, w=w: e.scalar_tensor_tensor(
                            out=Lb[:, c * 512:c * 512 + w], in0=zb[c][0][:, 0:w], scalar=-scale,
                            in1=E[:, c * 512:c * 512 + w], op0=ALU.mult, op1=ALU.subtract)),
                            reads=[zb[c][1], W_["E"]], writes=[W_["L"]])
                else:
                    S.add(SB_POOL_ENG, (lambda e: e.tensor_tensor(out=Lb[:, 0:L], in0=Lb[:, 0:L], in1=E[:, 0:L],
                                                                  op=ALU.subtract)),
                          reads=[W_["E"]], writes=[W_["L"]])
                S.add(SB_POOL_ENG, (lambda e: e.tensor_tensor(out=Lb[:, L - 128:L], in0=Lb[:, L - 128:L], in1=trf[:],
                                                         op=ALU.mult)),
                      reads=[b_const], writes=[W_["L"]])

            def s4():
                S.add("dve", (lambda e: e.tensor_tensor_scan(
                    out=Cb[:, 0:L], data0=ones[:, 0:1].to_broadcast([128, L]), data1=Lb[:, 0:L], initial=0.0,
                    op0=ALU.mult, op1=ALU.add)), reads=[W_["L"], b_const], writes=[W_["C"]])

            def s5():
                S.add(SB_POOL_ENG, (lambda e: e.tensor_tensor(out=Lb[:, 0:L], in0=Cb[:, 0:L], in1=E[:, 0:L], op=ALU.add)),
                      reads=[W_["C"], W_["E"]], writes=[W_["L"]])

            def s6():
                S.add("act", (lambda e: e.activation(out=A[:, 0:L], in_=Lb[:, 0:L], func=AF.Exp,
                                                     bias=Cb[:, L - 1:L], scale=-1.0)),
                      reads=[W_["L"], W_["C"]], writes=[W_["A"]])

            def s7():
                S.add(SB_POOL_ENG, (lambda e: e.tensor_tensor(out=A[:, L - 128:L], in0=A[:, L - 128:L], in1=trb[:],
                                                         op=ALU.mult)),
                      reads=[b_const], writes=[W_["A"]])

            def s8():
                ntb = (i + 1 + 7) // 8
                for tb_ in range(ntb):
                    nblk = min(8, i + 1 - tb_ * 8)
                    pt, ptb = C.psum.next()
                    ptv = pt.bitcast(BF16)

                    def tr(e, ptv=ptv, tb_=tb_, nblk=nblk):
                        ins = None
                        for bb in range(nblk):
                            blk = tb_ * 8 + bb
                            ins = e.transpose(out=ptv[:, bb * 128:(bb + 1) * 128],
                                              in_=A[:, blk * 128:(blk + 1) * 128], identity=C.ident[:])
                        return ins
                    S.add("pe", tr, reads=[W_["A"], C.b_ident], writes=[ptb])
                    evac_copy(C, AT[:, tb_ * 8:tb_ * 8 + nblk, :],
                              ptv[:, 0:nblk * 128].rearrange("p (b t) -> p b t", b=nblk), [ptb], [W_["AT"]],
                              eng="dve")

            def s9():
                po, pob = C.psum.next()

                def mm(e):
                    ins = None
                    for bb in range(i + 1):
                        ins = e.matmul(po[:, 0:128], lhsT=AT[:, bb, :], rhs=vtok[:, sl, bb, :],
                                       start=(bb == 0), stop=(bb == i))
                    return ins
                S.add("pe", mm, reads=[W_["AT"], B["v"]], writes=[pob])
                evac_copy(C, yo[:, sl, i, :], po[:, 0:128], [pob], [B["yo"]], eng="dve")

            return [s0, s1, s2, s3, s4, s5, s6, s7, s8, s9]

        pair = 0
        for s in range(NSEQ):
            tok0 = s * SEQ
            for hp in range(NH // 2):
                slots = [(pair % 2) * 2 + st for st in range(2)]
                pair += 1
                for st in range(2):
                    h = hp * 2 + st
                    sl = slots[st]
                    B = hb[sl]
                    load_T(C, qT[:, sl, :], Q, tok0, SEQ, h * HD, [B["q%d" % r] for r in range(4)])
                    load_T(C, kT[:, sl, :], KV, tok0, SEQ, h * HD, [B["k%d" % r] for r in range(4)])
                    S.add("sp", (lambda e, sl=sl, tok0=tok0, h=h: e.dma_start(
                        out=vtok[:, sl], in_=KV[tok0:tok0 + SEQ, MIXW + h * HD:MIXW + (h + 1) * HD].rearrange(
                            "(n p) d -> p n d", p=128))), writes=[B["v"]], dma=True)
                for i in range(NB):
                    a = block_stages(0, slots[0], i)
                    b = block_stages(1, slots[1], i)
                    order = [a[0], a[1], a[2], b[0], b[1], b[2], a[3], a[4], a[5], b[3], b[4], b[5],
                             a[6], a[7], b[6], b[7], a[8], a[9], b[8], b[9]]
                    if SB_SEQUENTIAL:
                        order = a + b
                    for f in order:
                        f()
                for st in range(2):
                    h = hp * 2 + st
                    sl = slots[st]
                    S.add("sp", (lambda e, sl=sl, tok0=tok0, h=h: e.dma_start(
                        out=ycat[tok0:tok0 + SEQ, h * HD:(h + 1) * HD].rearrange("(n p) d -> p n d", p=128),
                        in_=yo[:, sl])), reads=[hb[sl]["yo"]], writes=[b_ycst], dma=True)
        S.emit()


W_SPECS = [
    ("ffn1_norm_pre", [2, D]), ("ffn1_norm_post", [2, D]),
    ("ffn1_w_gate", [2, D, DFF]), ("ffn1_w_up", [2, D, DFF]), ("ffn1_w_down", [2, DFF, D]),
    ("mix_norm_pre", [2, D]), ("mix_norm_post", [2, D]), ("mem_norm", [2, D]),
    ("w_mem_kv", [2, D, 2 * MEMW]), ("w_o", [2, D, D]), ("ret_w_in", [1, D, 4 * MIXW + MEMW]),
    ("kv_norm", [D]), ("w_kv_shared", [D, 2 * MIXW]), ("sb_w_in", [1, D, MIXW + MEMW]),
    ("ffn2_norm_pre", [2, D]), ("ffn2_norm_post", [2, D]),
    ("ffn2_w_gate", [2, D, DFF]), ("ffn2_w_up", [2, D, DFF]), ("ffn2_w_down", [2, DFF, D]),
]


def make_consts():
    h = np.arange(NH, dtype=np.float32)
    lg = np.log1p(-np.exp2(-5.0 - h)).astype(np.float32)
    idx = np.arange(128, dtype=np.float32)
    diff = idx[None, :] - idx[:, None]
    dm = np.where(diff >= 0, np.exp(lg[:, None, None] * np.maximum(diff, 0.0)[None]), 0.0).astype(np.float32)
    dmaskT = np.ascontiguousarray(dm.transpose(1, 0, 2))
    qd = np.exp(lg[:, None] * (idx + 1.0)[None, :]).astype(np.float32)
    qdec = np.ascontiguousarray(np.broadcast_to(qd[None], (128, NH, 128))).astype(np.float32)
    kdec = np.ascontiguousarray(np.exp(lg[None, :] * (127.0 - idx)[:, None]).astype(np.float32))
    invf = (10000.0 ** (-np.arange(0, HD, 2, dtype=np.float32) / HD)).astype(np.float32)
    tril = (idx[None, :] < idx[:, None]).astype(np.float32)
    return {
        "c_ident": np.eye(128, dtype=np.float32).astype(ml_dtypes.bfloat16),
        "c_invf": invf, "c_dmaskT": dmaskT, "c_qdec": qdec, "c_kdec": kdec,
        "c_tril_f": tril, "c_tril_b": tril.astype(ml_dtypes.bfloat16),
    }


def build_program(phases=None):
    nc = bass.Bass("TRN2", target_bir_lowering=False)
    IN = lambda name, shape, dt=F32: nc.dram_tensor(name, shape, dt, kind="ExternalInput").ap()
    SCR = lambda name, shape, dt: nc.dram_tensor(name, shape, dt, kind="Internal").ap()
    x = IN("x", [NTOK, D])
    mem = IN("mem", [NSEQ * NMEM, D])
    pos = IN("positions", [128, NTOK // 128], I32)
    Wt = {name: IN(name, shape) for name, shape in W_SPECS}
    cst = {
        "c_ident": IN("c_ident", [128, 128], BF16), "c_invf": IN("c_invf", [64]),
        "c_dmaskT": IN("c_dmaskT", [128, NH, 128]), "c_qdec": IN("c_qdec", [128, NH, 128]),
        "c_kdec": IN("c_kdec", [128, NH]), "c_tril_f": IN("c_tril_f", [128, 128]),
        "c_tril_b": IN("c_tril_b", [128, 128], BF16),
    }
    out = nc.dram_tensor("out", [NTOK, D], F32, kind="ExternalOutput").ap()
    xa = SCR("s_xa", [NTOK, D], F32)
    xb = SCR("s_xb", [NTOK, D], F32)
    P = SCR("s_P", [NTOK, 4 * MIXW + MEMW], BF16)
    KVs = SCR("s_KV", [NTOK, 2 * MIXW], BF16)
    ycat = SCR("s_ycat", [NTOK, D], BF16)
    MKV = SCR("s_MKV", [NSEQ * NMEM, 2 * MEMW], BF16)
    wbf = {}
    for key in ((0, 2), (1, 1), (1, 2)):
        wbf[key] = (SCR("s_wg%d%d" % key, [D, DFF], BF16), SCR("s_wu%d%d" % key, [D, DFF], BF16),
                    SCR("s_wd%d%d" % key, [DFF, D], BF16))
    with ExitStackCompat() as stack:
        C = make_ctx(nc, stack, cst["c_ident"])

        cast_bufs = [Buf(), Buf()]
        cast_n = [0]

        def cast_ffn(l, which):
            p = "ffn%d_" % which
            dg, du, dd = wbf[(l, which)]
            for src, dst, rows, cw in ((Wt[p + "w_gate"][l], dg, D, DFF // 2), (Wt[p + "w_up"][l], du, D, DFF // 2),
                                       (Wt[p + "w_down"][l], dd, DFF, D)):
                cols = DFF if rows == D else D
                for r0 in range(0, rows, 128):
                    for c0 in range(0, cols, cw):
                        C.S.add("pool", (lambda e, src=src, dst=dst, r0=r0, c0=c0, cw=cw: e.dma_start(
                            out=dst[r0:r0 + 128, c0:c0 + cw], in_=src[r0:r0 + 128, c0:c0 + cw])),
                            writes=[cast_bufs[cast_n[0] % 2]], dma=True)
                        cast_n[0] += 1

        def ffn(l, which, src, dst):
            p = "ffn%d_" % which
            if (l, which) in wbf:
                wg_, wu_, wd_ = wbf[(l, which)]
            else:
                wg_, wu_, wd_ = Wt[p + "w_gate"][l], Wt[p + "w_up"][l], Wt[p + "w_down"][l]
            ffn_phase(C, src, dst, wg_, wu_, wd_, Wt[p + "norm_pre"][l], Wt[p + "norm_post"][l], NTOK)

        ffn(0, 1, x, xa)
        simple_linear(C, "mk0", NSEQ * NMEM, mem, Wt["mem_norm"][0], Wt["w_mem_kv"][0], 2 * MEMW, MKV)
        inproj_a_phase(C, xa, Wt["mix_norm_pre"][0], Wt["ret_w_in"][0], P, pos, cst["c_invf"])
        cast_ffn(0, 2)
        cast_ffn(1, 1)
        retention_phase(C, P, ycat, cst["c_dmaskT"], cst["c_qdec"], cst["c_kdec"])
        memattn_phase(C, P, 4 * MIXW, MKV, ycat)
        outproj_phase(C, ycat, Wt["w_o"][0], Wt["mix_norm_post"][0], xa, xb)
        ffn(0, 2, xb, xa)
        simple_linear(C, "kvs", NTOK, xa, Wt["kv_norm"], Wt["w_kv_shared"], 2 * MIXW, KVs)
        ffn(1, 1, xa, xb)
        simple_linear(C, "mk1", NSEQ * NMEM, mem, Wt["mem_norm"][1], Wt["w_mem_kv"][1], 2 * MEMW, MKV)
        simple_linear(C, "ib", NTOK, xb, Wt["mix_norm_pre"][1], Wt["sb_w_in"][0], MIXW + MEMW, P)
        stickbreak_phase(C, P, KVs, ycat, cst["c_tril_f"], cst["c_tril_b"])
        cast_ffn(1, 2)
        memattn_phase(C, P, MIXW, MKV, ycat)
        outproj_phase(C, ycat, Wt["w_o"][1], Wt["mix_norm_post"][1], xb, xa)
        ffn(1, 2, xa, out)
        C.S.close()
    return nc


_CACHE = {}


def kernel(**inputs):
    if "nc" not in _CACHE:
        _CACHE["nc"] = build_program()
        _CACHE["consts"] = make_consts()
    nc = _CACHE["nc"]
    consts = _CACHE["consts"]
    x = np.ascontiguousarray(inputs["x"], dtype=np.float32)
    mem = np.ascontiguousarray(inputs["mem"], dtype=np.float32)
    pos = np.ascontiguousarray(inputs["positions"], dtype=np.int32)
    shared = {name: np.ascontiguousarray(inputs[name], dtype=np.float32) for name, _ in W_SPECS}
    shared.update(consts)
    in_maps = []
    for c in range(N_CORES):
        m = dict(shared)
        m["x"] = x[c * NSEQ:(c + 1) * NSEQ].reshape(NTOK, D)
        m["mem"] = mem[c * NSEQ:(c + 1) * NSEQ].reshape(NSEQ * NMEM, D)
        m["positions"] = np.ascontiguousarray(pos[c * NSEQ:(c + 1) * NSEQ].reshape(NTOK // 128, 128).T)
        in_maps.append(m)
    res = run_bass_kernel_spmd(nc, in_maps, core_ids=list(range(N_CORES)))
    outs = [np.asarray(r["out"]).reshape(NSEQ, SEQ, D) for r in res.results]
    return np.concatenate(outs, axis=0).astype(np.float32)
```
